# Optimizing a Trainium2 kernel written in Bass

```python
import jax, jax.numpy as jnp
from jax import lax
import numpy as np

D_MODEL = 1024
BATCH = 32
SEQ = 256
DEPTH = 1
DEC_BATCH = 8
DEC_SEQ = 2048
PAST_LEN = 512

GRID_W = 64
CONV_WIDTH = 512
CONV_K = 3
N_HEADS = 8
HEAD_DIM = 64
ATTN_WIDTH = N_HEADS * HEAD_DIM
WIN_ROWS_MAX = 8
WIN_COLS = 16
D_FF = -(-8 * D_MODEL // (3 * 256)) * 256
N_IN = 3 * CONV_WIDTH + 3 * ATTN_WIDTH
EPS = 1e-6
NEG = -1e30

kernel_name = "hybrid_shortconv_natten_diffusion_step"


def _rmsnorm(x, g):
    xf = x.astype(jnp.float32)
    y = xf * lax.rsqrt(jnp.mean(xf * xf, axis=-1, keepdims=True) + EPS)
    return (y * g.astype(jnp.float32)).astype(x.dtype)


def _modulation(cond, w_mod, b_mod):
    m = jax.nn.silu(cond) @ w_mod + b_mod
    return jnp.split(m, 6, axis=-1)


def _split_heads(t):
    B, L, _ = t.shape
    return t.reshape(B, L, N_HEADS, HEAD_DIM).transpose(0, 2, 1, 3)


def _merge_heads(t):
    B, H, L, Dh = t.shape
    return t.transpose(0, 2, 1, 3).reshape(B, L, H * Dh)


def _short_conv(u, w_conv):
    L = u.shape[1]
    pad = CONV_K // 2
    up = jnp.pad(u, ((0, 0), (pad, pad), (0, 0)))
    out = w_conv[0] * up[:, 0:L]
    for j in range(1, CONV_K):
        out = out + w_conv[j] * up[:, j:j + L]
    return out


def _mixer_inputs(x, shift, scale, g_pre, w_in):
    h = _rmsnorm(x, g_pre) * (1.0 + scale) + shift
    proj = h @ w_in
    cw, aw = CONV_WIDTH, ATTN_WIDTH
    b, cg, xc, q, k, v = jnp.split(
        proj, [cw, 2 * cw, 3 * cw, 3 * cw + aw, 3 * cw + 2 * aw], axis=-1)
    return h, b, cg, xc, _split_heads(q), _split_heads(k), _split_heads(v)


def _merge_branches(h, y_conv, y_attn, w_br_conv, w_br_attn, w_gate, w_o):
    g_conv, g_attn = jnp.split(jax.nn.sigmoid(h @ w_gate), 2, axis=-1)
    branch_conv = y_conv @ w_br_conv
    branch_attn = _merge_heads(y_attn) @ w_br_attn
    return (g_conv * branch_conv + g_attn * branch_attn) @ w_o


def _ffn_sublayer(x, shift, scale, gate, g_pre, g_post, w1, w3, w2):
    h = _rmsnorm(x, g_pre) * (1.0 + scale) + shift
    y = (jax.nn.silu(h @ w1) * (h @ w3)) @ w2
    return x + gate * _rmsnorm(y, g_post)


def _context_attention(q, k, v):
    s = jnp.einsum('bhqd,bhkd->bhqk', q, k).astype(jnp.float32) * (HEAD_DIM ** -0.5)
    p = jax.nn.softmax(s, axis=-1).astype(v.dtype)
    return jnp.einsum('bhqk,bhkd->bhqd', p, v)


def _neighbourhood_attention(q, k, v, k_ctx, v_ctx, rpb):
    B, H, T, Dh = q.shape
    rows = T // GRID_W
    kh = min(WIN_ROWS_MAX, rows)
    kw = WIN_COLS
    qg = q.reshape(B, H, rows, GRID_W, Dh)
    kg = k.reshape(B, H, rows, GRID_W, Dh)
    vg = v.reshape(B, H, rows, GRID_W, Dh)
    r = jnp.arange(rows)
    r0 = jnp.clip(r - kh // 2, 0, rows - kh)
    row_idx = r0[:, None] + jnp.arange(kh)[None, :]
    k_band = kg[:, :, row_idx]
    v_band = vg[:, :, row_idx]
    cq = jnp.arange(GRID_W)
    c0 = jnp.clip(cq - kw // 2, 0, GRID_W - kw)
    ck = jnp.arange(GRID_W)
    col_in = (ck[None, :] >= c0[:, None]) & (ck[None, :] < c0[:, None] + kw)
    row_off = row_idx - r[:, None] + (WIN_ROWS_MAX - 1)
    col_off = jnp.clip(ck[None, :] - cq[:, None], -(WIN_COLS - 1), WIN_COLS - 1) + (WIN_COLS - 1)
    bias = rpb[:, row_off[:, None, :, None], col_off[None, :, None, :]]
    s_nb = jnp.einsum('bhrwd,bhrkxd->bhrwkx', qg, k_band).astype(jnp.float32) * (HEAD_DIM ** -0.5)
    s_nb = s_nb + bias[None].astype(jnp.float32)
    s_nb = jnp.where(col_in[:, None, :], s_nb, NEG)
    s_nb = s_nb.reshape(B, H, rows, GRID_W, kh * GRID_W)
    s_ctx = jnp.einsum('bhrwd,bhpd->bhrwp', qg, k_ctx).astype(jnp.float32) * (HEAD_DIM ** -0.5)
    n_nb = kh * GRID_W
    p = jax.nn.softmax(jnp.concatenate([s_nb, s_ctx], axis=-1), axis=-1).astype(v.dtype)
    p_nb = p[..., :n_nb].reshape(B, H, rows, GRID_W, kh, GRID_W)
    p_ctx = p[..., n_nb:]
    out = (jnp.einsum('bhrwkx,bhrkxd->bhrwd', p_nb, v_band)
           + jnp.einsum('bhrwp,bhpd->bhrwd', p_ctx, v_ctx))
    return out.reshape(B, H, T, Dh)


def setup_inputs(seed: int = 0) -> dict:
    key = jax.random.key(seed)
    ks = jax.random.split(key, 24)
    f32 = jnp.float32

    def nrm(k, shape, scale=1.0):
        return (jax.random.normal(k, shape, f32) * scale).astype(f32)

    L = DEPTH
    return {
        "x_prompt": nrm(ks[0], (BATCH, SEQ, D_MODEL)),
        "x_sample": nrm(ks[1], (DEC_BATCH, DEC_SEQ, D_MODEL)),
        "cache_k": nrm(ks[2], (DEC_BATCH, DEPTH, N_HEADS, PAST_LEN, HEAD_DIM)),
        "cache_v": nrm(ks[3], (DEC_BATCH, DEPTH, N_HEADS, PAST_LEN, HEAD_DIM)),
        "c": nrm(ks[4], (DEC_BATCH, D_MODEL)),
        "c_ctx": nrm(ks[5], (D_MODEL,)),
        "w_mod": nrm(ks[6], (L, D_MODEL, 6 * D_MODEL), D_MODEL ** -0.5),
        "b_mod": nrm(ks[7], (L, 6 * D_MODEL), 0.01),
        "g_pre_mix": 1.0 + nrm(ks[8], (L, D_MODEL), 0.1),
        "g_post_mix": 1.0 + nrm(ks[9], (L, D_MODEL), 0.1),
        "g_pre_ffn": 1.0 + nrm(ks[10], (L, D_MODEL), 0.1),
        "g_post_ffn": 1.0 + nrm(ks[11], (L, D_MODEL), 0.1),
        "w_in": nrm(ks[12], (L, D_MODEL, N_IN), D_MODEL ** -0.5),
        "w_conv": nrm(ks[13], (L, CONV_K, CONV_WIDTH), CONV_K ** -0.5),
        "w_br_conv": nrm(ks[14], (L, CONV_WIDTH, D_MODEL), CONV_WIDTH ** -0.5),
        "w_br_attn": nrm(ks[15], (L, ATTN_WIDTH, D_MODEL), ATTN_WIDTH ** -0.5),
        "rpb": nrm(ks[16], (L, N_HEADS, 2 * WIN_ROWS_MAX - 1, 2 * WIN_COLS - 1), 0.1),
        "w_gate": nrm(ks[17], (L, D_MODEL, 2 * D_MODEL), D_MODEL ** -0.5),
        "w_o": nrm(ks[18], (L, D_MODEL, D_MODEL), D_MODEL ** -0.5),
        "w_ff1": nrm(ks[19], (L, D_MODEL, D_FF), D_MODEL ** -0.5),
        "w_ff3": nrm(ks[20], (L, D_MODEL, D_FF), D_MODEL ** -0.5),
        "w_ff2": nrm(ks[21], (L, D_FF, D_MODEL), D_FF ** -0.5),
    }


def reference(x_prompt, x_sample, cache_k, cache_v, c, c_ctx, w_mod, b_mod,
              g_pre_mix, g_post_mix, g_pre_ffn, g_post_ffn, w_in, w_conv,
              w_br_conv, w_br_attn, rpb, w_gate, w_o, w_ff1, w_ff3, w_ff2):
    y_p = x_prompt
    y_s = x_sample
    new_k = []
    new_v = []
    for l in range(DEPTH):
        sh1, sc1, gt1, sh2, sc2, gt2 = _modulation(c_ctx, w_mod[l], b_mod[l])
        h, b, cg, xc, q, k, v = _mixer_inputs(y_p, sh1, sc1, g_pre_mix[l], w_in[l])
        y_conv = b * _short_conv(cg * xc, w_conv[l])
        y_att = _context_attention(q, k, v)
        mixed = _merge_branches(h, y_conv, y_att, w_br_conv[l], w_br_attn[l], w_gate[l], w_o[l])
        y_p = y_p + gt1 * _rmsnorm(mixed, g_post_mix[l])
        y_p = _ffn_sublayer(y_p, sh2, sc2, gt2, g_pre_ffn[l], g_post_ffn[l],
                            w_ff1[l], w_ff3[l], w_ff2[l])
        new_k.append(k)
        new_v.append(v)

        sh1, sc1, gt1, sh2, sc2, gt2 = _modulation(c[:, None, :], w_mod[l], b_mod[l])
        h, b, cg, xc, q, k, v = _mixer_inputs(y_s, sh1, sc1, g_pre_mix[l], w_in[l])
        y_conv = b * _short_conv(cg * xc, w_conv[l])
        y_att = _neighbourhood_attention(q, k, v, cache_k[:, l], cache_v[:, l], rpb[l])
        mixed = _merge_branches(h, y_conv, y_att, w_br_conv[l], w_br_attn[l], w_gate[l], w_o[l])
        y_s = y_s + gt1 * _rmsnorm(mixed, g_post_mix[l])
        y_s = _ffn_sublayer(y_s, sh2, sc2, gt2, g_pre_ffn[l], g_post_ffn[l],
                            w_ff1[l], w_ff3[l], w_ff2[l])
    new_cache_k = jnp.stack(new_k, axis=1)
    new_cache_v = jnp.stack(new_v, axis=1)
    return (y_p, y_s, new_cache_k, new_cache_v)
```

```python
from contextlib import ExitStack
import numpy as np
import concourse.bass as bass
import concourse.mybir as mybir
from concourse.bass_utils import run_bass_kernel_spmd

F32 = mybir.dt.float32
BF16 = mybir.dt.bfloat16
AF = mybir.ActivationFunctionType
ALU = mybir.AluOpType
AX = mybir.AxisListType

SAME_ENGINE_SYNC = True
D = 1024
DFF = 2816
EPS = 1e-6
NEG = -1e30
NSLOT = 4
NTP = 9
STAGE = 999


class _Stop(Exception):
    pass


def _stage(n):
    if STAGE in (-11, -12, -13):
        if n >= 0:
            raise _Stop()
        return
    if n > STAGE:
        raise _Stop()


class Region:
    __slots__ = ("name", "writers", "readers", "war", "dma_n", "sem", "full")

    def __init__(self, name):
        self.name = name
        self.full = []
        self.writers = []
        self.readers = []
        self.war = []
        self.dma_n = 0
        self.sem = None


class Op:
    __slots__ = ("eng", "fn", "deps", "needs_inc", "val", "is_dma", "dma_reg", "dma_val")

    def __init__(self, eng, fn, is_dma=False):
        self.eng = eng
        self.fn = fn
        self.deps = []
        self.needs_inc = False
        self.val = None
        self.is_dma = is_dma
        self.dma_reg = None
        self.dma_val = None


COMPUTE = ("tensor", "vector", "scalar", "gpsimd")
QUEUES = ("tensor", "vector", "scalar", "gpsimd", "sync")


class Sched:
    def __init__(self):
        self.ops = {q: [] for q in QUEUES}
        self.dma_regions = []

    def region(self, name):
        return Region(name)

    def fence(self, olds, news):
        ops = []
        for r in olds:
            ops += r.readers + r.writers
        for r in news:
            r.readers = list(r.readers) + ops

    def _track(self, op, reads, writes, partial, pwrites=()):
        deps = []
        for r in reads:
            deps.extend(r.writers)
            r.readers.append(op)
        wl = [(r, partial) for r in writes] + [(r, True) for r in pwrites]
        for r, partial in wl:
            if partial and not r.readers and r.writers:
                deps.extend(r.war)
                deps.extend(r.full)
                r.writers.append(op)
            else:
                war = list(r.readers) + list(r.writers)
                deps.extend(war)
                r.war = war
                r.writers = [op]
                r.readers = []
                r.full = [] if partial else [op]
        seen = set()
        for d in deps:
            if d is op or id(d) in seen:
                continue
            seen.add(id(d))
            if d.eng == op.eng and not d.is_dma:
                if op.eng == "tensor" or not SAME_ENGINE_SYNC:
                    continue
            op.deps.append(d)
            if not d.is_dma:
                d.needs_inc = True

    def op(self, eng, fn, reads=(), writes=(), partial=False, pwrites=()):
        o = Op(eng, fn)
        self._track(o, reads, writes, partial, pwrites)
        self.ops[eng].append(o)
        return o

    def dma(self, eng, out, in_, key, reads=(), writes=(), partial=True, **kw):
        def fn(e, out=out, in_=in_, kw=kw):
            return e.dma_start(out=out, in_=in_, **kw)
        o = Op(eng, fn, is_dma=True)
        o.dma_reg = key
        key.dma_n += 1
        o.dma_val = 16 * key.dma_n
        if key not in self.dma_regions:
            self.dma_regions.append(key)
        self._track(o, reads, writes, partial)
        self.ops[eng].append(o)
        return o

    def emit(self, nc, stack):
        esem = {q: stack.enter_context(nc.semaphore(f"e_{q}")) for q in COMPUTE}
        for i, r in enumerate(self.dma_regions):
            r.sem = stack.enter_context(nc.semaphore(f"d{i}_{r.name}"))
        for q in QUEUES:
            c = 0
            for o in self.ops[q]:
                if not o.is_dma and o.needs_inc:
                    c += 1
                    o.val = c
        block = stack.enter_context(nc.Block())
        all_dma = [(r.sem, 16 * r.dma_n) for r in self.dma_regions]

        def make(q):
            def body(e):
                waited = {}
                for o in self.ops[q]:
                    need = {}
                    for d in o.deps:
                        if d.is_dma:
                            sem, v = d.dma_reg.sem, d.dma_val
                        else:
                            sem, v = esem[d.eng], d.val
                        k = id(sem)
                        if k not in need or need[k][1] < v:
                            need[k] = (sem, v)
                    for k, (sem, v) in need.items():
                        if waited.get(k, 0) >= v:
                            continue
                        waited[k] = v
                        e.wait_ge(sem, v)
                    ins = o.fn(e)
                    if o.is_dma:
                        ins.then_inc(o.dma_reg.sem, 16)
                    elif o.needs_inc:
                        ins.then_inc(esem[q], 1)
                if q == "sync":
                    for sem, v in all_dma:
                        if waited.get(id(sem), 0) < v:
                            e.wait_ge(sem, v)
            return body

        block.tensor(make("tensor"))
        block.vector(make("vector"))
        block.scalar(make("scalar"))
        block.gpsimd(make("gpsimd"))
        block.sync(make("sync"))


def build():
    nc = bass.Bass("TRN2", target_bir_lowering=False)

    def din(name, shape):
        return nc.dram_tensor(name, shape, F32, kind="ExternalInput").ap()

    def dout(name, shape):
        return nc.dram_tensor(name, shape, F32, kind="ExternalOutput").ap()

    xp = din("xp", [1024, D])
    xs = din("xs", [2048, D])
    ck = din("ck", [8, 512, 64])
    cv = din("cv", [8, 512, 64])
    c2col = din("c2col", [128, 8, 2])
    bmodcol = din("bmodcol", [128, 48])
    gcol_d = din("gcol", [128, 4, 8])
    convw_d = din("convw", [128, 4, 3])
    rpb = din("rpb", [8, 15, 31])
    w_mod = din("w_mod", [D, 6 * D])
    w_in = din("w_in", [D, 3072])
    w_brc = din("w_brc", [512, D])
    w_bra = din("w_bra", [512, D])
    w_gate = din("w_gate", [D, 2 * D])
    w_o = din("w_o", [D, D])
    w_ff1 = din("w_ff1", [D, DFF])
    w_ff3 = din("w_ff3", [D, DFF])
    w_ff2 = din("w_ff2", [DFF, D])
    yp = dout("yp", [1024, D])
    ys = dout("ys", [2048, D])
    nk = dout("nk", [4, 8, 256, 64])
    nv = dout("nv", [4, 8, 256, 64])

    btimg_t = nc.dram_tensor("btimg", [64, 7680], F32, kind="Internal")
    btimg = btimg_t.ap()
    rpb_t = rpb.tensor

    S = Sched()
    with ExitStack() as st:
        def sb(name, shape, dt):
            return st.enter_context(nc.sbuf_tensor(name, shape, dt))

        NS = NSLOT + 1
        ring = [sb(f"ring{i}", [128, 8, 512], BF16) for i in range(NS)]
        R_ring = [S.region(f"ring{i}") for i in range(NS)]
        xin = [sb(f"xin{i}", [128, D], F32) for i in range(2)]
        R_xin = [S.region(f"xin{i}") for i in range(2)]
        ost = [sb(f"ost{i}", [128, D], F32) for i in range(2)]
        R_ost = [S.region(f"ost{i}") for i in range(2)]
        GG = [sb(f"GG{i}", [128, D], F32) for i in range(2)]
        R_GG = S.region("GG")
        kctxT = ring[NSLOT][:, 0:4, :]
        vctx = ring[NSLOT][:, 4:8, :]
        R_ctx = R_ring[NSLOT]
        R_ctmpk = S.region("ctmpk")
        Sb2 = [sb(f"Sb{i}", [128, 1088], F32) for i in range(2)]
        R_Sb2 = [S.region(f"Sb{i}") for i in range(2)]
        Pb2 = [sb(f"Pb{i}", [128, 1088], BF16) for i in range(3)]
        R_Pb2 = [S.region(f"Pb{i}") for i in range(3)]
        PTs2 = [sb(f"PTs{i}", [128, 9, 128], BF16) for i in range(2)]
        R_PTs2 = [S.region(f"PTs{i}") for i in range(2)]
        stat2 = [sb(f"stat{i}", [128, 16], F32) for i in range(5)]
        R_stat2 = [S.region(f"stat{i}") for i in range(5)]
        tmpA2 = [Sb2[i][:, 0:512] for i in range(2)]
        tmpB2 = [Sb2[i][:, 512:1024] for i in range(2)]
        R_tmpA2 = [S.region(f"tmpA{i}") for i in range(2)]
        R_tmpB2 = [S.region(f"tmpB{i}") for i in range(2)]
        par = {"v": 0}

        def flip():
            par["v"] ^= 1
            return par["v"]

        ident = sb("ident", [128, 128], BF16)
        identf = sb("identf", [128, 128], F32)
        diag = sb("diag", [128, 128], F32)
        dhi = sb("dhi", [128, 128], BF16)
        dlo = sb("dlo", [128, 128], BF16)
        onesb = sb("onesb", [128, 128], BF16)
        R_dhi = S.region("dhi")
        R_dlo = S.region("dlo")
        R_const = S.region("const")
        R_diag = S.region("diag")
        cT = sb("cT", [128, 8, 2], F32)
        scT = sb("scT", [128, 8, 2], BF16)
        bmc = sb("bmc", [128, 48], F32)
        modcol = sb("modcol", [128, 48, 2], F32)
        gcol = sb("gcolsb", [128, 4, 8], F32)
        convw = sb("convwsb", [128, 4, 3], F32)
        pcol = sb("pcol", [128, 12, 8], F32)
        R_mod = S.region("mod")
        R_pcol = S.region("pcol")
        ss0 = sb("ss0", [128, 8], F32)
        R_ss0 = S.region("ss0")

        FA = sb("FA", [128, 8256], F32)
        x1buf = FA[:, 0:8192].rearrange("p (t d) -> p t d", t=8)
        BT = FA[:, 0:7680].rearrange("p (h r w) -> p h r w", h=8, r=15)
        BTf_top = FA[:, 0:7680].rearrange("p (n w) -> p n w", w=64)
        BTf_bot = FA[:, 64:64 + 7680].rearrange("p (n w) -> p n w", w=64)
        hT = sb("HT", [128, 8, 1280], BF16)
        h2T = hT[:, :, 0:1024]
        BU = sb("BU", [128, 9232], BF16)
        UW = 1284
        bT = BU[:, 0:4096].rearrange("p (c t) -> p c t", c=4)
        uT = BU[:, 4096:4096 + 4 * UW].rearrange("p (c t) -> p c t", c=4)
        mgT = BU[:, 0:8192].rearrange("p (k t) -> p k t", k=8)
        ATb = sb("ATb", [128, 22528], BF16)
        aT = ATb[:, :].rearrange("p (k t) -> p k t", k=22)
        qT = ATb[:, 0:4096].rearrange("p (c t) -> p c t", c=4)
        kT = ATb[:, 4096:9216].rearrange("p (c t) -> p c t", c=4)
        vv = ATb[:, 9216:14336].rearrange("p (t f) -> p t f", t=10)
        yattT = ATb[:, 14336:18432].rearrange("p (c t) -> p c t", c=4)
        yconvT = ATb[:, 18432:22528].rearrange("p (c t) -> p c t", c=4)

        def y0ap(t):
            if t < 4:
                return xin[t // 2][:, (t % 2) * 512:(t % 2 + 1) * 512]
            return Sb2[(t - 4) // 2][:, ((t - 4) % 2) * 512:((t - 4) % 2 + 1) * 512]

        R_x1 = [S.region(f"x1_{i}") for i in range(8)]
        R_BT = S.region("BT")
        R_hT = [S.region(f"hT{i}") for i in range(10)]
        R_h2T = [S.region(f"h2T{i}") for i in range(8)]
        R_y0 = [R_xin[0], R_xin[0], R_xin[1], R_xin[1], R_Sb2[0], R_Sb2[0], R_Sb2[1], R_Sb2[1]]
        R_b, R_u, R_mg = S.region("b"), S.region("u"), S.region("mg")
        R_q, R_k, R_v = S.region("q"), S.region("k"), S.region("v")
        R_yatt, R_yconv, R_aT = S.region("yatt"), S.region("yconv"), S.region("aT")

        ps = [st.enter_context(nc.psum_tensor(f"ps{i}", [128, 512], F32)) for i in range(6)]
        pst = [st.enter_context(nc.psum_tensor(f"pst{i}", [128, 8, 128], BF16)) for i in range(2)]
        R_ps = [S.region(f"ps{i}") for i in range(6)]
        R_pst = [S.region(f"pst{i}") for i in range(2)]
        rr = {"ps": 0, "pst": 0, "xin": 0, "ost": 0}

        def nb():
            i = rr["ps"]
            rr["ps"] = (i + 1) % 6
            return i

        def nxt(key, n=2):
            i = rr[key]
            rr[key] = (i + 1) % n
            return i

        chunks = []
        state = {"issued": 0}
        slot_of = {}
        slot_free = [True] * NS
        slot_enabled = [True] * NS

        def ensure(i, i1=None, disable_ctx_slot=False):
            i1 = i if i1 is None else i1
            for j in list(slot_of):
                if j < i:
                    slot_free[slot_of.pop(j)] = True
            if disable_ctx_slot:
                slot_enabled[NSLOT] = False
            while state["issued"] < len(chunks):
                cand = [q for q in range(NS) if slot_free[q] and slot_enabled[q]]
                if not cand:
                    break
                j = state["issued"]
                q = cand[0]
                slot_free[q] = False
                slot_of[j] = q
                for (k0, k1, f0, f1, src) in chunks[j]:
                    S.dma("gpsimd", ring[q][:, k0:k1, f0:f1], src.rearrange("(kc p) f -> p kc f", p=128),
                          key=R_ring[q], writes=[R_ring[q]])
                state["issued"] += 1
            assert state["issued"] > i1, (i, i1, state["issued"])
            return slot_of[i]

        def std(w, f0, nf=512):
            return [(0, 8, 0, nf, w[:, f0:f0 + nf])]

        def sb_chunks():
            cl = []
            for c in (3, 4, 5, 0, 1, 2):
                cl.append(std(w_in, c * 512))
            for mg in range(2):
                cl.append(std(w_gate, mg * 512))
                cl.append(std(w_gate, 1024 + mg * 512))
                cl.append([(0, 4, 0, 512, w_brc[:, mg * 512:(mg + 1) * 512]),
                           (4, 8, 0, 512, w_bra[:, mg * 512:(mg + 1) * 512])])
            cl.append(std(w_o, 0))
            cl.append(std(w_o, 512))
            for g in range(6):
                nf = 512 if g < 5 else 256
                cl.append(std(w_ff1, g * 512, nf))
                cl.append(std(w_ff3, g * 512, nf))
            for nh in range(2):
                for (r0, r1) in ((0, 1024), (1024, 2048), (2048, 2816)):
                    cl.append([(0, (r1 - r0) // 128, 0, 512, w_ff2[r0:r1, nh * 512:(nh + 1) * 512])])
            return cl

        for j in range(12):
            chunks.append(std(w_mod, j * 512))
        NMOD = 12
        per_sb = sb_chunks()
        NPER = len(per_sb)
        for _ in range(3):
            chunks.extend(per_sb)

        S.op("gpsimd", lambda e: e.memset(identf[:], 0.0), writes=[R_const])
        S.op("gpsimd", lambda e: e.affine_select(out=identf[:], in_=identf[:], compare_op=ALU.not_equal, fill=1.0,
                                                 base=0, pattern=[[-1, 128]], channel_multiplier=1),
             reads=[R_const], writes=[R_const])
        S.op("vector", lambda e: e.tensor_copy(ident[:], identf[:]), reads=[R_const], writes=[R_const])
        S.op("vector", lambda e: e.memset(onesb[:], 1.0), reads=[R_const], writes=[R_const])
        S.dma("sync", cT[:], c2col[:, :, :], key=R_mod, writes=[R_mod])
        S.dma("sync", bmc[:], bmodcol[:, :], key=R_mod, writes=[R_mod])
        S.dma("sync", gcol[:], gcol_d[:, :, :], key=R_mod, writes=[R_mod])
        S.dma("sync", convw[:], convw_d[:, :, :], key=R_mod, writes=[R_mod])
        S.op("scalar", lambda e: e.activation(out=scT[:], in_=cT[:], func=AF.Silu), reads=[R_mod], writes=[R_mod])
        def setup_mod():
            mp = nb()
            modps = ps[mp][:, 0:96].rearrange("p (j g) -> p j g", g=2)
            for j in range(NMOD):
                s = ensure(j)
                for m4 in range(4):
                    for kc in range(8):
                        S.op("tensor", lambda e, s=s, m4=m4, kc=kc, j=j: e.matmul(
                            modps[:, j * 4 + m4, :], lhsT=ring[s][:, kc, m4 * 128:(m4 + 1) * 128], rhs=scT[:, kc, :],
                            start=(kc == 0), stop=(kc == 7)),
                            reads=[R_ring[s], R_mod], writes=[R_ps[mp]], partial=True)
            for g in range(2):
                S.op("vector", lambda e, g=g: e.tensor_tensor(out=modcol[:, :, g], in0=modps[:, :, g], in1=bmc[:, :], op=ALU.add),
                     reads=[R_ps[mp], R_mod], writes=[R_mod], partial=True)

        def group_pcol(g):
            def mc(i):
                return modcol[:, i * 8:(i + 1) * 8, g]
            V = "vector"
            o = g * 6
            S.op(V, lambda e: e.scalar_tensor_tensor(out=pcol[:, o + 0, :], in0=mc(1), scalar=1.0, in1=gcol[:, 0, :], op0=ALU.add, op1=ALU.mult),
                 reads=[R_mod], writes=[R_pcol], partial=True)
            S.op(V, lambda e: e.tensor_copy(pcol[:, o + 1, :], mc(0)), reads=[R_mod], writes=[R_pcol], partial=True)
            S.op(V, lambda e: e.scalar_tensor_tensor(out=pcol[:, o + 2, :], in0=mc(4), scalar=1.0, in1=gcol[:, 2, :], op0=ALU.add, op1=ALU.mult),
                 reads=[R_mod], writes=[R_pcol], partial=True)
            S.op(V, lambda e: e.tensor_copy(pcol[:, o + 3, :], mc(3)), reads=[R_mod], writes=[R_pcol], partial=True)
            S.op(V, lambda e: e.tensor_tensor(out=pcol[:, o + 4, :], in0=mc(2), in1=gcol[:, 1, :], op=ALU.mult),
                 reads=[R_mod], writes=[R_pcol], partial=True)
            S.op(V, lambda e: e.tensor_tensor(out=pcol[:, o + 5, :], in0=mc(5), in1=gcol[:, 3, :], op=ALU.mult),
                 reads=[R_mod], writes=[R_pcol], partial=True)

        def build_GG(g):
            V = "vector"
            first = True
            for gi in range(2):
                for half in range(2):
                    b = nb()
                    for k4 in range(4):
                        kc = half * 4 + k4
                        S.op(V, lambda e, kc=kc, gi=gi: e.tensor_scalar(diag[:], identf[:], pcol[:, g * 6 + 4 + gi, kc:kc + 1], None, ALU.mult),
                             reads=[R_pcol, R_const], writes=[R_diag])
                        S.op(V, lambda e: e.tensor_copy(dhi[:], diag[:]), reads=[R_diag], writes=[R_dhi])
                        S.op(V, lambda e: e.tensor_tensor(out=diag[:], in0=diag[:], in1=dhi[:], op=ALU.subtract), reads=[R_diag, R_dhi], writes=[R_diag])
                        S.op(V, lambda e: e.tensor_copy(dlo[:], diag[:]), reads=[R_diag], writes=[R_dlo])
                        S.op("tensor", lambda e, b=b, k4=k4: e.matmul(ps[b][:, k4 * 128:(k4 + 1) * 128], lhsT=onesb[:], rhs=dhi[:],
                                                                       start=True, stop=False),
                             reads=[R_dhi, R_const], writes=[R_ps[b]], partial=True)
                        S.op("tensor", lambda e, b=b, k4=k4: e.matmul(ps[b][:, k4 * 128:(k4 + 1) * 128], lhsT=onesb[:], rhs=dlo[:],
                                                                       start=False, stop=True),
                             reads=[R_dlo, R_const], writes=[R_ps[b]], partial=True)
                    S.op("vector", lambda e, b=b, gi=gi, half=half: e.tensor_copy(GG[gi][:, half * 512:(half + 1) * 512], ps[b][:, :]),
                         reads=[R_ps[b]], writes=[R_GG], partial=not first)
                    first = False
                    yield

        nt_par = {"v": 0}

        def norm_transpose(src_ap, R_src, dstT, R_dst, col0, pa, ps_, defer=False):
            nt_par["v"] ^= 1
            p = nt_par["v"]
            junk, R_junk, stat, R_stat = Pb2[p][:, 0:1024], R_Pb2[p], stat2[2 + p], R_stat2[2 + p]
            xsb, R_xsb = PTs2[p][:, 0:8, :].rearrange("p k t -> p (k t)"), R_PTs2[p]
            S.op("scalar", lambda e: e.activation(out=junk[:], in_=src_ap, func=AF.Square, accum_out=stat[:, 0:1]),
                 reads=[R_src], writes=[R_junk, R_stat])
            S.op("scalar", lambda e: e.activation(out=stat[:, 1:2], in_=stat[:, 0:1], func=AF.Ln, scale=1.0 / D, bias=EPS),
                 reads=[R_stat], writes=[R_stat])
            S.op("scalar", lambda e: e.activation(out=stat[:, 2:3], in_=stat[:, 1:2], func=AF.Exp, scale=-0.5),
                 reads=[R_stat], writes=[R_stat])
            if NTP < 2:
                return
            S.op("vector", lambda e: e.tensor_scalar(xsb[:], src_ap, stat[:, 2:3], None, ALU.mult),
                 reads=[R_src, R_stat], writes=[R_xsb])
            if NTP < 3:
                return

            def stage2():
                nt_stage2(dstT, R_dst, col0, pa, ps_, xsb, R_xsb)
            if defer:
                prev = nt_pending["f"]
                nt_pending["f"] = stage2
                if prev is not None:
                    prev()
            else:
                stage2()

        nt_pending = {"f": None}

        def nt_flush():
            if nt_pending["f"] is not None:
                nt_pending["f"]()
                nt_pending["f"] = None

        def nt_stage2(dstT, R_dst, col0, pa, ps_, xsb, R_xsb):
            t = nxt("pst")
            for kc in range(8):
                S.op("tensor", lambda e, kc=kc, t=t: e.transpose(pst[t][:, kc, :], xsb[:, kc * 128:(kc + 1) * 128], ident[:]),
                     reads=[R_xsb, R_const], writes=[R_pst[t]], partial=True)
            if NTP < 4:
                return
            use_act = (col0 // 128) % 2 == 0
            for kc in range(8):
                if use_act:
                    S.op("scalar", lambda e, kc=kc, t=t: e.activation(out=dstT[:, kc, col0:col0 + 128], in_=pst[t][:, kc, :], func=AF.Identity,
                                                                       scale=pcol[:, pa, kc:kc + 1], bias=pcol[:, ps_, kc:kc + 1]),
                         reads=[R_pst[t], R_pcol], writes=[R_dst], partial=True)
                else:
                    S.op("vector", lambda e, kc=kc, t=t: e.tensor_scalar(dstT[:, kc, col0:col0 + 128], pst[t][:, kc, :],
                                                                          pcol[:, pa, kc:kc + 1], pcol[:, ps_, kc:kc + 1], ALU.mult, ALU.add),
                         reads=[R_pst[t], R_pcol], writes=[R_dst], partial=True)

        def projA(slot, m4, src, R_src_list, blocks, nk_, kmap=None):
            banks = [nb() for _ in blocks]
            for kc in range(nk_):
                kk = kc if kmap is None else kmap(kc)
                for bi, (c0, n) in enumerate(blocks):
                    b = banks[bi]
                    S.op("tensor", lambda e, b=b, kk=kk, kc=kc, c0=c0, n=n: e.matmul(
                        ps[b][:, 0:n], lhsT=ring[slot][:, kk, m4 * 128:(m4 + 1) * 128], rhs=src[:, kc, c0:c0 + n],
                        start=(kc == 0), stop=(kc == nk_ - 1)),
                        reads=[R_ring[slot]] + R_src_list, writes=[R_ps[b]], partial=True)
            return banks

        def run_sb(sbi, base):
            prompt = (sbi == 0)
            goff = 0 if prompt else 6
            NTE = 8 if prompt else 10
            c0t = 0 if sbi < 2 else 2
            ext_row0 = 0 if sbi < 2 else 12
            core_row0 = 0 if sbi == 1 else 16
            xsrc = xp if prompt else xs
            xrow0 = 0 if sbi < 2 else 768
            ydst = yp if prompt else ys
            yrow0 = 0 if prompt else (0 if sbi == 1 else 1024)
            NEXT = NTE * 128
            ccol = c0t * 128

            def ucol(t):
                return (t + 1 + 2 * (t // 256)) if prompt else (t + ccol + 1)

            def chunk(i, n=1, **kw):
                ensure(base + i, base + i + n - 1, **kw)
                return [slot_of[base + i + j] for j in range(n)] if n > 1 else slot_of[base + i]

            if not prompt:
                for j in list(slot_of):
                    if j < base:
                        slot_free[slot_of.pop(j)] = True
                assert slot_free[NSLOT] and not slot_enabled[NSLOT]
                load_ctx()
            S.fence([R_h2T[i] for i in range(8)], R_hT)
            S.fence(R_tmpA2 + R_tmpB2, R_Sb2)
            S.fence([R_mg], [R_b, R_u])
            S.fence([R_aT], [R_q, R_k, R_v, R_yatt, R_yconv])

            for i in range(NTE):
                xi = nxt("xin")
                S.dma("sync", xin[xi][:], xsrc[xrow0 + i * 128: xrow0 + (i + 1) * 128, :], key=R_xin[xi], writes=[R_xin[xi]])
                norm_transpose(xin[xi][:], R_xin[xi], hT, R_hT[i], i * 128, goff + 0, goff + 1, defer=True)
            nt_flush()
            if sbi == 0:
                build_btimg()

            _stage(sbi * 10 + 1)
            extblocks = [(0, 512), (512, 512)] + ([(1024, 256)] if not prompt else [])
            coreblocks = [(ccol, 512), (ccol + 512, 512)]
            allhT = R_hT[:NTE]

            s = chunk(0)
            for m4 in range(4):
                banks = projA(s, m4, hT, allhT, coreblocks, 8)
                for bi, b in enumerate(banks):
                    S.op("scalar", lambda e, b=b, bi=bi, m4=m4: e.activation(out=qT[:, m4, bi * 512:(bi + 1) * 512], in_=ps[b][:, :],
                                                                             func=AF.Copy, scale=0.125),
                         reads=[R_ps[b]], writes=[R_q], partial=True)
            s = chunk(1)
            for m4 in range(4):
                banks = projA(s, m4, hT, allhT, extblocks, 8)
                for bi, b in enumerate(banks):
                    c0_, n = extblocks[bi]
                    S.op("vector", lambda e, b=b, c0_=c0_, n=n, m4=m4: e.tensor_copy(kT[:, m4, c0_:c0_ + n], ps[b][:, 0:n]),
                         reads=[R_ps[b]], writes=[R_k], partial=True)
            if prompt:
                for i in range(8):
                    b = nb()
                    for kc in range(8):
                        S.op("tensor", lambda e, b=b, kc=kc, i=i, s=s: e.matmul(ps[b][:, :], lhsT=hT[:, kc, i * 128:(i + 1) * 128], rhs=ring[s][:, kc, :],
                                                                          start=(kc == 0), stop=(kc == 7)),
                             reads=[R_ring[s], R_hT[i]], writes=[R_ps[b]], partial=True)
                    oi = nxt("ost")
                    S.op("scalar", lambda e, b=b, oi=oi: e.copy(ost[oi][:, 0:512], ps[b][:, :]), reads=[R_ps[b]], writes=[R_ost[oi]])
                    S.dma("sync", nk[i // 2].rearrange("h s d -> s h d")[(i % 2) * 128:(i % 2 + 1) * 128],
                          ost[oi][:, 0:512].rearrange("p (h d) -> p h d", h=8), key=R_ost[oi], reads=[R_ost[oi]])
            s = chunk(2)
            for i in range(NTE):
                b = nb()
                for kc in range(8):
                    S.op("tensor", lambda e, b=b, kc=kc, i=i, s=s: e.matmul(ps[b][:, :], lhsT=hT[:, kc, i * 128:(i + 1) * 128], rhs=ring[s][:, kc, :],
                                                                      start=(kc == 0), stop=(kc == 7)),
                         reads=[R_ring[s], R_hT[i]], writes=[R_ps[b]], partial=True)
                if prompt:
                    oi = nxt("ost")
                    S.op("scalar", lambda e, b=b, oi=oi: e.copy(ost[oi][:, 0:512], ps[b][:, :]), reads=[R_ps[b]], writes=[R_ost[oi]])
                    S.op("vector", lambda e, oi=oi, i=i: e.tensor_copy(vv[:, i, :], ost[oi][:, 0:512]), reads=[R_ost[oi]], writes=[R_v], partial=True)
                else:
                    S.op("vector", lambda e, b=b, i=i: e.tensor_copy(vv[:, i, :], ps[b][:, :]), reads=[R_ps[b]], writes=[R_v], partial=True)
                if prompt:
                    S.dma("sync", nv[i // 2].rearrange("h s d -> s h d")[(i % 2) * 128:(i % 2 + 1) * 128],
                          ost[oi][:, 0:512].rearrange("p (h d) -> p h d", h=8), key=R_ost[oi], reads=[R_ost[oi]])

            _stage(sbi * 10 + 2)
            if prompt:
                for qi in range(8):
                    sq = qi // 2
                    for hh in range(8):
                        hp, ho = hh // 2, (hh % 2) * 64
                        segs = [(kT[ho:ho + 64, hp, sq * 256:(sq + 1) * 256], 256, None)]
                        pv = [(j * 128, 128, vv[:, sq * 2 + j, hp * 128:(hp + 1) * 128]) for j in range(2)]
                        attention_unit(qi, hh, segs, pv, [R_k, R_v])
            else:
                S.fence(R_x1, [R_BT])
                S.dma("sync", FA[0:64, 0:7680], btimg[:, :], key=R_BT, reads=[R_btimg], writes=[R_BT])
                S.dma("sync", FA[64:128, 64:64 + 7680], btimg[:, :], key=R_BT, reads=[R_btimg], writes=[R_BT])
                for qi in range(8):
                    Rr = core_row0 + 2 * qi
                    r0t = min(max(Rr - 4, 0), 24)
                    r0b = min(max(Rr + 1 - 4, 0), 24)
                    rot = r0t - Rr + 7
                    nband = 576 if r0b != r0t else 512
                    kc0 = (r0t - ext_row0) * 64
                    vt0 = (r0t - ext_row0) // 2
                    for hh in range(8):
                        hp, ho = hh // 2, (hh % 2) * 64
                        segs = [(kT[ho:ho + 64, hp, kc0:kc0 + nband], nband, rot),
                                (kctxT[ho:ho + 64, hp, :], 512, None)]
                        pv = []
                        for j in range(4):
                            pv.append((j * 128, 128, vv[:, vt0 + j, hp * 128:(hp + 1) * 128]))
                        if nband == 576:
                            pv.append((512, 64, vv[0:64, vt0 + 4, hp * 128:(hp + 1) * 128]))
                        for j in range(4):
                            pv.append((nband + j * 128, 128, vctx[:, j, hp * 128:(hp + 1) * 128]))
                        if nband == 512:
                            pv = pv[:4] + [None] + pv[4:]
                        attention_unit_s(qi, hh, segs, pv)

            _stage(sbi * 10 + 3)
            att_flush(build_GG(0 if sbi == 0 else 1) if sbi < 2 else None)
            slot_enabled[NSLOT] = True
            S.fence(R_Sb2, R_tmpA2 + R_tmpB2)
            s = chunk(3)
            for m4 in range(4):
                banks = projA(s, m4, hT, allhT, coreblocks, 8)
                for bi, b in enumerate(banks):
                    S.op("scalar", lambda e, b=b, bi=bi, m4=m4: e.copy(bT[:, m4, bi * 512:(bi + 1) * 512], ps[b][:, :]),
                         reads=[R_ps[b]], writes=[R_b], partial=True)
            S.op("gpsimd", lambda e: e.memset(uT[:, :, :], 0.0), writes=[R_u])
            s1, s2 = chunk(4, 2)
            ublocks = extblocks
            for m4 in range(4):
                bk1 = projA(s1, m4, hT, allhT, ublocks, 8)
                bk2 = projA(s2, m4, hT, allhT, ublocks, 8)
                for bi, (c0_, n) in enumerate(ublocks):
                    pt_ = flip(); tmpA, tmpB, R_tmpA, R_tmpB = tmpA2[pt_], tmpB2[pt_], R_tmpA2[pt_], R_tmpB2[pt_]
                    S.op("scalar", lambda e, tmpA=tmpA, b=bk1[bi], n=n: e.copy(tmpA[:, 0:n], ps[b][:, 0:n]), reads=[R_ps[bk1[bi]]], writes=[R_tmpA])
                    if prompt:
                        for hs in range(2):
                            t0 = c0_ + hs * 256
                            S.op("vector", lambda e, tmpA=tmpA, b=bk2[bi], hs=hs, t0=t0, m4=m4: e.tensor_tensor(
                                out=uT[:, m4, ucol(t0):ucol(t0) + 256], in0=tmpA[:, hs * 256:(hs + 1) * 256],
                                in1=ps[b][:, hs * 256:(hs + 1) * 256], op=ALU.mult),
                                reads=[R_tmpA, R_ps[bk2[bi]]], writes=[R_u], partial=True)
                    else:
                        S.op("vector", lambda e, tmpA=tmpA, b=bk2[bi], c0_=c0_, n=n, m4=m4: e.tensor_tensor(
                            out=uT[:, m4, c0_ + 1:c0_ + 1 + n], in0=tmpA[:, 0:n], in1=ps[b][:, 0:n], op=ALU.mult),
                            reads=[R_tmpA, R_ps[bk2[bi]]], writes=[R_u], partial=True)

            ranges = [(sq * 256, 256) for sq in range(4)] if prompt else [(0, 512), (512, 512)]
            for m4 in range(4):
                for (t0, n) in ranges:
                    pt_ = flip(); tmpA, tmpB, R_tmpA, R_tmpB = tmpA2[pt_], tmpB2[pt_], R_tmpA2[pt_], R_tmpB2[pt_]
                    u0 = ucol(t0)
                    S.op("vector", lambda e, tmpB=tmpB, m4=m4, u0=u0, n=n: e.tensor_scalar(tmpB[:, 0:n], uT[:, m4, u0:u0 + n], convw[:, m4, 1:2], None, ALU.mult),
                         reads=[R_u], writes=[R_tmpB])
                    S.op("vector", lambda e, tmpB=tmpB, m4=m4, u0=u0, n=n: e.scalar_tensor_tensor(
                        out=tmpB[:, 0:n], in0=uT[:, m4, u0 - 1:u0 - 1 + n], scalar=convw[:, m4, 0:1], in1=tmpB[:, 0:n], op0=ALU.mult, op1=ALU.add),
                        reads=[R_u, R_tmpB], writes=[R_tmpB])
                    S.op("vector", lambda e, tmpB=tmpB, m4=m4, u0=u0, n=n: e.scalar_tensor_tensor(
                        out=tmpB[:, 0:n], in0=uT[:, m4, u0 + 1:u0 + 1 + n], scalar=convw[:, m4, 2:3], in1=tmpB[:, 0:n], op0=ALU.mult, op1=ALU.add),
                        reads=[R_u, R_tmpB], writes=[R_tmpB])
                    S.op("vector", lambda e, tmpB=tmpB, m4=m4, t0=t0, n=n: e.tensor_tensor(out=yconvT[:, m4, t0:t0 + n], in0=tmpB[:, 0:n],
                                                                               in1=bT[:, m4, t0:t0 + n], op=ALU.mult),
                         reads=[R_tmpB, R_b], writes=[R_yconv], partial=True)

            _stage(sbi * 10 + 4)
            S.fence([R_b, R_u], [R_mg])
            cb = [(0, 512), (512, 512)]
            for mg in range(2):
                sgc, sga, sbr = chunk(6 + mg * 3, 3)
                for m4 in range(4):
                    m = mg * 4 + m4
                    for (t0, n) in cb:
                        pt_ = flip(); tmpA, tmpB, R_tmpA, R_tmpB = tmpA2[pt_], tmpB2[pt_], R_tmpA2[pt_], R_tmpB2[pt_]
                        (bgc,) = projA(sgc, m4, hT, allhT, [(ccol + t0, n)], 8)
                        (bbc,) = projA(sbr, m4, yconvT, [R_yconv], [(t0, n)], 4)
                        (bga,) = projA(sga, m4, hT, allhT, [(ccol + t0, n)], 8)
                        (bba,) = projA(sbr, m4, yattT, [R_yatt], [(t0, n)], 4, kmap=lambda kc: kc + 4)
                        S.op("scalar", lambda e, tmpA=tmpA, b=bgc: e.activation(out=tmpA[:, :], in_=ps[b][:, :], func=AF.Sigmoid),
                             reads=[R_ps[bgc]], writes=[R_tmpA])
                        S.op("vector", lambda e, tmpA=tmpA, b=bbc: e.tensor_tensor(out=tmpA[:, :], in0=tmpA[:, :], in1=ps[b][:, :], op=ALU.mult),
                             reads=[R_tmpA, R_ps[bbc]], writes=[R_tmpA])
                        S.op("scalar", lambda e, tmpB=tmpB, b=bga: e.activation(out=tmpB[:, :], in_=ps[b][:, :], func=AF.Sigmoid),
                             reads=[R_ps[bga]], writes=[R_tmpB])
                        S.op("vector", lambda e, tmpB=tmpB, b=bba: e.tensor_tensor(out=tmpB[:, :], in0=tmpB[:, :], in1=ps[b][:, :], op=ALU.mult),
                             reads=[R_tmpB, R_ps[bba]], writes=[R_tmpB])
                        S.op("gpsimd", lambda e, tmpA=tmpA, tmpB=tmpB, m=m, t0=t0, n=n: e.tensor_tensor(out=mgT[:, m, t0:t0 + n], in0=tmpA[:, 0:n], in1=tmpB[:, 0:n], op=ALU.add),
                             reads=[R_tmpA, R_tmpB], writes=[R_mg], partial=True)

            _stage(sbi * 10 + 5)
            S.fence([R_BT], R_x1)
            S.fence(R_hT, R_h2T)
            so = chunk(12, 2)
            def wo_mm(t):
                bk = [nb(), nb()]
                for nh in range(2):
                    for kc in range(8):
                        S.op("tensor", lambda e, nh=nh, kc=kc, t=t, bk=bk: e.matmul(ps[bk[nh]][:, :], lhsT=mgT[:, kc, t * 128:(t + 1) * 128],
                                                                            rhs=ring[so[nh]][:, kc, :], start=(kc == 0), stop=(kc == 7)),
                             reads=[R_mg, R_ring[so[nh]]], writes=[R_ps[bk[nh]]], partial=True)
                return bk
            bks = {0: wo_mm(0)}
            for t in range(8):
                if t + 1 < 8:
                    bks[t + 1] = wo_mm(t + 1)
                bk = bks[t]
                xi = nxt("xin")
                S.dma("sync", xin[xi][:], xsrc[xrow0 + (c0t + t) * 128: xrow0 + (c0t + t + 1) * 128, :], key=R_xin[xi], writes=[R_xin[xi]])
                post_norm_residual(bk, None, xin[xi], R_xin[xi], GG[0], x1buf[:, t, :], R_x1[t])
                if t >= 2:
                    norm_transpose(x1buf[:, t - 2, :], R_x1[t - 2], h2T, R_h2T[t - 2], (t - 2) * 128, goff + 2, goff + 3, defer=True)
            for t in (6, 7):
                norm_transpose(x1buf[:, t, :], R_x1[t], h2T, R_h2T[t], t * 128, goff + 2, goff + 3, defer=True)
            nt_flush()

            _stage(sbi * 10 + 6)
            S.fence([R_q, R_k, R_v, R_yatt, R_yconv], [R_aT])
            allh2 = R_h2T
            for g in range(6):
                s1, s3 = chunk(14 + 2 * g, 2)
                for m4 in range(4 if g < 5 else 2):
                    m = g * 4 + m4
                    for (t0, n) in cb:
                        pt_ = flip(); tmpA, tmpB, R_tmpA, R_tmpB = tmpA2[pt_], tmpB2[pt_], R_tmpA2[pt_], R_tmpB2[pt_]
                        (b1,) = projA(s1, m4, h2T, allh2, [(t0, n)], 8)
                        (b3,) = projA(s3, m4, h2T, allh2, [(t0, n)], 8)
                        S.op("scalar", lambda e, tmpA=tmpA, b=b1: e.activation(out=tmpA[:, :], in_=ps[b][:, :], func=AF.Silu),
                             reads=[R_ps[b1]], writes=[R_tmpA])
                        S.op("vector", lambda e, tmpA=tmpA, b=b3, m=m, t0=t0, n=n: e.tensor_tensor(out=aT[:, m, t0:t0 + n], in0=tmpA[:, 0:n], in1=ps[b][:, 0:n], op=ALU.mult),
                             reads=[R_tmpA, R_ps[b3]], writes=[R_aT], partial=True)

            _stage(sbi * 10 + 7)
            S.fence(R_tmpA2 + R_tmpB2, R_Sb2)
            for nh in range(2):
                sl = chunk(26 + nh * 3, 3, disable_ctx_slot=(nh == 1 and sbi < 2))
                for t in range(8):
                    b = nb()
                    for kc in range(22):
                        S.op("tensor", lambda e, b=b, kc=kc, t=t, sl=sl: e.matmul(ps[b][:, :], lhsT=aT[:, kc, t * 128:(t + 1) * 128],
                                                                          rhs=ring[sl[kc // 8]][:, kc % 8, :], start=(kc == 0), stop=(kc == 21)),
                             reads=[R_aT, R_ring[sl[kc // 8]]], writes=[R_ps[b]], partial=True)
                    if nh == 0:
                        S.op("vector", lambda e, b=b, t=t: e.tensor_copy(y0ap(t), ps[b][:, :]), reads=[R_ps[b]], writes=[R_y0[t]], partial=True)
                        pj = flip()
                        S.op("scalar", lambda e, t=t, pj=pj: e.activation(out=Pb2[pj][:, 0:512], in_=y0ap(t), func=AF.Square, accum_out=ss0[:, t:t + 1]),
                             reads=[R_y0[t]], writes=[R_Pb2[pj]], pwrites=[R_ss0])
                    else:
                        oi = nxt("ost")
                        post_norm_residual([None, b], (y0ap(t), R_y0[t], ss0[:, t:t + 1]), None, R_x1[t], GG[1], ost[oi][:], R_ost[oi],
                                           xres=x1buf[:, t, :])
                        S.dma("sync", ydst[yrow0 + t * 128: yrow0 + (t + 1) * 128, :], ost[oi][:], key=R_ost[oi], reads=[R_ost[oi]])

        def post_norm_residual(bk, half0, xres_t, R_xres, GGt, dst, R_dst, xres=None):
            p = flip()
            junk, R_junk, stat, R_stat = Pb2[p][:, 0:1024], R_Pb2[p], stat2[p], R_stat2[p]
            if xres is None:
                xres = xres_t[:]
            if bk[0] is not None:
                S.op("scalar", lambda e: e.activation(out=junk[:, 0:512], in_=ps[bk[0]][:, :], func=AF.Square, accum_out=stat[:, 8:9]),
                     reads=[R_ps[bk[0]]], writes=[R_junk, R_stat])
                s0 = stat[:, 8:9]
                rs0 = [R_stat]
            else:
                s0 = half0[2]
                rs0 = [R_ss0]
            S.op("scalar", lambda e: e.activation(out=junk[:, 512:1024], in_=ps[bk[1]][:, :], func=AF.Square, accum_out=stat[:, 9:10]),
                 reads=[R_ps[bk[1]]], writes=[R_junk, R_stat])
            S.op("vector", lambda e: e.tensor_tensor(out=stat[:, 10:11], in0=s0, in1=stat[:, 9:10], op=ALU.add), reads=[R_stat] + rs0, writes=[R_stat])
            S.op("scalar", lambda e: e.activation(out=stat[:, 11:12], in_=stat[:, 10:11], func=AF.Ln, scale=1.0 / D, bias=EPS), reads=[R_stat], writes=[R_stat])
            S.op("scalar", lambda e: e.activation(out=stat[:, 12:13], in_=stat[:, 11:12], func=AF.Exp, scale=-0.5), reads=[R_stat], writes=[R_stat])
            for nh in range(2):
                if bk[nh] is not None:
                    src, rsrc = ps[bk[nh]][:, :], [R_ps[bk[nh]]]
                else:
                    src, rsrc = half0[0], [half0[1]]
                lo, hi = nh * 512, (nh + 1) * 512
                S.op("vector", lambda e, src=src, lo=lo, hi=hi: e.scalar_tensor_tensor(out=dst[:, lo:hi], in0=src, scalar=stat[:, 12:13], in1=GGt[:, lo:hi],
                                                                                     op0=ALU.mult, op1=ALU.mult),
                     reads=rsrc + [R_stat, R_GG], writes=[R_dst], partial=(nh == 1))
            S.op("gpsimd", lambda e: e.tensor_tensor(out=dst, in0=dst, in1=xres, op=ALU.add), reads=[R_dst, R_xres], writes=[R_dst])

        def attention_unit_s(qi, hh, segs, pv):
            pv2 = []
            for i, item in enumerate(pv):
                if item is None:
                    continue
                pv2.append((i,) + item)
            attention_core(qi, hh, segs, pv2, [R_k, R_v, R_ctx])

        att_units = []
        att_ctr = {"n": 0}

        def attention_core(qi, hh, segs, pv, R_kv):
            u = att_ctr["n"]
            att_ctr["n"] += 1
            hp, ho = hh // 2, (hh % 2) * 64
            Sb, R_Sb = Sb2[u % 2], R_Sb2[u % 2]
            Pb, R_Pb = Pb2[u % 3], R_Pb2[u % 3]
            PTs, R_PTs = PTs2[u % 2], R_PTs2[u % 2]
            stat, R_stat = stat2[u % 5], R_stat2[u % 5]
            qap = qT[ho:ho + 64, hp, qi * 128:(qi + 1) * 128]
            ncol = sum(n for (_, n, _) in segs)

            def stA():
                col = 0
                first = True
                for (rhs, n, bias) in segs:
                    off = 0
                    while off < n:
                        w = min(512, n - off)
                        b = nb()
                        S.op("tensor", lambda e, b=b, rhs=rhs, off=off, w=w: e.matmul(ps[b][:, 0:w], lhsT=qap, rhs=rhs[:, off:off + w], start=True, stop=True),
                             reads=[R_q] + R_kv, writes=[R_ps[b]])
                        c = col + off
                        if bias is None:
                            S.op("scalar", lambda e, b=b, c=c, w=w: e.copy(Sb[:, c:c + w], ps[b][:, 0:w]), reads=[R_ps[b]], writes=[R_Sb], partial=not first)
                            first = False
                        else:
                            blk0 = bias + off // 64

                            def badd(p0, p1, c0, c1, fp, b=b, c=c, blk0=blk0):
                                S.op("vector", lambda e: e.tensor_tensor(
                                    out=Sb[p0:p1, c + c0:c + c1], in0=ps[b][p0:p1, c0:c1],
                                    in1=BT[p0:p1, hh, blk0 + c0 // 64:blk0 + c1 // 64, :].rearrange("p r w -> p (r w)"), op=ALU.add),
                                    reads=[R_ps[b], R_BT], writes=[R_Sb], partial=not fp)
                            if n == 576 and off == 0:
                                badd(0, 128, 64, 512, first)
                                first = False
                                badd(0, 64, 0, 64, False)
                                S.op("vector", lambda e, c=c: e.memset(Sb[64:128, c:c + 64], NEG), writes=[R_Sb], partial=True)
                            elif n == 576:
                                badd(64, 128, 0, 64, first)
                                first = False
                                S.op("vector", lambda e, c=c: e.memset(Sb[0:64, c:c + 64], NEG), writes=[R_Sb], partial=True)
                            else:
                                badd(0, 128, 0, w, first)
                                first = False
                        off += w
                    col += n

            def stB():
                S.op("vector", lambda e: e.reduce_max(stat[:, 4:5], Sb[:, 0:ncol], axis=AX.X), reads=[R_Sb], writes=[R_stat])
                S.op("vector", lambda e: e.tensor_scalar(stat[:, 6:7], stat[:, 4:5], -1.0, None, ALU.mult), reads=[R_stat], writes=[R_stat])
                S.op("scalar", lambda e: e.activation(out=Pb[:, 0:ncol], in_=Sb[:, 0:ncol], func=AF.Exp, bias=stat[:, 6:7], scale=1.0, accum_out=stat[:, 5:6]),
                     reads=[R_Sb, R_stat], writes=[R_Pb, R_stat])

            def stC1():
                S.op("vector", lambda e: e.reciprocal(stat[:, 7:8], stat[:, 5:6]), reads=[R_stat], writes=[R_stat])
                if ncol > 600:
                    nsplit = ncol - 512
                    S.op("vector", lambda e: e.tensor_scalar(Pb[:, 0:nsplit], Pb[:, 0:nsplit], stat[:, 7:8], None, ALU.mult),
                         reads=[R_stat], pwrites=[R_Pb])
                    S.op("scalar", lambda e: e.activation(out=Pb[:, nsplit:ncol], in_=Pb[:, nsplit:ncol], func=AF.Copy, scale=stat[:, 7:8]),
                         reads=[R_stat], pwrites=[R_Pb])
                else:
                    S.op("scalar", lambda e: e.activation(out=Pb[:, 0:ncol], in_=Pb[:, 0:ncol], func=AF.Copy, scale=stat[:, 7:8]), reads=[R_Pb, R_stat], writes=[R_Pb])

            used0 = [p for p in pv if p[0] < 5]
            used1 = [p for p in pv if p[0] >= 5]

            def stC2():
                for (i, c0, K, _) in pv:
                    t = 0 if i < 5 else 1
                    sl = i if i < 5 else i - 5
                    S.op("tensor", lambda e, t=t, sl=sl, c0=c0, K=K: e.transpose(pst[t][0:K, sl, :], Pb[:, c0:c0 + K], ident[:]),
                         reads=[R_Pb, R_const], writes=[R_pst[t]], partial=True)
                n0 = max(p[0] for p in used0) + 1
                S.op("vector", lambda e: e.tensor_copy(PTs[:, 0:n0, :], pst[0][:, 0:n0, :]), reads=[R_pst[0]], writes=[R_PTs])
                if used1:
                    n1 = max(p[0] for p in used1) - 4
                    S.op("scalar", lambda e: e.copy(PTs[:, 5:5 + n1, :], pst[1][:, 0:n1, :]), reads=[R_pst[1]], writes=[R_PTs], partial=True)

            def stD():
                ob = nb()
                for j, (i, c0, K, vap) in enumerate(pv):
                    S.op("tensor", lambda e, j=j, i=i, K=K, vap=vap: e.matmul(ps[ob][:, 0:128], lhsT=vap, rhs=PTs[0:K, i, :],
                                                                             start=(j == 0), stop=(j == len(pv) - 1)),
                         reads=[R_PTs] + R_kv, writes=[R_ps[ob]], partial=True)
                S.op("scalar", lambda e: e.copy(yattT[ho:ho + 64, hp, qi * 128:(qi + 1) * 128], ps[ob][ho:ho + 64, 0:128]),
                     reads=[R_ps[ob]], writes=[R_yatt], partial=True)

            att_units.append((stA, stB, stC1, stC2, stD))

        def att_flush(side=None):
            n = len(att_units)
            for j in range(n + 4):
                for st in range(5):
                    u = j - st
                    if 0 <= u < n:
                        att_units[u][st]()
                if side is not None and j % 8 == 7:
                    next(side, None)
            if side is not None:
                for _ in side:
                    pass
            att_units.clear()

        def attention_unit(qi, hh, segs, pv, R_kv):
            attention_core(qi, hh, segs, [(i,) + p for i, p in enumerate(pv)], R_kv)

        def load_ctx():
            ctmp = ost[0][:, :].bitcast(BF16).rearrange("p (c f) -> p c f", c=4)
            for c in range(4):
                S.dma("gpsimd", ctmp[:, c, :].rearrange("p (h d) -> p h d", h=8), ck[:, c * 128:(c + 1) * 128, :].rearrange("h p d -> p h d"),
                      key=R_ctmpk, writes=[R_ost[0]])
                S.dma("gpsimd", vctx[:, c, :].rearrange("p (h d) -> p h d", h=8), cv[:, c * 128:(c + 1) * 128, :].rearrange("h p d -> p h d"),
                      key=R_ctx, writes=[R_ctx])
            for hp in range(4):
                t = nxt("pst")
                for c in range(4):
                    S.op("tensor", lambda e, t=t, c=c, hp=hp: e.transpose(pst[t][:, c, :], ctmp[:, c, hp * 128:(hp + 1) * 128], ident[:]),
                         reads=[R_ost[0], R_const], writes=[R_pst[t]], partial=True)
                S.op("vector", lambda e, t=t, hp=hp: e.tensor_copy(kctxT[:, hp, :].rearrange("p (c k) -> p c k", c=4), pst[t][:, 0:4, :]),
                     reads=[R_pst[t]], writes=[R_ctx], partial=True)

        R_btimg = S.region("btimg")

        def build_btimg():
            negt = Sb2[0]
            S.op("vector", lambda e: e.memset(negt[0:64, 0:960], NEG), writes=[R_Sb2[0]])
            for k in range(8):
                S.dma("sync", btimg[:, k * 960:(k + 1) * 960], negt[0:64, 0:960], key=R_btimg, reads=[R_Sb2[0]], writes=[R_btimg])
            img_h = btimg.tensor
            dst = bass.AP(tensor=img_h, offset=8 * 7680 + 0, ap=[[7680 + 1, 49], [64, 120], [1, 16]])
            src = bass.AP(tensor=rpb_t, offset=7, ap=[[0, 49], [31, 120], [1, 16]])
            S.dma("sync", dst, src, key=R_btimg2, reads=[R_btimg], writes=[R_btimg])
            for wq in list(range(0, 8)) + list(range(57, 64)):
                cc = min(max(wq - 8, 0), 48)
                a = 15 + cc - wq
                dst = bass.AP(tensor=img_h, offset=wq * 7680 + cc, ap=[[64, 120], [1, 16]])
                src = bass.AP(tensor=rpb_t, offset=a, ap=[[31, 120], [1, 16]])
                S.dma("sync", dst, src, key=R_btimg2, reads=[R_btimg], writes=[R_btimg])

        R_btimg2 = S.region("btimg2")

        base = NMOD
        try:
            _stage(-3)
            setup_mod()
            _stage(-2)
            group_pcol(0)
            group_pcol(1)
            _stage(0)
            run_sb(0, base)
            _stage(8)
            _stage(9)
            run_sb(1, base + NPER)
            _stage(18)
            run_sb(2, base + 2 * NPER)
        except _Stop:
            pass

        S.emit(nc, st)
    return nc


_NC_CACHE = {}


def kernel(x_prompt, x_sample, cache_k, cache_v, c, c_ctx, w_mod, b_mod, g_pre_mix, g_post_mix, g_pre_ffn, g_post_ffn,
           w_in, w_conv, w_br_conv, w_br_attn, rpb, w_gate, w_o, w_ff1, w_ff3, w_ff2):
    f = lambda a: np.ascontiguousarray(np.asarray(a, dtype=np.float32))
    x_prompt, x_sample, cache_k, cache_v, c, c_ctx = map(f, (x_prompt, x_sample, cache_k, cache_v, c, c_ctx))
    if "nc" not in _NC_CACHE:
        _NC_CACHE["nc"] = build()
    nc = _NC_CACHE["nc"]

    def col(v):
        return f(v).reshape(8, 128).T

    shared = {
        "bmodcol": f(f(b_mod).reshape(48, 128).T),
        "gcol": f(np.stack([col(g_pre_mix[0]), col(g_post_mix[0]), col(g_pre_ffn[0]), col(g_post_ffn[0])], axis=1)),
        "convw": f(f(w_conv)[0].reshape(3, 4, 128).transpose(2, 1, 0)),
        "rpb": f(rpb)[0],
        "w_mod": f(w_mod)[0], "w_in": f(w_in)[0], "w_brc": f(w_br_conv)[0], "w_bra": f(w_br_attn)[0],
        "w_gate": f(w_gate)[0], "w_o": f(w_o)[0], "w_ff1": f(w_ff1)[0], "w_ff3": f(w_ff3)[0], "w_ff2": f(w_ff2)[0],
    }
    in_maps = []
    for i in range(8):
        m = dict(shared)
        m["xp"] = f(x_prompt[4 * i:4 * i + 4].reshape(1024, D))
        m["xs"] = f(x_sample[i])
        m["ck"] = f(cache_k[i, 0])
        m["cv"] = f(cache_v[i, 0])
        m["c2col"] = f(np.stack([col(c_ctx), col(c[i])], axis=2))
        in_maps.append(m)
    res = run_bass_kernel_spmd(nc, in_maps, core_ids=list(range(8)))
    r = res.results
    y_p = np.concatenate([r[i]["yp"].reshape(4, 256, D) for i in range(8)], axis=0)
    y_s = np.stack([r[i]["ys"] for i in range(8)], axis=0)
    n_k = np.concatenate([r[i]["nk"] for i in range(8)], axis=0)[:, None]
    n_v = np.concatenate([r[i]["nv"] for i in range(8)], axis=0)[:, None]
    return (y_p.astype(np.float32), y_s.astype(np.float32), n_k.astype(np.float32), n_v.astype(np.float32))
```

```python
from contextlib import ExitStack
import numpy as np
import concourse.bass as bass
import concourse.mybir as mybir
from concourse.bass_utils import run_bass_kernel_spmd

F32 = mybir.dt.float32
BF16 = mybir.dt.bfloat16
AF = mybir.ActivationFunctionType
ALU = mybir.AluOpType
AX = mybir.AxisListType

SAME_ENGINE_SYNC = True
D = 1024
DFF = 2816
EPS = 1e-6
NEG = -1e30
NSLOT = 4
NTP = 9
STAGE = 999


class _Stop(Exception):
    pass


def _stage(n):
    if STAGE in (-11, -12, -13):
        if n >= 0:
            raise _Stop()
        return
    if n > STAGE:
        raise _Stop()


class Region:
    __slots__ = ("name", "writers", "readers", "war", "dma_n", "sem", "full")

    def __init__(self, name):
        self.name = name
        self.full = []
        self.writers = []
        self.readers = []
        self.war = []
        self.dma_n = 0
        self.sem = None


class Op:
    __slots__ = ("eng", "fn", "deps", "needs_inc", "val", "is_dma", "dma_reg", "dma_val")

    def __init__(self, eng, fn, is_dma=False):
        self.eng = eng
        self.fn = fn
        self.deps = []
        self.needs_inc = False
        self.val = None
        self.is_dma = is_dma
        self.dma_reg = None
        self.dma_val = None


COMPUTE = ("tensor", "vector", "scalar", "gpsimd")
QUEUES = ("tensor", "vector", "scalar", "gpsimd", "sync")


class Sched:
    def __init__(self):
        self.ops = {q: [] for q in QUEUES}
        self.dma_regions = []

    def region(self, name):
        return Region(name)

    def fence(self, olds, news):
        ops = []
        for r in olds:
            ops += r.readers + r.writers
        for r in news:
            r.readers = list(r.readers) + ops

    def _track(self, op, reads, writes, partial, pwrites=()):
        deps = []
        for r in reads:
            deps.extend(r.writers)
            r.readers.append(op)
        wl = [(r, partial) for r in writes] + [(r, True) for r in pwrites]
        for r, partial in wl:
            if partial and not r.readers and r.writers:
                deps.extend(r.war)
                deps.extend(r.full)
                r.writers.append(op)
            else:
                war = list(r.readers) + list(r.writers)
                deps.extend(war)
                r.war = war
                r.writers = [op]
                r.readers = []
                r.full = [] if partial else [op]
        seen = set()
        for d in deps:
            if d is op or id(d) in seen:
                continue
            seen.add(id(d))
            if d.eng == op.eng and not d.is_dma:
                if op.eng == "tensor" or not SAME_ENGINE_SYNC:
                    continue
            op.deps.append(d)
            if not d.is_dma:
                d.needs_inc = True

    def op(self, eng, fn, reads=(), writes=(), partial=False, pwrites=()):
        o = Op(eng, fn)
        self._track(o, reads, writes, partial, pwrites)
        self.ops[eng].append(o)
        return o

    def dma(self, eng, out, in_, key, reads=(), writes=(), partial=True, **kw):
        def fn(e, out=out, in_=in_, kw=kw):
            return e.dma_start(out=out, in_=in_, **kw)
        o = Op(eng, fn, is_dma=True)
        o.dma_reg = key
        key.dma_n += 1
        o.dma_val = 16 * key.dma_n
        if key not in self.dma_regions:
            self.dma_regions.append(key)
        self._track(o, reads, writes, partial)
        self.ops[eng].append(o)
        return o

    def emit(self, nc, stack):
        esem = {q: stack.enter_context(nc.semaphore(f"e_{q}")) for q in COMPUTE}
        for i, r in enumerate(self.dma_regions):
            r.sem = stack.enter_context(nc.semaphore(f"d{i}_{r.name}"))
        for q in QUEUES:
            c = 0
            for o in self.ops[q]:
                if not o.is_dma and o.needs_inc:
                    c += 1
                    o.val = c
        block = stack.enter_context(nc.Block())
        all_dma = [(r.sem, 16 * r.dma_n) for r in self.dma_regions]

        def make(q):
            def body(e):
                waited = {}
                for o in self.ops[q]:
                    need = {}
                    for d in o.deps:
                        if d.is_dma:
                            sem, v = d.dma_reg.sem, d.dma_val
                        else:
                            sem, v = esem[d.eng], d.val
                        k = id(sem)
                        if k not in need or need[k][1] < v:
                            need[k] = (sem, v)
                    for k, (sem, v) in need.items():
                        if waited.get(k, 0) >= v:
                            continue
                        waited[k] = v
                        e.wait_ge(sem, v)
                    ins = o.fn(e)
                    if o.is_dma:
                        ins.then_inc(o.dma_reg.sem, 16)
                    elif o.needs_inc:
                        ins.then_inc(esem[q], 1)
                if q == "sync":
                    for sem, v in all_dma:
                        if waited.get(id(sem), 0) < v:
                            e.wait_ge(sem, v)
            return body

        block.tensor(make("tensor"))
        block.vector(make("vector"))
        block.scalar(make("scalar"))
        block.gpsimd(make("gpsimd"))
        block.sync(make("sync"))


def build():
    nc = bass.Bass("TRN2", target_bir_lowering=False)

    def din(name, shape):
        return nc.dram_tensor(name, shape, F32, kind="ExternalInput").ap()

    def dout(name, shape):
        return nc.dram_tensor(name, shape, F32, kind="ExternalOutput").ap()

    xp = din("xp", [1024, D])
    xs = din("xs", [2048, D])
    ck = din("ck", [8, 512, 64])
    cv = din("cv", [8, 512, 64])
    c2col = din("c2col", [128, 8, 2])
    bmodcol = din("bmodcol", [128, 48])
    gcol_d = din("gcol", [128, 4, 8])
    convw_d = din("convw", [128, 4, 3])
    rpb = din("rpb", [8, 15, 31])
    w_mod = din("w_mod", [D, 6 * D])
    w_in = din("w_in", [D, 3072])
    w_brc = din("w_brc", [512, D])
    w_bra = din("w_bra", [512, D])
    w_gate = din("w_gate", [D, 2 * D])
    w_o = din("w_o", [D, D])
    w_ff1 = din("w_ff1", [D, DFF])
    w_ff3 = din("w_ff3", [D, DFF])
    w_ff2 = din("w_ff2", [DFF, D])
    yp = dout("yp", [1024, D])
    ys = dout("ys", [2048, D])
    nk = dout("nk", [4, 8, 256, 64])
    nv = dout("nv", [4, 8, 256, 64])

    btimg_t = nc.dram_tensor("btimg", [64, 7680], F32, kind="Internal")
    btimg = btimg_t.ap()
    rpb_t = rpb.tensor

    S = Sched()
    with ExitStack() as st:
        def sb(name, shape, dt):
            return st.enter_context(nc.sbuf_tensor(name, shape, dt))

        NS = NSLOT + 1
        ring = [sb(f"ring{i}", [128, 8, 512], BF16) for i in range(NS)]
        R_ring = [S.region(f"ring{i}") for i in range(NS)]
        xin = [sb(f"xin{i}", [128, D], F32) for i in range(2)]
        R_xin = [S.region(f"xin{i}") for i in range(2)]
        ost = [sb(f"ost{i}", [128, D], F32) for i in range(2)]
        R_ost = [S.region(f"ost{i}") for i in range(2)]
        GG = [sb(f"GG{i}", [128, D], F32) for i in range(2)]
        R_GG = S.region("GG")
        kctxT = ring[NSLOT][:, 0:4, :]
        vctx = ring[NSLOT][:, 4:8, :]
        R_ctx = R_ring[NSLOT]
        R_ctmpk = S.region("ctmpk")
        Sb2 = [sb(f"Sb{i}", [128, 1088], F32) for i in range(2)]
        R_Sb2 = [S.region(f"Sb{i}") for i in range(2)]
        Pb2 = [sb(f"Pb{i}", [128, 1088], BF16) for i in range(3)]
        R_Pb2 = [S.region(f"Pb{i}") for i in range(3)]
        PTs2 = [sb(f"PTs{i}", [128, 9, 128], BF16) for i in range(2)]
        R_PTs2 = [S.region(f"PTs{i}") for i in range(2)]
        stat2 = [sb(f"stat{i}", [128, 16], F32) for i in range(5)]
        R_stat2 = [S.region(f"stat{i}") for i in range(5)]
        tmpA2 = [Sb2[i][:, 0:512] for i in range(2)]
        tmpB2 = [Sb2[i][:, 512:1024] for i in range(2)]
        R_tmpA2 = [S.region(f"tmpA{i}") for i in range(2)]
        R_tmpB2 = [S.region(f"tmpB{i}") for i in range(2)]
        par = {"v": 0}

        def flip():
            par["v"] ^= 1
            return par["v"]

        ident = sb("ident", [128, 128], BF16)
        identf = sb("identf", [128, 128], F32)
        diag = sb("diag", [128, 128], F32)
        dhi = sb("dhi", [128, 128], BF16)
        dlo = sb("dlo", [128, 128], BF16)
        onesb = sb("onesb", [128, 128], BF16)
        R_dhi = S.region("dhi")
        R_dlo = S.region("dlo")
        R_const = S.region("const")
        R_diag = S.region("diag")
        cT = sb("cT", [128, 8, 2], F32)
        scT = sb("scT", [128, 8, 2], BF16)
        bmc = sb("bmc", [128, 48], F32)
        modcol = sb("modcol", [128, 48, 2], F32)
        gcol = sb("gcolsb", [128, 4, 8], F32)
        convw = sb("convwsb", [128, 4, 3], F32)
        pcol = sb("pcol", [128, 12, 8], F32)
        R_mod = S.region("mod")
        R_pcol = S.region("pcol")
        ss0 = sb("ss0", [128, 8], F32)
        R_ss0 = S.region("ss0")

        FA = sb("FA", [128, 8256], F32)
        x1buf = FA[:, 0:8192].rearrange("p (t d) -> p t d", t=8)
        BT = FA[:, 0:7680].rearrange("p (h r w) -> p h r w", h=8, r=15)
        BTf_top = FA[:, 0:7680].rearrange("p (n w) -> p n w", w=64)
        BTf_bot = FA[:, 64:64 + 7680].rearrange("p (n w) -> p n w", w=64)
        hT = sb("HT", [128, 8, 1280], BF16)
        h2T = hT[:, :, 0:1024]
        BU = sb("BU", [128, 9232], BF16)
        UW = 1284
        bT = BU[:, 0:4096].rearrange("p (c t) -> p c t", c=4)
        uT = BU[:, 4096:4096 + 4 * UW].rearrange("p (c t) -> p c t", c=4)
        mgT = BU[:, 0:8192].rearrange("p (k t) -> p k t", k=8)
        ATb = sb("ATb", [128, 22528], BF16)
        aT = ATb[:, :].rearrange("p (k t) -> p k t", k=22)
        qT = ATb[:, 0:4096].rearrange("p (c t) -> p c t", c=4)
        kT = ATb[:, 4096:9216].rearrange("p (c t) -> p c t", c=4)
        vv = ATb[:, 9216:14336].rearrange("p (t f) -> p t f", t=10)
        yattT = ATb[:, 14336:18432].rearrange("p (c t) -> p c t", c=4)
        yconvT = ATb[:, 18432:22528].rearrange("p (c t) -> p c t", c=4)

        def y0ap(t):
            if t < 4:
                return xin[t // 2][:, (t % 2) * 512:(t % 2 + 1) * 512]
            return Sb2[(t - 4) // 2][:, ((t - 4) % 2) * 512:((t - 4) % 2 + 1) * 512]

        R_x1 = [S.region(f"x1_{i}") for i in range(8)]
        R_BT = S.region("BT")
        R_hT = [S.region(f"hT{i}") for i in range(10)]
        R_h2T = [S.region(f"h2T{i}") for i in range(8)]
        R_y0 = [R_xin[0], R_xin[0], R_xin[1], R_xin[1], R_Sb2[0], R_Sb2[0], R_Sb2[1], R_Sb2[1]]
        R_b, R_u, R_mg = S.region("b"), S.region("u"), S.region("mg")
        R_q, R_k, R_v = S.region("q"), S.region("k"), S.region("v")
        R_yatt, R_yconv, R_aT = S.region("yatt"), S.region("yconv"), S.region("aT")

        ps = [st.enter_context(nc.psum_tensor(f"ps{i}", [128, 512], F32)) for i in range(6)]
        pst = [st.enter_context(nc.psum_tensor(f"pst{i}", [128, 8, 128], BF16)) for i in range(2)]
        R_ps = [S.region(f"ps{i}") for i in range(6)]
        R_pst = [S.region(f"pst{i}") for i in range(2)]
        rr = {"ps": 0, "pst": 0, "xin": 0, "ost": 0}

        def nb():
            i = rr["ps"]
            rr["ps"] = (i + 1) % 6
            return i

        def nxt(key, n=2):
            i = rr[key]
            rr[key] = (i + 1) % n
            return i

        chunks = []
        state = {"issued": 0}
        slot_of = {}
        slot_free = [True] * NS
        slot_enabled = [True] * NS

        def ensure(i, i1=None, disable_ctx_slot=False):
            i1 = i if i1 is None else i1
            for j in list(slot_of):
                if j < i:
                    slot_free[slot_of.pop(j)] = True
            if disable_ctx_slot:
                slot_enabled[NSLOT] = False
            while state["issued"] < len(chunks):
                cand = [q for q in range(NS) if slot_free[q] and slot_enabled[q]]
                if not cand:
                    break
                j = state["issued"]
                q = cand[0]
                slot_free[q] = False
                slot_of[j] = q
                for (k0, k1, f0, f1, src) in chunks[j]:
                    S.dma("gpsimd", ring[q][:, k0:k1, f0:f1], src.rearrange("(kc p) f -> p kc f", p=128),
                          key=R_ring[q], writes=[R_ring[q]])
                state["issued"] += 1
            assert state["issued"] > i1, (i, i1, state["issued"])
            return slot_of[i]

        def std(w, f0, nf=512):
            return [(0, 8, 0, nf, w[:, f0:f0 + nf])]

        def sb_chunks():
            cl = []
            for c in (3, 4, 5, 0, 1, 2):
                cl.append(std(w_in, c * 512))
            for mg in range(2):
                cl.append(std(w_gate, mg * 512))
                cl.append(std(w_gate, 1024 + mg * 512))
                cl.append([(0, 4, 0, 512, w_brc[:, mg * 512:(mg + 1) * 512]),
                           (4, 8, 0, 512, w_bra[:, mg * 512:(mg + 1) * 512])])
            cl.append(std(w_o, 0))
            cl.append(std(w_o, 512))
            for g in range(6):
                nf = 512 if g < 5 else 256
                cl.append(std(w_ff1, g * 512, nf))
                cl.append(std(w_ff3, g * 512, nf))
            for nh in range(2):
                for (r0, r1) in ((0, 1024), (1024, 2048), (2048, 2816)):
                    cl.append([(0, (r1 - r0) // 128, 0, 512, w_ff2[r0:r1, nh * 512:(nh + 1) * 512])])
            return cl

        for j in range(12):
            chunks.append(std(w_mod, j * 512))
        NMOD = 12
        per_sb = sb_chunks()
        NPER = len(per_sb)
        for _ in range(3):
            chunks.extend(per_sb)

        S.op("gpsimd", lambda e: e.memset(identf[:], 0.0), writes=[R_const])
        S.op("gpsimd", lambda e: e.affine_select(out=identf[:], in_=identf[:], compare_op=ALU.not_equal, fill=1.0,
                                                 base=0, pattern=[[-1, 128]], channel_multiplier=1),
             reads=[R_const], writes=[R_const])
        S.op("vector", lambda e: e.tensor_copy(ident[:], identf[:]), reads=[R_const], writes=[R_const])
        S.op("vector", lambda e: e.memset(onesb[:], 1.0), reads=[R_const], writes=[R_const])
        S.dma("sync", cT[:], c2col[:, :, :], key=R_mod, writes=[R_mod])
        S.dma("sync", bmc[:], bmodcol[:, :], key=R_mod, writes=[R_mod])
        S.dma("sync", gcol[:], gcol_d[:, :, :], key=R_mod, writes=[R_mod])
        S.dma("sync", convw[:], convw_d[:, :, :], key=R_mod, writes=[R_mod])
        S.op("scalar", lambda e: e.activation(out=scT[:], in_=cT[:], func=AF.Silu), reads=[R_mod], writes=[R_mod])
        def setup_mod():
            mp = nb()
            modps = ps[mp][:, 0:96].rearrange("p (j g) -> p j g", g=2)
            for j in range(NMOD):
                s = ensure(j)
                for m4 in range(4):
                    for kc in range(8):
                        S.op("tensor", lambda e, s=s, m4=m4, kc=kc, j=j: e.matmul(
                            modps[:, j * 4 + m4, :], lhsT=ring[s][:, kc, m4 * 128:(m4 + 1) * 128], rhs=scT[:, kc, :],
                            start=(kc == 0), stop=(kc == 7)),
                            reads=[R_ring[s], R_mod], writes=[R_ps[mp]], partial=True)
            for g in range(2):
                S.op("vector", lambda e, g=g: e.tensor_tensor(out=modcol[:, :, g], in0=modps[:, :, g], in1=bmc[:, :], op=ALU.add),
                     reads=[R_ps[mp], R_mod], writes=[R_mod], partial=True)

        def group_pcol(g):
            def mc(i):
                return modcol[:, i * 8:(i + 1) * 8, g]
            V = "vector"
            o = g * 6
            S.op(V, lambda e: e.scalar_tensor_tensor(out=pcol[:, o + 0, :], in0=mc(1), scalar=1.0, in1=gcol[:, 0, :], op0=ALU.add, op1=ALU.mult),
                 reads=[R_mod], writes=[R_pcol], partial=True)
            S.op(V, lambda e: e.tensor_copy(pcol[:, o + 1, :], mc(0)), reads=[R_mod], writes=[R_pcol], partial=True)
            S.op(V, lambda e: e.scalar_tensor_tensor(out=pcol[:, o + 2, :], in0=mc(4), scalar=1.0, in1=gcol[:, 2, :], op0=ALU.add, op1=ALU.mult),
                 reads=[R_mod], writes=[R_pcol], partial=True)
            S.op(V, lambda e: e.tensor_copy(pcol[:, o + 3, :], mc(3)), reads=[R_mod], writes=[R_pcol], partial=True)
            S.op(V, lambda e: e.tensor_tensor(out=pcol[:, o + 4, :], in0=mc(2), in1=gcol[:, 1, :], op=ALU.mult),
                 reads=[R_mod], writes=[R_pcol], partial=True)
            S.op(V, lambda e: e.tensor_tensor(out=pcol[:, o + 5, :], in0=mc(5), in1=gcol[:, 3, :], op=ALU.mult),
                 reads=[R_mod], writes=[R_pcol], partial=True)

        def build_GG(g):
            V = "vector"
            first = True
            for gi in range(2):
                for half in range(2):
                    b = nb()
                    for k4 in range(4):
                        kc = half * 4 + k4
                        S.op(V, lambda e, kc=kc, gi=gi: e.tensor_scalar(diag[:], identf[:], pcol[:, g * 6 + 4 + gi, kc:kc + 1], None, ALU.mult),
                             reads=[R_pcol, R_const], writes=[R_diag])
                        S.op(V, lambda e: e.tensor_copy(dhi[:], diag[:]), reads=[R_diag], writes=[R_dhi])
                        S.op(V, lambda e: e.tensor_tensor(out=diag[:], in0=diag[:], in1=dhi[:], op=ALU.subtract), reads=[R_diag, R_dhi], writes=[R_diag])
                        S.op(V, lambda e: e.tensor_copy(dlo[:], diag[:]), reads=[R_diag], writes=[R_dlo])
                        S.op("tensor", lambda e, b=b, k4=k4: e.matmul(ps[b][:, k4 * 128:(k4 + 1) * 128], lhsT=onesb[:], rhs=dhi[:],
                                                                       start=True, stop=False),
                             reads=[R_dhi, R_const], writes=[R_ps[b]], partial=True)
                        S.op("tensor", lambda e, b=b, k4=k4: e.matmul(ps[b][:, k4 * 128:(k4 + 1) * 128], lhsT=onesb[:], rhs=dlo[:],
                                                                       start=False, stop=True),
                             reads=[R_dlo, R_const], writes=[R_ps[b]], partial=True)
                    S.op("vector", lambda e, b=b, gi=gi, half=half: e.tensor_copy(GG[gi][:, half * 512:(half + 1) * 512], ps[b][:, :]),
                         reads=[R_ps[b]], writes=[R_GG], partial=not first)
                    first = False
                    yield

        nt_par = {"v": 0}

        def norm_transpose(src_ap, R_src, dstT, R_dst, col0, pa, ps_, defer=False):
            nt_par["v"] ^= 1
            p = nt_par["v"]
            junk, R_junk, stat, R_stat = Pb2[p][:, 0:1024], R_Pb2[p], stat2[2 + p], R_stat2[2 + p]
            xsb, R_xsb = PTs2[p][:, 0:8, :].rearrange("p k t -> p (k t)"), R_PTs2[p]
            S.op("scalar", lambda e: e.activation(out=junk[:], in_=src_ap, func=AF.Square, accum_out=stat[:, 0:1]),
                 reads=[R_src], writes=[R_junk, R_stat])
            S.op("scalar", lambda e: e.activation(out=stat[:, 1:2], in_=stat[:, 0:1], func=AF.Ln, scale=1.0 / D, bias=EPS),
                 reads=[R_stat], writes=[R_stat])
            S.op("scalar", lambda e: e.activation(out=stat[:, 2:3], in_=stat[:, 1:2], func=AF.Exp, scale=-0.5),
                 reads=[R_stat], writes=[R_stat])
            if NTP < 2:
                return
            S.op("vector", lambda e: e.tensor_scalar(xsb[:], src_ap, stat[:, 2:3], None, ALU.mult),
                 reads=[R_src, R_stat], writes=[R_xsb])
            if NTP < 3:
                return

            def stage2():
                nt_stage2(dstT, R_dst, col0, pa, ps_, xsb, R_xsb)
            if defer:
                prev = nt_pending["f"]
                nt_pending["f"] = stage2
                if prev is not None:
                    prev()
            else:
                stage2()

        nt_pending = {"f": None}

        def nt_flush():
            if nt_pending["f"] is not None:
                nt_pending["f"]()
                nt_pending["f"] = None

        def nt_stage2(dstT, R_dst, col0, pa, ps_, xsb, R_xsb):
            t = nxt("pst")
            for kc in range(8):
                S.op("tensor", lambda e, kc=kc, t=t: e.transpose(pst[t][:, kc, :], xsb[:, kc * 128:(kc + 1) * 128], ident[:]),
                     reads=[R_xsb, R_const], writes=[R_pst[t]], partial=True)
            if NTP < 4:
                return
            use_act = (col0 // 128) % 2 == 0
            for kc in range(8):
                if use_act:
                    S.op("scalar", lambda e, kc=kc, t=t: e.activation(out=dstT[:, kc, col0:col0 + 128], in_=pst[t][:, kc, :], func=AF.Identity,
                                                                       scale=pcol[:, pa, kc:kc + 1], bias=pcol[:, ps_, kc:kc + 1]),
                         reads=[R_pst[t], R_pcol], writes=[R_dst], partial=True)
                else:
                    S.op("vector", lambda e, kc=kc, t=t: e.tensor_scalar(dstT[:, kc, col0:col0 + 128], pst[t][:, kc, :],
                                                                          pcol[:, pa, kc:kc + 1], pcol[:, ps_, kc:kc + 1], ALU.mult, ALU.add),
                         reads=[R_pst[t], R_pcol], writes=[R_dst], partial=True)

        def projA(slot, m4, src, R_src_list, blocks, nk_, kmap=None):
            banks = [nb() for _ in blocks]
            for kc in range(nk_):
                kk = kc if kmap is None else kmap(kc)
                for bi, (c0, n) in enumerate(blocks):
                    b = banks[bi]
                    S.op("tensor", lambda e, b=b, kk=kk, kc=kc, c0=c0, n=n: e.matmul(
                        ps[b][:, 0:n], lhsT=ring[slot][:, kk, m4 * 128:(m4 + 1) * 128], rhs=src[:, kc, c0:c0 + n],
                        start=(kc == 0), stop=(kc == nk_ - 1)),
                        reads=[R_ring[slot]] + R_src_list, writes=[R_ps[b]], partial=True)
            return banks

        def run_sb(sbi, base):
            prompt = (sbi == 0)
            goff = 0 if prompt else 6
            NTE = 8 if prompt else 10
            c0t = 0 if sbi < 2 else 2
            ext_row0 = 0 if sbi < 2 else 12
            core_row0 = 0 if sbi == 1 else 16
            xsrc = xp if prompt else xs
            xrow0 = 0 if sbi < 2 else 768
            ydst = yp if prompt else ys
            yrow0 = 0 if prompt else (0 if sbi == 1 else 1024)
            NEXT = NTE * 128
            ccol = c0t * 128

            def ucol(t):
                return (t + 1 + 2 * (t // 256)) if prompt else (t + ccol + 1)

            def chunk(i, n=1, **kw):
                ensure(base + i, base + i + n - 1, **kw)
                return [slot_of[base + i + j] for j in range(n)] if n > 1 else slot_of[base + i]

            if not prompt:
                for j in list(slot_of):
                    if j < base:
                        slot_free[slot_of.pop(j)] = True
                assert slot_free[NSLOT] and not slot_enabled[NSLOT]
                load_ctx()
            S.fence([R_h2T[i] for i in range(8)], R_hT)
            S.fence(R_tmpA2 + R_tmpB2, R_Sb2)
            S.fence([R_mg], [R_b, R_u])
            S.fence([R_aT], [R_q, R_k, R_v, R_yatt, R_yconv])

            for i in range(NTE):
                xi = nxt("xin")
                S.dma("sync", xin[xi][:], xsrc[xrow0 + i * 128: xrow0 + (i + 1) * 128, :], key=R_xin[xi], writes=[R_xin[xi]])
                norm_transpose(xin[xi][:], R_xin[xi], hT, R_hT[i], i * 128, goff + 0, goff + 1, defer=True)
            nt_flush()
            if sbi == 0:
                build_btimg()

            _stage(sbi * 10 + 1)
            extblocks = [(0, 512), (512, 512)] + ([(1024, 256)] if not prompt else [])
            coreblocks = [(ccol, 512), (ccol + 512, 512)]
            allhT = R_hT[:NTE]

            s = chunk(0)
            for m4 in range(4):
                banks = projA(s, m4, hT, allhT, coreblocks, 8)
                for bi, b in enumerate(banks):
                    S.op("scalar", lambda e, b=b, bi=bi, m4=m4: e.activation(out=qT[:, m4, bi * 512:(bi + 1) * 512], in_=ps[b][:, :],
                                                                             func=AF.Copy, scale=0.125),
                         reads=[R_ps[b]], writes=[R_q], partial=True)
            s = chunk(1)
            for m4 in range(4):
                banks = projA(s, m4, hT, allhT, extblocks, 8)
                for bi, b in enumerate(banks):
                    c0_, n = extblocks[bi]
                    S.op("vector", lambda e, b=b, c0_=c0_, n=n, m4=m4: e.tensor_copy(kT[:, m4, c0_:c0_ + n], ps[b][:, 0:n]),
                         reads=[R_ps[b]], writes=[R_k], partial=True)
            if prompt:
                for i in range(8):
                    b = nb()
                    for kc in range(8):
                        S.op("tensor", lambda e, b=b, kc=kc, i=i, s=s: e.matmul(ps[b][:, :], lhsT=hT[:, kc, i * 128:(i + 1) * 128], rhs=ring[s][:, kc, :],
                                                                          start=(kc == 0), stop=(kc == 7)),
                             reads=[R_ring[s], R_hT[i]], writes=[R_ps[b]], partial=True)
                    oi = nxt("ost")
                    S.op("scalar", lambda e, b=b, oi=oi: e.copy(ost[oi][:, 0:512], ps[b][:, :]), reads=[R_ps[b]], writes=[R_ost[oi]])
                    S.dma("sync", nk[i // 2].rearrange("h s d -> s h d")[(i % 2) * 128:(i % 2 + 1) * 128],
                          ost[oi][:, 0:512].rearrange("p (h d) -> p h d", h=8), key=R_ost[oi], reads=[R_ost[oi]])
            s = chunk(2)
            for i in range(NTE):
                b = nb()
                for kc in range(8):
                    S.op("tensor", lambda e, b=b, kc=kc, i=i, s=s: e.matmul(ps[b][:, :], lhsT=hT[:, kc, i * 128:(i + 1) * 128], rhs=ring[s][:, kc, :],
                                                                      start=(kc == 0), stop=(kc == 7)),
                         reads=[R_ring[s], R_hT[i]], writes=[R_ps[b]], partial=True)
                if prompt:
                    oi = nxt("ost")
                    S.op("scalar", lambda e, b=b, oi=oi: e.copy(ost[oi][:, 0:512], ps[b][:, :]), reads=[R_ps[b]], writes=[R_ost[oi]])
                    S.op("vector", lambda e, oi=oi, i=i: e.tensor_copy(vv[:, i, :], ost[oi][:, 0:512]), reads=[R_ost[oi]], writes=[R_v], partial=True)
                else:
                    S.op("vector", lambda e, b=b, i=i: e.tensor_copy(vv[:, i, :], ps[b][:, :]), reads=[R_ps[b]], writes=[R_v], partial=True)
                if prompt:
                    S.dma("sync", nv[i // 2].rearrange("h s d -> s h d")[(i % 2) * 128:(i % 2 + 1) * 128],
                          ost[oi][:, 0:512].rearrange("p (h d) -> p h d", h=8), key=R_ost[oi], reads=[R_ost[oi]])

            _stage(sbi * 10 + 2)
            if prompt:
                for qi in range(8):
                    sq = qi // 2
                    for hh in range(8):
                        hp, ho = hh // 2, (hh % 2) * 64
                        segs = [(kT[ho:ho + 64, hp, sq * 256:(sq + 1) * 256], 256, None)]
                        pv = [(j * 128, 128, vv[:, sq * 2 + j, hp * 128:(hp + 1) * 128]) for j in range(2)]
                        attention_unit(qi, hh, segs, pv, [R_k, R_v])
            else:
                S.fence(R_x1, [R_BT])
                S.dma("sync", FA[0:64, 0:7680], btimg[:, :], key=R_BT, reads=[R_btimg], writes=[R_BT])
                S.dma("sync", FA[64:128, 64:64 + 7680], btimg[:, :], key=R_BT, reads=[R_btimg], writes=[R_BT])
                for qi in range(8):
                    Rr = core_row0 + 2 * qi
                    r0t = min(max(Rr - 4, 0), 24)
                    r0b = min(max(Rr + 1 - 4, 0), 24)
                    rot = r0t - Rr + 7
                    nband = 576 if r0b != r0t else 512
                    kc0 = (r0t - ext_row0) * 64
                    vt0 = (r0t - ext_row0) // 2
                    for hh in range(8):
                        hp, ho = hh // 2, (hh % 2) * 64
                        segs = [(kT[ho:ho + 64, hp, kc0:kc0 + nband], nband, rot),
                                (kctxT[ho:ho + 64, hp, :], 512, None)]
                        pv = []
                        for j in range(4):
                            pv.append((j * 128, 128, vv[:, vt0 + j, hp * 128:(hp + 1) * 128]))
                        if nband == 576:
                            pv.append((512, 64, vv[0:64, vt0 + 4, hp * 128:(hp + 1) * 128]))
                        for j in range(4):
                            pv.append((nband + j * 128, 128, vctx[:, j, hp * 128:(hp + 1) * 128]))
                        if nband == 512:
                            pv = pv[:4] + [None] + pv[4:]
                        attention_unit_s(qi, hh, segs, pv)

            _stage(sbi * 10 + 3)
            def side_gen():
                if sbi == 1:
                    yield from build_GG(1)
                sb_ = chunk(3)
                for m4 in range(4):
                    banks = projA(sb_, m4, hT, allhT, coreblocks, 8)
                    for bi, b in enumerate(banks):
                        S.op("scalar", lambda e, b=b, bi=bi, m4=m4: e.copy(bT[:, m4, bi * 512:(bi + 1) * 512], ps[b][:, :]),
                             reads=[R_ps[b]], writes=[R_b], partial=True)
                    yield
            att_flush(side_gen())
            slot_enabled[NSLOT] = True
            S.fence(R_Sb2, R_tmpA2 + R_tmpB2)
            S.op("gpsimd", lambda e: e.memset(uT[:, :, :], 0.0), writes=[R_u])
            s1, s2 = chunk(4, 2)
            ublocks = extblocks
            for m4 in range(4):
                bk1 = projA(s1, m4, hT, allhT, ublocks, 8)
                bk2 = projA(s2, m4, hT, allhT, ublocks, 8)
                for bi, (c0_, n) in enumerate(ublocks):
                    pt_ = flip(); tmpA, tmpB, R_tmpA, R_tmpB = tmpA2[pt_], tmpB2[pt_], R_tmpA2[pt_], R_tmpB2[pt_]
                    S.op("scalar", lambda e, tmpA=tmpA, b=bk1[bi], n=n: e.copy(tmpA[:, 0:n], ps[b][:, 0:n]), reads=[R_ps[bk1[bi]]], writes=[R_tmpA])
                    if prompt:
                        for hs in range(2):
                            t0 = c0_ + hs * 256
                            S.op("vector", lambda e, tmpA=tmpA, b=bk2[bi], hs=hs, t0=t0, m4=m4: e.tensor_tensor(
                                out=uT[:, m4, ucol(t0):ucol(t0) + 256], in0=tmpA[:, hs * 256:(hs + 1) * 256],
                                in1=ps[b][:, hs * 256:(hs + 1) * 256], op=ALU.mult),
                                reads=[R_tmpA, R_ps[bk2[bi]]], writes=[R_u], partial=True)
                    else:
                        S.op("vector", lambda e, tmpA=tmpA, b=bk2[bi], c0_=c0_, n=n, m4=m4: e.tensor_tensor(
                            out=uT[:, m4, c0_ + 1:c0_ + 1 + n], in0=tmpA[:, 0:n], in1=ps[b][:, 0:n], op=ALU.mult),
                            reads=[R_tmpA, R_ps[bk2[bi]]], writes=[R_u], partial=True)

            ranges = [(sq * 256, 256) for sq in range(4)] if prompt else [(0, 512), (512, 512)]
            for m4 in range(4):
                for (t0, n) in ranges:
                    pt_ = flip(); tmpA, tmpB, R_tmpA, R_tmpB = tmpA2[pt_], tmpB2[pt_], R_tmpA2[pt_], R_tmpB2[pt_]
                    u0 = ucol(t0)
                    S.op("vector", lambda e, tmpB=tmpB, m4=m4, u0=u0, n=n: e.tensor_scalar(tmpB[:, 0:n], uT[:, m4, u0:u0 + n], convw[:, m4, 1:2], None, ALU.mult),
                         reads=[R_u], writes=[R_tmpB])
                    S.op("vector", lambda e, tmpB=tmpB, m4=m4, u0=u0, n=n: e.scalar_tensor_tensor(
                        out=tmpB[:, 0:n], in0=uT[:, m4, u0 - 1:u0 - 1 + n], scalar=convw[:, m4, 0:1], in1=tmpB[:, 0:n], op0=ALU.mult, op1=ALU.add),
                        reads=[R_u, R_tmpB], writes=[R_tmpB])
                    S.op("vector", lambda e, tmpB=tmpB, m4=m4, u0=u0, n=n: e.scalar_tensor_tensor(
                        out=tmpB[:, 0:n], in0=uT[:, m4, u0 + 1:u0 + 1 + n], scalar=convw[:, m4, 2:3], in1=tmpB[:, 0:n], op0=ALU.mult, op1=ALU.add),
                        reads=[R_u, R_tmpB], writes=[R_tmpB])
                    S.op("vector", lambda e, tmpB=tmpB, m4=m4, t0=t0, n=n: e.tensor_tensor(out=yconvT[:, m4, t0:t0 + n], in0=tmpB[:, 0:n],
                                                                               in1=bT[:, m4, t0:t0 + n], op=ALU.mult),
                         reads=[R_tmpB, R_b], writes=[R_yconv], partial=True)

            _stage(sbi * 10 + 4)
            S.fence([R_b, R_u], [R_mg])
            cb = [(0, 512), (512, 512)]
            for mg in range(2):
                sgc, sga, sbr = chunk(6 + mg * 3, 3)
                for m4 in range(4):
                    m = mg * 4 + m4
                    for (t0, n) in cb:
                        pt_ = flip(); tmpA, tmpB, R_tmpA, R_tmpB = tmpA2[pt_], tmpB2[pt_], R_tmpA2[pt_], R_tmpB2[pt_]
                        (bgc,) = projA(sgc, m4, hT, allhT, [(ccol + t0, n)], 8)
                        (bbc,) = projA(sbr, m4, yconvT, [R_yconv], [(t0, n)], 4)
                        (bga,) = projA(sga, m4, hT, allhT, [(ccol + t0, n)], 8)
                        (bba,) = projA(sbr, m4, yattT, [R_yatt], [(t0, n)], 4, kmap=lambda kc: kc + 4)
                        S.op("scalar", lambda e, tmpA=tmpA, b=bgc: e.activation(out=tmpA[:, :], in_=ps[b][:, :], func=AF.Sigmoid),
                             reads=[R_ps[bgc]], writes=[R_tmpA])
                        S.op("vector", lambda e, tmpA=tmpA, b=bbc: e.tensor_tensor(out=tmpA[:, :], in0=tmpA[:, :], in1=ps[b][:, :], op=ALU.mult),
                             reads=[R_tmpA, R_ps[bbc]], writes=[R_tmpA])
                        S.op("scalar", lambda e, tmpB=tmpB, b=bga: e.activation(out=tmpB[:, :], in_=ps[b][:, :], func=AF.Sigmoid),
                             reads=[R_ps[bga]], writes=[R_tmpB])
                        S.op("vector", lambda e, tmpB=tmpB, b=bba: e.tensor_tensor(out=tmpB[:, :], in0=tmpB[:, :], in1=ps[b][:, :], op=ALU.mult),
                             reads=[R_tmpB, R_ps[bba]], writes=[R_tmpB])
                        S.op("gpsimd", lambda e, tmpA=tmpA, tmpB=tmpB, m=m, t0=t0, n=n: e.tensor_tensor(out=mgT[:, m, t0:t0 + n], in0=tmpA[:, 0:n], in1=tmpB[:, 0:n], op=ALU.add),
                             reads=[R_tmpA, R_tmpB], writes=[R_mg], partial=True)

            _stage(sbi * 10 + 5)
            S.fence([R_BT], R_x1)
            S.fence(R_hT, R_h2T)
            so = chunk(12, 2)
            def wo_mm(t):
                bk = [nb(), nb()]
                for nh in range(2):
                    for kc in range(8):
                        S.op("tensor", lambda e, nh=nh, kc=kc, t=t, bk=bk: e.matmul(ps[bk[nh]][:, :], lhsT=mgT[:, kc, t * 128:(t + 1) * 128],
                                                                            rhs=ring[so[nh]][:, kc, :], start=(kc == 0), stop=(kc == 7)),
                             reads=[R_mg, R_ring[so[nh]]], writes=[R_ps[bk[nh]]], partial=True)
                return bk
            bks = {0: wo_mm(0)}
            for t in range(8):
                if t + 1 < 8:
                    bks[t + 1] = wo_mm(t + 1)
                bk = bks[t]
                xi = nxt("xin")
                S.dma("sync", xin[xi][:], xsrc[xrow0 + (c0t + t) * 128: xrow0 + (c0t + t + 1) * 128, :], key=R_xin[xi], writes=[R_xin[xi]])
                post_norm_residual(bk, None, xin[xi], R_xin[xi], GG[0], x1buf[:, t, :], R_x1[t])
                if t >= 2:
                    norm_transpose(x1buf[:, t - 2, :], R_x1[t - 2], h2T, R_h2T[t - 2], (t - 2) * 128, goff + 2, goff + 3, defer=True)
            for t in (6, 7):
                norm_transpose(x1buf[:, t, :], R_x1[t], h2T, R_h2T[t], t * 128, goff + 2, goff + 3, defer=True)
            nt_flush()

            _stage(sbi * 10 + 6)
            S.fence([R_q, R_k, R_v, R_yatt, R_yconv], [R_aT])
            allh2 = R_h2T
            for g in range(6):
                s1, s3 = chunk(14 + 2 * g, 2)
                for m4 in range(4 if g < 5 else 2):
                    m = g * 4 + m4
                    for (t0, n) in cb:
                        pt_ = flip(); tmpA, tmpB, R_tmpA, R_tmpB = tmpA2[pt_], tmpB2[pt_], R_tmpA2[pt_], R_tmpB2[pt_]
                        (b1,) = projA(s1, m4, h2T, allh2, [(t0, n)], 8)
                        (b3,) = projA(s3, m4, h2T, allh2, [(t0, n)], 8)
                        S.op("scalar", lambda e, tmpA=tmpA, b=b1: e.activation(out=tmpA[:, :], in_=ps[b][:, :], func=AF.Silu),
                             reads=[R_ps[b1]], writes=[R_tmpA])
                        S.op("vector", lambda e, tmpA=tmpA, b=b3, m=m, t0=t0, n=n: e.tensor_tensor(out=aT[:, m, t0:t0 + n], in0=tmpA[:, 0:n], in1=ps[b][:, 0:n], op=ALU.mult),
                             reads=[R_tmpA, R_ps[b3]], writes=[R_aT], partial=True)

            _stage(sbi * 10 + 7)
            S.fence(R_tmpA2 + R_tmpB2, R_Sb2)
            for nh in range(2):
                sl = chunk(26 + nh * 3, 3, disable_ctx_slot=(nh == 1 and sbi < 2))
                for t in range(8):
                    b = nb()
                    for kc in range(22):
                        S.op("tensor", lambda e, b=b, kc=kc, t=t, sl=sl: e.matmul(ps[b][:, :], lhsT=aT[:, kc, t * 128:(t + 1) * 128],
                                                                          rhs=ring[sl[kc // 8]][:, kc % 8, :], start=(kc == 0), stop=(kc == 21)),
                             reads=[R_aT, R_ring[sl[kc // 8]]], writes=[R_ps[b]], partial=True)
                    if nh == 0:
                        S.op("vector", lambda e, b=b, t=t: e.tensor_copy(y0ap(t), ps[b][:, :]), reads=[R_ps[b]], writes=[R_y0[t]], partial=True)
                        pj = flip()
                        S.op("scalar", lambda e, t=t, pj=pj: e.activation(out=Pb2[pj][:, 0:512], in_=y0ap(t), func=AF.Square, accum_out=ss0[:, t:t + 1]),
                             reads=[R_y0[t]], writes=[R_Pb2[pj]], pwrites=[R_ss0])
                    else:
                        oi = nxt("ost")
                        post_norm_residual([None, b], (y0ap(t), R_y0[t], ss0[:, t:t + 1]), None, R_x1[t], GG[1], ost[oi][:], R_ost[oi],
                                           xres=x1buf[:, t, :])
                        S.dma("sync", ydst[yrow0 + t * 128: yrow0 + (t + 1) * 128, :], ost[oi][:], key=R_ost[oi], reads=[R_ost[oi]])

        def post_norm_residual(bk, half0, xres_t, R_xres, GGt, dst, R_dst, xres=None):
            p = flip()
            junk, R_junk, stat, R_stat = Pb2[p][:, 0:1024], R_Pb2[p], stat2[p], R_stat2[p]
            if xres is None:
                xres = xres_t[:]
            if bk[0] is not None:
                S.op("scalar", lambda e: e.activation(out=junk[:, 0:512], in_=ps[bk[0]][:, :], func=AF.Square, accum_out=stat[:, 8:9]),
                     reads=[R_ps[bk[0]]], writes=[R_junk, R_stat])
                s0 = stat[:, 8:9]
                rs0 = [R_stat]
            else:
                s0 = half0[2]
                rs0 = [R_ss0]
            S.op("scalar", lambda e: e.activation(out=junk[:, 512:1024], in_=ps[bk[1]][:, :], func=AF.Square, accum_out=stat[:, 9:10]),
                 reads=[R_ps[bk[1]]], writes=[R_junk, R_stat])
            S.op("vector", lambda e: e.tensor_tensor(out=stat[:, 10:11], in0=s0, in1=stat[:, 9:10], op=ALU.add), reads=[R_stat] + rs0, writes=[R_stat])
            S.op("scalar", lambda e: e.activation(out=stat[:, 11:12], in_=stat[:, 10:11], func=AF.Ln, scale=1.0 / D, bias=EPS), reads=[R_stat], writes=[R_stat])
            S.op("scalar", lambda e: e.activation(out=stat[:, 12:13], in_=stat[:, 11:12], func=AF.Exp, scale=-0.5), reads=[R_stat], writes=[R_stat])
            for nh in range(2):
                if bk[nh] is not None:
                    src, rsrc = ps[bk[nh]][:, :], [R_ps[bk[nh]]]
                else:
                    src, rsrc = half0[0], [half0[1]]
                lo, hi = nh * 512, (nh + 1) * 512
                S.op("vector", lambda e, src=src, lo=lo, hi=hi: e.scalar_tensor_tensor(out=dst[:, lo:hi], in0=src, scalar=stat[:, 12:13], in1=GGt[:, lo:hi],
                                                                                     op0=ALU.mult, op1=ALU.mult),
                     reads=rsrc + [R_stat, R_GG], writes=[R_dst], partial=(nh == 1))
            S.op("gpsimd", lambda e: e.tensor_tensor(out=dst, in0=dst, in1=xres, op=ALU.add), reads=[R_dst, R_xres], writes=[R_dst])

        def attention_unit_s(qi, hh, segs, pv):
            pv2 = []
            for i, item in enumerate(pv):
                if item is None:
                    continue
                pv2.append((i,) + item)
            attention_core(qi, hh, segs, pv2, [R_k, R_v, R_ctx])

        att_units = []
        att_ctr = {"n": 0}

        def attention_core(qi, hh, segs, pv, R_kv):
            u = att_ctr["n"]
            att_ctr["n"] += 1
            hp, ho = hh // 2, (hh % 2) * 64
            Sb, R_Sb = Sb2[u % 2], R_Sb2[u % 2]
            Pb, R_Pb = Pb2[u % 3], R_Pb2[u % 3]
            PTs, R_PTs = PTs2[u % 2], R_PTs2[u % 2]
            stat, R_stat = stat2[u % 5], R_stat2[u % 5]
            qap = qT[ho:ho + 64, hp, qi * 128:(qi + 1) * 128]
            ncol = sum(n for (_, n, _) in segs)

            def stA():
                col = 0
                first = True
                for (rhs, n, bias) in segs:
                    off = 0
                    while off < n:
                        w = min(512, n - off)
                        b = nb()
                        S.op("tensor", lambda e, b=b, rhs=rhs, off=off, w=w: e.matmul(ps[b][:, 0:w], lhsT=qap, rhs=rhs[:, off:off + w], start=True, stop=True),
                             reads=[R_q] + R_kv, writes=[R_ps[b]])
                        c = col + off
                        if bias is None:
                            S.op("scalar", lambda e, b=b, c=c, w=w: e.copy(Sb[:, c:c + w], ps[b][:, 0:w]), reads=[R_ps[b]], writes=[R_Sb], partial=not first)
                            first = False
                        else:
                            blk0 = bias + off // 64

                            def badd(p0, p1, c0, c1, fp, b=b, c=c, blk0=blk0):
                                S.op("vector", lambda e: e.tensor_tensor(
                                    out=Sb[p0:p1, c + c0:c + c1], in0=ps[b][p0:p1, c0:c1],
                                    in1=BT[p0:p1, hh, blk0 + c0 // 64:blk0 + c1 // 64, :].rearrange("p r w -> p (r w)"), op=ALU.add),
                                    reads=[R_ps[b], R_BT], writes=[R_Sb], partial=not fp)
                            if n == 576 and off == 0:
                                badd(0, 128, 64, 512, first)
                                first = False
                                badd(0, 64, 0, 64, False)
                                S.op("vector", lambda e, c=c: e.memset(Sb[64:128, c:c + 64], NEG), writes=[R_Sb], partial=True)
                            elif n == 576:
                                badd(64, 128, 0, 64, first)
                                first = False
                                S.op("vector", lambda e, c=c: e.memset(Sb[0:64, c:c + 64], NEG), writes=[R_Sb], partial=True)
                            else:
                                badd(0, 128, 0, w, first)
                                first = False
                        off += w
                    col += n

            def stB():
                S.op("vector", lambda e: e.reduce_max(stat[:, 4:5], Sb[:, 0:ncol], axis=AX.X), reads=[R_Sb], writes=[R_stat])
                S.op("vector", lambda e: e.tensor_scalar(stat[:, 6:7], stat[:, 4:5], -1.0, None, ALU.mult), reads=[R_stat], writes=[R_stat])
                S.op("scalar", lambda e: e.activation(out=Pb[:, 0:ncol], in_=Sb[:, 0:ncol], func=AF.Exp, bias=stat[:, 6:7], scale=1.0, accum_out=stat[:, 5:6]),
                     reads=[R_Sb, R_stat], writes=[R_Pb, R_stat])

            def stC1():
                S.op("vector", lambda e: e.reciprocal(stat[:, 7:8], stat[:, 5:6]), reads=[R_stat], writes=[R_stat])
                if ncol > 600:
                    nsplit = ncol - 512
                    S.op("vector", lambda e: e.tensor_scalar(Pb[:, 0:nsplit], Pb[:, 0:nsplit], stat[:, 7:8], None, ALU.mult),
                         reads=[R_stat], pwrites=[R_Pb])
                    S.op("scalar", lambda e: e.activation(out=Pb[:, nsplit:ncol], in_=Pb[:, nsplit:ncol], func=AF.Copy, scale=stat[:, 7:8]),
                         reads=[R_stat], pwrites=[R_Pb])
                else:
                    S.op("scalar", lambda e: e.activation(out=Pb[:, 0:ncol], in_=Pb[:, 0:ncol], func=AF.Copy, scale=stat[:, 7:8]), reads=[R_Pb, R_stat], writes=[R_Pb])

            used0 = [p for p in pv if p[0] < 5]
            used1 = [p for p in pv if p[0] >= 5]

            def stC2():
                for (i, c0, K, _) in pv:
                    t = 0 if i < 5 else 1
                    sl = i if i < 5 else i - 5
                    S.op("tensor", lambda e, t=t, sl=sl, c0=c0, K=K: e.transpose(pst[t][0:K, sl, :], Pb[:, c0:c0 + K], ident[:]),
                         reads=[R_Pb, R_const], writes=[R_pst[t]], partial=True)
                n0 = max(p[0] for p in used0) + 1
                S.op("vector", lambda e: e.tensor_copy(PTs[:, 0:n0, :], pst[0][:, 0:n0, :]), reads=[R_pst[0]], writes=[R_PTs])
                if used1:
                    n1 = max(p[0] for p in used1) - 4
                    S.op("scalar", lambda e: e.copy(PTs[:, 5:5 + n1, :], pst[1][:, 0:n1, :]), reads=[R_pst[1]], writes=[R_PTs], partial=True)

            def stD():
                ob = nb()
                for j, (i, c0, K, vap) in enumerate(pv):
                    S.op("tensor", lambda e, j=j, i=i, K=K, vap=vap: e.matmul(ps[ob][:, 0:128], lhsT=vap, rhs=PTs[0:K, i, :],
                                                                             start=(j == 0), stop=(j == len(pv) - 1)),
                         reads=[R_PTs] + R_kv, writes=[R_ps[ob]], partial=True)
                S.op("scalar", lambda e: e.copy(yattT[ho:ho + 64, hp, qi * 128:(qi + 1) * 128], ps[ob][ho:ho + 64, 0:128]),
                     reads=[R_ps[ob]], writes=[R_yatt], partial=True)

            att_units.append((stA, stB, stC1, stC2, stD))

        def att_flush(side=None):
            n = len(att_units)
            for j in range(n + 4):
                for st in range(5):
                    u = j - st
                    if 0 <= u < n:
                        att_units[u][st]()
                if side is not None and j % 8 == 7:
                    next(side, None)
            if side is not None:
                for _ in side:
                    pass
            att_units.clear()

        def attention_unit(qi, hh, segs, pv, R_kv):
            attention_core(qi, hh, segs, [(i,) + p for i, p in enumerate(pv)], R_kv)

        def load_ctx():
            ctmp = ost[0][:, :].bitcast(BF16).rearrange("p (c f) -> p c f", c=4)
            for c in range(4):
                S.dma("gpsimd", ctmp[:, c, :].rearrange("p (h d) -> p h d", h=8), ck[:, c * 128:(c + 1) * 128, :].rearrange("h p d -> p h d"),
                      key=R_ctmpk, writes=[R_ost[0]])
                S.dma("gpsimd", vctx[:, c, :].rearrange("p (h d) -> p h d", h=8), cv[:, c * 128:(c + 1) * 128, :].rearrange("h p d -> p h d"),
                      key=R_ctx, writes=[R_ctx])
            for hp in range(4):
                t = nxt("pst")
                for c in range(4):
                    S.op("tensor", lambda e, t=t, c=c, hp=hp: e.transpose(pst[t][:, c, :], ctmp[:, c, hp * 128:(hp + 1) * 128], ident[:]),
                         reads=[R_ost[0], R_const], writes=[R_pst[t]], partial=True)
                S.op("vector", lambda e, t=t, hp=hp: e.tensor_copy(kctxT[:, hp, :].rearrange("p (c k) -> p c k", c=4), pst[t][:, 0:4, :]),
                     reads=[R_pst[t]], writes=[R_ctx], partial=True)

        R_btimg = S.region("btimg")

        def build_btimg():
            negt = Sb2[0]
            S.op("vector", lambda e: e.memset(negt[0:64, 0:960], NEG), writes=[R_Sb2[0]])
            for k in range(8):
                S.dma("sync", btimg[:, k * 960:(k + 1) * 960], negt[0:64, 0:960], key=R_btimg, reads=[R_Sb2[0]], writes=[R_btimg])
            img_h = btimg.tensor
            dst = bass.AP(tensor=img_h, offset=8 * 7680 + 0, ap=[[7680 + 1, 49], [64, 120], [1, 16]])
            src = bass.AP(tensor=rpb_t, offset=7, ap=[[0, 49], [31, 120], [1, 16]])
            S.dma("sync", dst, src, key=R_btimg2, reads=[R_btimg], writes=[R_btimg])
            for wq in list(range(0, 8)) + list(range(57, 64)):
                cc = min(max(wq - 8, 0), 48)
                a = 15 + cc - wq
                dst = bass.AP(tensor=img_h, offset=wq * 7680 + cc, ap=[[64, 120], [1, 16]])
                src = bass.AP(tensor=rpb_t, offset=a, ap=[[31, 120], [1, 16]])
                S.dma("sync", dst, src, key=R_btimg2, reads=[R_btimg], writes=[R_btimg])

        R_btimg2 = S.region("btimg2")

        base = NMOD
        try:
            _stage(-3)
            setup_mod()
            _stage(-2)
            group_pcol(0)
            group_pcol(1)
            for _ in build_GG(0):
                pass
            _stage(0)
            run_sb(0, base)
            _stage(8)
            _stage(9)
            run_sb(1, base + NPER)
            _stage(18)
            run_sb(2, base + 2 * NPER)
        except _Stop:
            pass

        S.emit(nc, st)
    return nc


_NC_CACHE = {}


def kernel(x_prompt, x_sample, cache_k, cache_v, c, c_ctx, w_mod, b_mod, g_pre_mix, g_post_mix, g_pre_ffn, g_post_ffn,
           w_in, w_conv, w_br_conv, w_br_attn, rpb, w_gate, w_o, w_ff1, w_ff3, w_ff2):
    f = lambda a: np.ascontiguousarray(np.asarray(a, dtype=np.float32))
    x_prompt, x_sample, cache_k, cache_v, c, c_ctx = map(f, (x_prompt, x_sample, cache_k, cache_v, c, c_ctx))
    if "nc" not in _NC_CACHE:
        _NC_CACHE["nc"] = build()
    nc = _NC_CACHE["nc"]

    def col(v):
        return f(v).reshape(8, 128).T

    shared = {
        "bmodcol": f(f(b_mod).reshape(48, 128).T),
        "gcol": f(np.stack([col(g_pre_mix[0]), col(g_post_mix[0]), col(g_pre_ffn[0]), col(g_post_ffn[0])], axis=1)),
        "convw": f(f(w_conv)[0].reshape(3, 4, 128).transpose(2, 1, 0)),
        "rpb": f(rpb)[0],
        "w_mod": f(w_mod)[0], "w_in": f(w_in)[0], "w_brc": f(w_br_conv)[0], "w_bra": f(w_br_attn)[0],
        "w_gate": f(w_gate)[0], "w_o": f(w_o)[0], "w_ff1": f(w_ff1)[0], "w_ff3": f(w_ff3)[0], "w_ff2": f(w_ff2)[0],
    }
    in_maps = []
    for i in range(8):
        m = dict(shared)
        m["xp"] = f(x_prompt[4 * i:4 * i + 4].reshape(1024, D))
        m["xs"] = f(x_sample[i])
        m["ck"] = f(cache_k[i, 0])
        m["cv"] = f(cache_v[i, 0])
        m["c2col"] = f(np.stack([col(c_ctx), col(c[i])], axis=2))
        in_maps.append(m)
    res = run_bass_kernel_spmd(nc, in_maps, core_ids=list(range(8)))
    r = res.results
    y_p = np.concatenate([r[i]["yp"].reshape(4, 256, D) for i in range(8)], axis=0)
    y_s = np.stack([r[i]["ys"] for i in range(8)], axis=0)
    n_k = np.concatenate([r[i]["nk"] for i in range(8)], axis=0)[:, None]
    n_v = np.concatenate([r[i]["nv"] for i in range(8)], axis=0)[:, None]
    return (y_p.astype(np.float32), y_s.astype(np.float32), n_k.astype(np.float32), n_v.astype(np.float32))
```

```python
from contextlib import ExitStack
import numpy as np
import concourse.bass as bass
import concourse.mybir as mybir
from concourse.bass_utils import run_bass_kernel_spmd

F32 = mybir.dt.float32
BF16 = mybir.dt.bfloat16
AF = mybir.ActivationFunctionType
ALU = mybir.AluOpType
AX = mybir.AxisListType

SAME_ENGINE_SYNC = True
D = 1024
DFF = 2816
EPS = 1e-6
NEG = -1e30
NSLOT = 4
NTP = 9
STAGE = 999


class _Stop(Exception):
    pass


def _stage(n):
    if STAGE in (-11, -12, -13):
        if n >= 0:
            raise _Stop()
        return
    if n > STAGE:
        raise _Stop()


class Region:
    __slots__ = ("name", "writers", "readers", "war", "dma_n", "sem", "full")

    def __init__(self, name):
        self.name = name
        self.full = []
        self.writers = []
        self.readers = []
        self.war = []
        self.dma_n = 0
        self.sem = None


class Op:
    __slots__ = ("eng", "fn", "deps", "needs_inc", "val", "is_dma", "dma_reg", "dma_val")

    def __init__(self, eng, fn, is_dma=False):
        self.eng = eng
        self.fn = fn
        self.deps = []
        self.needs_inc = False
        self.val = None
        self.is_dma = is_dma
        self.dma_reg = None
        self.dma_val = None


COMPUTE = ("tensor", "vector", "scalar", "gpsimd")
QUEUES = ("tensor", "vector", "scalar", "gpsimd", "sync")


class Sched:
    def __init__(self):
        self.ops = {q: [] for q in QUEUES}
        self.dma_regions = []

    def region(self, name):
        return Region(name)

    def fence(self, olds, news):
        ops = []
        for r in olds:
            ops += r.readers + r.writers
        for r in news:
            r.readers = list(r.readers) + ops

    def _track(self, op, reads, writes, partial, pwrites=()):
        deps = []
        for r in reads:
            deps.extend(r.writers)
            r.readers.append(op)
        wl = [(r, partial) for r in writes] + [(r, True) for r in pwrites]
        for r, partial in wl:
            if partial and not r.readers and r.writers:
                deps.extend(r.war)
                deps.extend(r.full)
                r.writers.append(op)
            else:
                war = list(r.readers) + list(r.writers)
                deps.extend(war)
                r.war = war
                r.writers = [op]
                r.readers = []
                r.full = [] if partial else [op]
        seen = set()
        for d in deps:
            if d is op or id(d) in seen:
                continue
            seen.add(id(d))
            if d.eng == op.eng and not d.is_dma:
                if op.eng == "tensor" or not SAME_ENGINE_SYNC:
                    continue
            op.deps.append(d)
            if not d.is_dma:
                d.needs_inc = True

    def op(self, eng, fn, reads=(), writes=(), partial=False, pwrites=()):
        o = Op(eng, fn)
        self._track(o, reads, writes, partial, pwrites)
        self.ops[eng].append(o)
        return o

    def dma(self, eng, out, in_, key, reads=(), writes=(), partial=True, **kw):
        def fn(e, out=out, in_=in_, kw=kw):
            return e.dma_start(out=out, in_=in_, **kw)
        o = Op(eng, fn, is_dma=True)
        o.dma_reg = key
        key.dma_n += 1
        o.dma_val = 16 * key.dma_n
        if key not in self.dma_regions:
            self.dma_regions.append(key)
        self._track(o, reads, writes, partial)
        self.ops[eng].append(o)
        return o

    def emit(self, nc, stack):
        esem = {q: stack.enter_context(nc.semaphore(f"e_{q}")) for q in COMPUTE}
        for i, r in enumerate(self.dma_regions):
            r.sem = stack.enter_context(nc.semaphore(f"d{i}_{r.name}"))
        for q in QUEUES:
            c = 0
            for o in self.ops[q]:
                if not o.is_dma and o.needs_inc:
                    c += 1
                    o.val = c
        block = stack.enter_context(nc.Block())
        all_dma = [(r.sem, 16 * r.dma_n) for r in self.dma_regions]

        def make(q):
            def body(e):
                waited = {}
                for o in self.ops[q]:
                    need = {}
                    for d in o.deps:
                        if d.is_dma:
                            sem, v = d.dma_reg.sem, d.dma_val
                        else:
                            sem, v = esem[d.eng], d.val
                        k = id(sem)
                        if k not in need or need[k][1] < v:
                            need[k] = (sem, v)
                    for k, (sem, v) in need.items():
                        if waited.get(k, 0) >= v:
                            continue
                        waited[k] = v
                        e.wait_ge(sem, v)
                    ins = o.fn(e)
                    if o.is_dma:
                        ins.then_inc(o.dma_reg.sem, 16)
                    elif o.needs_inc:
                        ins.then_inc(esem[q], 1)
                if q == "sync":
                    for sem, v in all_dma:
                        if waited.get(id(sem), 0) < v:
                            e.wait_ge(sem, v)
            return body

        block.tensor(make("tensor"))
        block.vector(make("vector"))
        block.scalar(make("scalar"))
        block.gpsimd(make("gpsimd"))
        block.sync(make("sync"))


def build():
    nc = bass.Bass("TRN2", target_bir_lowering=False)

    def din(name, shape):
        return nc.dram_tensor(name, shape, F32, kind="ExternalInput").ap()

    def dout(name, shape):
        return nc.dram_tensor(name, shape, F32, kind="ExternalOutput").ap()

    xp = din("xp", [1024, D])
    xs = din("xs", [2048, D])
    ck = din("ck", [8, 512, 64])
    cv = din("cv", [8, 512, 64])
    c2col = din("c2col", [128, 8, 2])
    bmodcol = din("bmodcol", [128, 48])
    gcol_d = din("gcol", [128, 4, 8])
    convw_d = din("convw", [128, 4, 3])
    rpb = din("rpb", [8, 15, 31])
    w_mod = din("w_mod", [D, 6 * D])
    w_in = din("w_in", [D, 3072])
    w_brc = din("w_brc", [512, D])
    w_bra = din("w_bra", [512, D])
    w_gate = din("w_gate", [D, 2 * D])
    w_o = din("w_o", [D, D])
    w_ff1 = din("w_ff1", [D, DFF])
    w_ff3 = din("w_ff3", [D, DFF])
    w_ff2 = din("w_ff2", [DFF, D])
    yp = dout("yp", [1024, D])
    ys = dout("ys", [2048, D])
    nk = dout("nk", [4, 8, 256, 64])
    nv = dout("nv", [4, 8, 256, 64])

    btimg_t = nc.dram_tensor("btimg", [64, 7680], F32, kind="Internal")
    btimg = btimg_t.ap()
    rpb_t = rpb.tensor

    S = Sched()
    with ExitStack() as st:
        def sb(name, shape, dt):
            return st.enter_context(nc.sbuf_tensor(name, shape, dt))

        NS = NSLOT + 1
        ring = [sb(f"ring{i}", [128, 8, 512], BF16) for i in range(NS)]
        R_ring = [S.region(f"ring{i}") for i in range(NS)]
        xin = [sb(f"xin{i}", [128, D], F32) for i in range(2)]
        R_xin = [S.region(f"xin{i}") for i in range(2)]
        ost = [sb(f"ost{i}", [128, D], F32) for i in range(2)]
        R_ost = [S.region(f"ost{i}") for i in range(2)]
        GG = [sb(f"GG{i}", [128, D], F32) for i in range(2)]
        R_GG = S.region("GG")
        kctxT = ring[NSLOT][:, 0:4, :]
        vctx = ring[NSLOT][:, 4:8, :]
        R_ctx = R_ring[NSLOT]
        R_ctmpk = S.region("ctmpk")
        Sb2 = [sb(f"Sb{i}", [128, 1088], F32) for i in range(2)]
        R_Sb2 = [S.region(f"Sb{i}") for i in range(2)]
        Pb2 = [sb(f"Pb{i}", [128, 1088], BF16) for i in range(3)]
        R_Pb2 = [S.region(f"Pb{i}") for i in range(3)]
        PTs2 = [sb(f"PTs{i}", [128, 9, 128], BF16) for i in range(2)]
        R_PTs2 = [S.region(f"PTs{i}") for i in range(2)]
        stat2 = [sb(f"stat{i}", [128, 16], F32) for i in range(5)]
        R_stat2 = [S.region(f"stat{i}") for i in range(5)]
        tmpA2 = [Sb2[i][:, 0:512] for i in range(2)]
        tmpB2 = [Sb2[i][:, 512:1024] for i in range(2)]
        R_tmpA2 = [S.region(f"tmpA{i}") for i in range(2)]
        R_tmpB2 = [S.region(f"tmpB{i}") for i in range(2)]
        par = {"v": 0}

        def flip():
            par["v"] ^= 1
            return par["v"]

        ident = sb("ident", [128, 128], BF16)
        identf = sb("identf", [128, 128], F32)
        diag = sb("diag", [128, 128], F32)
        dhi = sb("dhi", [128, 128], BF16)
        dlo = sb("dlo", [128, 128], BF16)
        onesb = sb("onesb", [128, 128], BF16)
        R_dhi = S.region("dhi")
        R_dlo = S.region("dlo")
        R_const = S.region("const")
        R_diag = S.region("diag")
        cT = sb("cT", [128, 8, 2], F32)
        scT = sb("scT", [128, 8, 2], BF16)
        bmc = sb("bmc", [128, 48], F32)
        modcol = sb("modcol", [128, 48, 2], F32)
        gcol = sb("gcolsb", [128, 4, 8], F32)
        convw = sb("convwsb", [128, 4, 3], F32)
        pcol = sb("pcol", [128, 12, 8], F32)
        R_mod = S.region("mod")
        R_pcol = S.region("pcol")
        ss0 = sb("ss0", [128, 8], F32)
        R_ss0 = S.region("ss0")

        FA = sb("FA", [128, 8256], F32)
        x1buf = FA[:, 0:8192].rearrange("p (t d) -> p t d", t=8)
        BT = FA[:, 0:7680].rearrange("p (h r w) -> p h r w", h=8, r=15)
        BTf_top = FA[:, 0:7680].rearrange("p (n w) -> p n w", w=64)
        BTf_bot = FA[:, 64:64 + 7680].rearrange("p (n w) -> p n w", w=64)
        hT = sb("HT", [128, 8, 1280], BF16)
        h2T = hT[:, :, 0:1024]
        BU = sb("BU", [128, 9232], BF16)
        UW = 1284
        bT = BU[:, 0:4096].rearrange("p (c t) -> p c t", c=4)
        uT = BU[:, 4096:4096 + 4 * UW].rearrange("p (c t) -> p c t", c=4)
        mgT = BU[:, 0:8192].rearrange("p (k t) -> p k t", k=8)
        ATb = sb("ATb", [128, 22528], BF16)
        aT = ATb[:, :].rearrange("p (k t) -> p k t", k=22)
        qT = ATb[:, 0:4096].rearrange("p (c t) -> p c t", c=4)
        kT = ATb[:, 4096:9216].rearrange("p (c t) -> p c t", c=4)
        vv = ATb[:, 9216:14336].rearrange("p (t f) -> p t f", t=10)
        yattT = ATb[:, 14336:18432].rearrange("p (c t) -> p c t", c=4)
        yconvT = ATb[:, 18432:22528].rearrange("p (c t) -> p c t", c=4)

        def y0ap(t):
            if t < 4:
                return xin[t // 2][:, (t % 2) * 512:(t % 2 + 1) * 512]
            return Sb2[(t - 4) // 2][:, ((t - 4) % 2) * 512:((t - 4) % 2 + 1) * 512]

        R_x1 = [S.region(f"x1_{i}") for i in range(8)]
        R_BT = S.region("BT")
        R_hT = [S.region(f"hT{i}") for i in range(10)]
        R_h2T = [S.region(f"h2T{i}") for i in range(8)]
        R_y0 = [R_xin[0], R_xin[0], R_xin[1], R_xin[1], R_Sb2[0], R_Sb2[0], R_Sb2[1], R_Sb2[1]]
        R_b, R_u, R_mg = S.region("b"), S.region("u"), S.region("mg")
        R_q, R_k, R_v = S.region("q"), S.region("k"), S.region("v")
        R_yatt, R_yconv, R_aT = S.region("yatt"), S.region("yconv"), S.region("aT")

        ps = [st.enter_context(nc.psum_tensor(f"ps{i}", [128, 512], F32)) for i in range(6)]
        pst = [st.enter_context(nc.psum_tensor(f"pst{i}", [128, 8, 128], BF16)) for i in range(2)]
        R_ps = [S.region(f"ps{i}") for i in range(6)]
        R_pst = [S.region(f"pst{i}") for i in range(2)]
        rr = {"ps": 0, "pst": 0, "xin": 0, "ost": 0}

        def nb():
            i = rr["ps"]
            rr["ps"] = (i + 1) % 6
            return i

        def nxt(key, n=2):
            i = rr[key]
            rr[key] = (i + 1) % n
            return i

        chunks = []
        state = {"issued": 0}
        slot_of = {}
        slot_free = [True] * NS
        slot_enabled = [True] * NS

        def ensure(i, i1=None, disable_ctx_slot=False):
            i1 = i if i1 is None else i1
            for j in list(slot_of):
                if j < i:
                    slot_free[slot_of.pop(j)] = True
            if disable_ctx_slot:
                slot_enabled[NSLOT] = False
            while state["issued"] < len(chunks):
                cand = [q for q in range(NS) if slot_free[q] and slot_enabled[q]]
                if not cand:
                    break
                j = state["issued"]
                q = cand[0]
                slot_free[q] = False
                slot_of[j] = q
                for (k0, k1, f0, f1, src) in chunks[j]:
                    S.dma("gpsimd", ring[q][:, k0:k1, f0:f1], src.rearrange("(kc p) f -> p kc f", p=128),
                          key=R_ring[q], writes=[R_ring[q]])
                state["issued"] += 1
            assert state["issued"] > i1, (i, i1, state["issued"])
            return slot_of[i]

        def std(w, f0, nf=512):
            return [(0, 8, 0, nf, w[:, f0:f0 + nf])]

        def sb_chunks():
            cl = []
            for c in (3, 4, 5, 0, 1, 2):
                cl.append(std(w_in, c * 512))
            for mg in range(2):
                cl.append(std(w_gate, mg * 512))
                cl.append(std(w_gate, 1024 + mg * 512))
                cl.append([(0, 4, 0, 512, w_brc[:, mg * 512:(mg + 1) * 512]),
                           (4, 8, 0, 512, w_bra[:, mg * 512:(mg + 1) * 512])])
            cl.append(std(w_o, 0))
            cl.append(std(w_o, 512))
            for g in range(6):
                nf = 512 if g < 5 else 256
                cl.append(std(w_ff1, g * 512, nf))
                cl.append(std(w_ff3, g * 512, nf))
            for nh in range(2):
                for (r0, r1) in ((0, 1024), (1024, 2048), (2048, 2816)):
                    cl.append([(0, (r1 - r0) // 128, 0, 512, w_ff2[r0:r1, nh * 512:(nh + 1) * 512])])
            return cl

        for j in range(12):
            chunks.append(std(w_mod, j * 512))
        NMOD = 12
        per_sb = sb_chunks()
        NPER = len(per_sb)
        for _ in range(3):
            chunks.extend(per_sb)

        S.op("gpsimd", lambda e: e.memset(identf[:], 0.0), writes=[R_const])
        S.op("gpsimd", lambda e: e.affine_select(out=identf[:], in_=identf[:], compare_op=ALU.not_equal, fill=1.0,
                                                 base=0, pattern=[[-1, 128]], channel_multiplier=1),
             reads=[R_const], writes=[R_const])
        S.op("vector", lambda e: e.tensor_copy(ident[:], identf[:]), reads=[R_const], writes=[R_const])
        S.op("vector", lambda e: e.memset(onesb[:], 1.0), reads=[R_const], writes=[R_const])
        S.dma("sync", cT[:], c2col[:, :, :], key=R_mod, writes=[R_mod])
        S.dma("sync", bmc[:], bmodcol[:, :], key=R_mod, writes=[R_mod])
        S.dma("sync", gcol[:], gcol_d[:, :, :], key=R_mod, writes=[R_mod])
        S.dma("sync", convw[:], convw_d[:, :, :], key=R_mod, writes=[R_mod])
        S.op("scalar", lambda e: e.activation(out=scT[:], in_=cT[:], func=AF.Silu), reads=[R_mod], writes=[R_mod])
        def setup_mod():
            mp = nb()
            modps = ps[mp][:, 0:96].rearrange("p (j g) -> p j g", g=2)
            for j in range(NMOD):
                s = ensure(j)
                for m4 in range(4):
                    for kc in range(8):
                        S.op("tensor", lambda e, s=s, m4=m4, kc=kc, j=j: e.matmul(
                            modps[:, j * 4 + m4, :], lhsT=ring[s][:, kc, m4 * 128:(m4 + 1) * 128], rhs=scT[:, kc, :],
                            start=(kc == 0), stop=(kc == 7)),
                            reads=[R_ring[s], R_mod], writes=[R_ps[mp]], partial=True)
            for g in range(2):
                S.op("vector", lambda e, g=g: e.tensor_tensor(out=modcol[:, :, g], in0=modps[:, :, g], in1=bmc[:, :], op=ALU.add),
                     reads=[R_ps[mp], R_mod], writes=[R_mod], partial=True)

        def group_pcol(g):
            def mc(i):
                return modcol[:, i * 8:(i + 1) * 8, g]
            V = "vector"
            o = g * 6
            S.op(V, lambda e: e.scalar_tensor_tensor(out=pcol[:, o + 0, :], in0=mc(1), scalar=1.0, in1=gcol[:, 0, :], op0=ALU.add, op1=ALU.mult),
                 reads=[R_mod], writes=[R_pcol], partial=True)
            S.op(V, lambda e: e.tensor_copy(pcol[:, o + 1, :], mc(0)), reads=[R_mod], writes=[R_pcol], partial=True)
            S.op(V, lambda e: e.scalar_tensor_tensor(out=pcol[:, o + 2, :], in0=mc(4), scalar=1.0, in1=gcol[:, 2, :], op0=ALU.add, op1=ALU.mult),
                 reads=[R_mod], writes=[R_pcol], partial=True)
            S.op(V, lambda e: e.tensor_copy(pcol[:, o + 3, :], mc(3)), reads=[R_mod], writes=[R_pcol], partial=True)
            S.op(V, lambda e: e.tensor_tensor(out=pcol[:, o + 4, :], in0=mc(2), in1=gcol[:, 1, :], op=ALU.mult),
                 reads=[R_mod], writes=[R_pcol], partial=True)
            S.op(V, lambda e: e.tensor_tensor(out=pcol[:, o + 5, :], in0=mc(5), in1=gcol[:, 3, :], op=ALU.mult),
                 reads=[R_mod], writes=[R_pcol], partial=True)

        def build_GG(g):
            V = "vector"
            first = True
            for gi in range(2):
                for half in range(2):
                    b = nb()
                    for k4 in range(4):
                        kc = half * 4 + k4
                        S.op(V, lambda e, kc=kc, gi=gi: e.tensor_scalar(diag[:], identf[:], pcol[:, g * 6 + 4 + gi, kc:kc + 1], None, ALU.mult),
                             reads=[R_pcol, R_const], writes=[R_diag])
                        S.op(V, lambda e: e.tensor_copy(dhi[:], diag[:]), reads=[R_diag], writes=[R_dhi])
                        S.op(V, lambda e: e.tensor_tensor(out=diag[:], in0=diag[:], in1=dhi[:], op=ALU.subtract), reads=[R_diag, R_dhi], writes=[R_diag])
                        S.op(V, lambda e: e.tensor_copy(dlo[:], diag[:]), reads=[R_diag], writes=[R_dlo])
                        S.op("tensor", lambda e, b=b, k4=k4: e.matmul(ps[b][:, k4 * 128:(k4 + 1) * 128], lhsT=onesb[:], rhs=dhi[:],
                                                                       start=True, stop=False),
                             reads=[R_dhi, R_const], writes=[R_ps[b]], partial=True)
                        S.op("tensor", lambda e, b=b, k4=k4: e.matmul(ps[b][:, k4 * 128:(k4 + 1) * 128], lhsT=onesb[:], rhs=dlo[:],
                                                                       start=False, stop=True),
                             reads=[R_dlo, R_const], writes=[R_ps[b]], partial=True)
                    S.op("vector", lambda e, b=b, gi=gi, half=half: e.tensor_copy(GG[gi][:, half * 512:(half + 1) * 512], ps[b][:, :]),
                         reads=[R_ps[b]], writes=[R_GG], partial=not first)
                    first = False
                    yield

        nt_par = {"v": 0}

        def norm_transpose(src_ap, R_src, dstT, R_dst, col0, pa, ps_, defer=False):
            nt_par["v"] ^= 1
            p = nt_par["v"]
            junk, R_junk, stat, R_stat = Pb2[p][:, 0:1024], R_Pb2[p], stat2[2 + p], R_stat2[2 + p]
            xsb, R_xsb = PTs2[p][:, 0:8, :].rearrange("p k t -> p (k t)"), R_PTs2[p]
            S.op("scalar", lambda e: e.activation(out=junk[:], in_=src_ap, func=AF.Square, accum_out=stat[:, 0:1]),
                 reads=[R_src], writes=[R_junk, R_stat])
            S.op("scalar", lambda e: e.activation(out=stat[:, 1:2], in_=stat[:, 0:1], func=AF.Ln, scale=1.0 / D, bias=EPS),
                 reads=[R_stat], writes=[R_stat])
            S.op("scalar", lambda e: e.activation(out=stat[:, 2:3], in_=stat[:, 1:2], func=AF.Exp, scale=-0.5),
                 reads=[R_stat], writes=[R_stat])
            if NTP < 2:
                return
            S.op("vector", lambda e: e.tensor_scalar(xsb[:], src_ap, stat[:, 2:3], None, ALU.mult),
                 reads=[R_src, R_stat], writes=[R_xsb])
            if NTP < 3:
                return

            def stage2():
                nt_stage2(dstT, R_dst, col0, pa, ps_, xsb, R_xsb)
            if defer:
                prev = nt_pending["f"]
                nt_pending["f"] = stage2
                if prev is not None:
                    prev()
            else:
                stage2()

        nt_pending = {"f": None}

        def nt_flush():
            if nt_pending["f"] is not None:
                nt_pending["f"]()
                nt_pending["f"] = None

        def nt_stage2(dstT, R_dst, col0, pa, ps_, xsb, R_xsb):
            t = nxt("pst")
            for kc in range(8):
                S.op("tensor", lambda e, kc=kc, t=t: e.transpose(pst[t][:, kc, :], xsb[:, kc * 128:(kc + 1) * 128], ident[:]),
                     reads=[R_xsb, R_const], writes=[R_pst[t]], partial=True)
            if NTP < 4:
                return
            use_act = (col0 // 128) % 2 == 0
            for kc in range(8):
                if use_act:
                    S.op("scalar", lambda e, kc=kc, t=t: e.activation(out=dstT[:, kc, col0:col0 + 128], in_=pst[t][:, kc, :], func=AF.Identity,
                                                                       scale=pcol[:, pa, kc:kc + 1], bias=pcol[:, ps_, kc:kc + 1]),
                         reads=[R_pst[t], R_pcol], writes=[R_dst], partial=True)
                else:
                    S.op("vector", lambda e, kc=kc, t=t: e.tensor_scalar(dstT[:, kc, col0:col0 + 128], pst[t][:, kc, :],
                                                                          pcol[:, pa, kc:kc + 1], pcol[:, ps_, kc:kc + 1], ALU.mult, ALU.add),
                         reads=[R_pst[t], R_pcol], writes=[R_dst], partial=True)

        def projA(slot, m4, src, R_src_list, blocks, nk_, kmap=None):
            banks = [nb() for _ in blocks]
            for kc in range(nk_):
                kk = kc if kmap is None else kmap(kc)
                for bi, (c0, n) in enumerate(blocks):
                    b = banks[bi]
                    S.op("tensor", lambda e, b=b, kk=kk, kc=kc, c0=c0, n=n: e.matmul(
                        ps[b][:, 0:n], lhsT=ring[slot][:, kk, m4 * 128:(m4 + 1) * 128], rhs=src[:, kc, c0:c0 + n],
                        start=(kc == 0), stop=(kc == nk_ - 1)),
                        reads=[R_ring[slot]] + R_src_list, writes=[R_ps[b]], partial=True)
            return banks

        def run_sb(sbi, base):
            prompt = (sbi == 0)
            goff = 0 if prompt else 6
            NTE = 8 if prompt else 10
            c0t = 0 if sbi < 2 else 2
            ext_row0 = 0 if sbi < 2 else 12
            core_row0 = 0 if sbi == 1 else 16
            xsrc = xp if prompt else xs
            xrow0 = 0 if sbi < 2 else 768
            ydst = yp if prompt else ys
            yrow0 = 0 if prompt else (0 if sbi == 1 else 1024)
            NEXT = NTE * 128
            ccol = c0t * 128

            def ucol(t):
                return (t + 1 + 2 * (t // 256)) if prompt else (t + ccol + 1)

            def chunk(i, n=1, **kw):
                ensure(base + i, base + i + n - 1, **kw)
                return [slot_of[base + i + j] for j in range(n)] if n > 1 else slot_of[base + i]

            if not prompt:
                for j in list(slot_of):
                    if j < base:
                        slot_free[slot_of.pop(j)] = True
                assert slot_free[NSLOT] and not slot_enabled[NSLOT]
                load_ctx()
            S.fence([R_h2T[i] for i in range(8)], R_hT)
            S.fence(R_tmpA2 + R_tmpB2, R_Sb2)
            S.fence([R_mg], [R_b, R_u])
            S.fence([R_aT], [R_q, R_k, R_v, R_yatt, R_yconv])

            for i in range(NTE):
                xi = nxt("xin")
                S.dma("sync", xin[xi][:], xsrc[xrow0 + i * 128: xrow0 + (i + 1) * 128, :], key=R_xin[xi], writes=[R_xin[xi]])
                norm_transpose(xin[xi][:], R_xin[xi], hT, R_hT[i], i * 128, goff + 0, goff + 1, defer=True)
            nt_flush()
            if sbi == 0:
                build_btimg()

            _stage(sbi * 10 + 1)
            extblocks = [(0, 512), (512, 512)] + ([(1024, 256)] if not prompt else [])
            coreblocks = [(ccol, 512), (ccol + 512, 512)]
            allhT = R_hT[:NTE]

            s = chunk(0)
            for m4 in range(4):
                banks = projA(s, m4, hT, allhT, coreblocks, 8)
                for bi, b in enumerate(banks):
                    S.op("scalar", lambda e, b=b, bi=bi, m4=m4: e.activation(out=qT[:, m4, bi * 512:(bi + 1) * 512], in_=ps[b][:, :],
                                                                             func=AF.Copy, scale=0.125),
                         reads=[R_ps[b]], writes=[R_q], partial=True)
            s = chunk(1)
            for m4 in range(4):
                banks = projA(s, m4, hT, allhT, extblocks, 8)
                for bi, b in enumerate(banks):
                    c0_, n = extblocks[bi]
                    S.op("vector", lambda e, b=b, c0_=c0_, n=n, m4=m4: e.tensor_copy(kT[:, m4, c0_:c0_ + n], ps[b][:, 0:n]),
                         reads=[R_ps[b]], writes=[R_k], partial=True)
            if prompt:
                for i in range(8):
                    b = nb()
                    for kc in range(8):
                        S.op("tensor", lambda e, b=b, kc=kc, i=i, s=s: e.matmul(ps[b][:, :], lhsT=hT[:, kc, i * 128:(i + 1) * 128], rhs=ring[s][:, kc, :],
                                                                          start=(kc == 0), stop=(kc == 7)),
                             reads=[R_ring[s], R_hT[i]], writes=[R_ps[b]], partial=True)
                    oi = nxt("ost")
                    S.op("scalar", lambda e, b=b, oi=oi: e.copy(ost[oi][:, 0:512], ps[b][:, :]), reads=[R_ps[b]], writes=[R_ost[oi]])
                    S.dma("sync", nk[i // 2].rearrange("h s d -> s h d")[(i % 2) * 128:(i % 2 + 1) * 128],
                          ost[oi][:, 0:512].rearrange("p (h d) -> p h d", h=8), key=R_ost[oi], reads=[R_ost[oi]])
            s = chunk(2)
            for i in range(NTE):
                b = nb()
                for kc in range(8):
                    S.op("tensor", lambda e, b=b, kc=kc, i=i, s=s: e.matmul(ps[b][:, :], lhsT=hT[:, kc, i * 128:(i + 1) * 128], rhs=ring[s][:, kc, :],
                                                                      start=(kc == 0), stop=(kc == 7)),
                         reads=[R_ring[s], R_hT[i]], writes=[R_ps[b]], partial=True)
                if prompt:
                    oi = nxt("ost")
                    S.op("scalar", lambda e, b=b, oi=oi: e.copy(ost[oi][:, 0:512], ps[b][:, :]), reads=[R_ps[b]], writes=[R_ost[oi]])
                    S.op("vector", lambda e, oi=oi, i=i: e.tensor_copy(vv[:, i, :], ost[oi][:, 0:512]), reads=[R_ost[oi]], writes=[R_v], partial=True)
                else:
                    S.op("vector", lambda e, b=b, i=i: e.tensor_copy(vv[:, i, :], ps[b][:, :]), reads=[R_ps[b]], writes=[R_v], partial=True)
                if prompt:
                    S.dma("sync", nv[i // 2].rearrange("h s d -> s h d")[(i % 2) * 128:(i % 2 + 1) * 128],
                          ost[oi][:, 0:512].rearrange("p (h d) -> p h d", h=8), key=R_ost[oi], reads=[R_ost[oi]])

            _stage(sbi * 10 + 2)
            if prompt:
                for qi in range(8):
                    sq = qi // 2
                    for hh in range(8):
                        hp, ho = hh // 2, (hh % 2) * 64
                        segs = [(kT[ho:ho + 64, hp, sq * 256:(sq + 1) * 256], 256, None)]
                        pv = [(j * 128, 128, vv[:, sq * 2 + j, hp * 128:(hp + 1) * 128]) for j in range(2)]
                        attention_unit(qi, hh, segs, pv, [R_k, R_v])
            else:
                S.fence(R_x1, [R_BT])
                S.dma("sync", FA[0:64, 0:7680], btimg[:, :], key=R_BT, reads=[R_btimg], writes=[R_BT])
                S.dma("sync", FA[64:128, 64:64 + 7680], btimg[:, :], key=R_BT, reads=[R_btimg], writes=[R_BT])
                for qi in range(8):
                    Rr = core_row0 + 2 * qi
                    r0t = min(max(Rr - 4, 0), 24)
                    r0b = min(max(Rr + 1 - 4, 0), 24)
                    rot = r0t - Rr + 7
                    nband = 576 if r0b != r0t else 512
                    kc0 = (r0t - ext_row0) * 64
                    vt0 = (r0t - ext_row0) // 2
                    for hh in range(8):
                        hp, ho = hh // 2, (hh % 2) * 64
                        segs = [(kT[ho:ho + 64, hp, kc0:kc0 + nband], nband, rot),
                                (kctxT[ho:ho + 64, hp, :], 512, None)]
                        pv = []
                        for j in range(4):
                            pv.append((j * 128, 128, vv[:, vt0 + j, hp * 128:(hp + 1) * 128]))
                        if nband == 576:
                            pv.append((512, 64, vv[0:64, vt0 + 4, hp * 128:(hp + 1) * 128]))
                        for j in range(4):
                            pv.append((nband + j * 128, 128, vctx[:, j, hp * 128:(hp + 1) * 128]))
                        if nband == 512:
                            pv = pv[:4] + [None] + pv[4:]
                        attention_unit_s(qi, hh, segs, pv)

            _stage(sbi * 10 + 3)
            att_flush(build_GG(1) if sbi == 1 else None)
            slot_enabled[NSLOT] = True
            S.fence(R_Sb2, R_tmpA2 + R_tmpB2)
            s = chunk(3)
            for m4 in range(4):
                banks = projA(s, m4, hT, allhT, coreblocks, 8)
                for bi, b in enumerate(banks):
                    S.op("scalar", lambda e, b=b, bi=bi, m4=m4: e.copy(bT[:, m4, bi * 512:(bi + 1) * 512], ps[b][:, :]),
                         reads=[R_ps[b]], writes=[R_b], partial=True)
            S.op("gpsimd", lambda e: e.memset(uT[:, :, :], 0.0), writes=[R_u])
            s1, s2 = chunk(4, 2)
            ublocks = [(0, 512), (512, 512), (1024, 2)] if sbi == 1 else extblocks
            for m4 in range(4):
                bk1 = projA(s1, m4, hT, allhT, ublocks, 8)
                bk2 = projA(s2, m4, hT, allhT, ublocks, 8)
                for bi, (c0_, n) in enumerate(ublocks):
                    pt_ = flip(); tmpA, tmpB, R_tmpA, R_tmpB = tmpA2[pt_], tmpB2[pt_], R_tmpA2[pt_], R_tmpB2[pt_]
                    S.op("scalar", lambda e, tmpA=tmpA, b=bk1[bi], n=n: e.copy(tmpA[:, 0:n], ps[b][:, 0:n]), reads=[R_ps[bk1[bi]]], writes=[R_tmpA])
                    if prompt:
                        for hs in range(2):
                            t0 = c0_ + hs * 256
                            S.op("vector", lambda e, tmpA=tmpA, b=bk2[bi], hs=hs, t0=t0, m4=m4: e.tensor_tensor(
                                out=uT[:, m4, ucol(t0):ucol(t0) + 256], in0=tmpA[:, hs * 256:(hs + 1) * 256],
                                in1=ps[b][:, hs * 256:(hs + 1) * 256], op=ALU.mult),
                                reads=[R_tmpA, R_ps[bk2[bi]]], writes=[R_u], partial=True)
                    else:
                        S.op("vector", lambda e, tmpA=tmpA, b=bk2[bi], c0_=c0_, n=n, m4=m4: e.tensor_tensor(
                            out=uT[:, m4, c0_ + 1:c0_ + 1 + n], in0=tmpA[:, 0:n], in1=ps[b][:, 0:n], op=ALU.mult),
                            reads=[R_tmpA, R_ps[bk2[bi]]], writes=[R_u], partial=True)

            ranges = [(sq * 256, 256) for sq in range(4)] if prompt else [(0, 512), (512, 512)]
            for m4 in range(4):
                for (t0, n) in ranges:
                    pt_ = flip(); tmpA, tmpB, R_tmpA, R_tmpB = tmpA2[pt_], tmpB2[pt_], R_tmpA2[pt_], R_tmpB2[pt_]
                    u0 = ucol(t0)
                    S.op("vector", lambda e, tmpB=tmpB, m4=m4, u0=u0, n=n: e.tensor_scalar(tmpB[:, 0:n], uT[:, m4, u0:u0 + n], convw[:, m4, 1:2], None, ALU.mult),
                         reads=[R_u], writes=[R_tmpB])
                    S.op("vector", lambda e, tmpB=tmpB, m4=m4, u0=u0, n=n: e.scalar_tensor_tensor(
                        out=tmpB[:, 0:n], in0=uT[:, m4, u0 - 1:u0 - 1 + n], scalar=convw[:, m4, 0:1], in1=tmpB[:, 0:n], op0=ALU.mult, op1=ALU.add),
                        reads=[R_u, R_tmpB], writes=[R_tmpB])
                    S.op("vector", lambda e, tmpB=tmpB, m4=m4, u0=u0, n=n: e.scalar_tensor_tensor(
                        out=tmpB[:, 0:n], in0=uT[:, m4, u0 + 1:u0 + 1 + n], scalar=convw[:, m4, 2:3], in1=tmpB[:, 0:n], op0=ALU.mult, op1=ALU.add),
                        reads=[R_u, R_tmpB], writes=[R_tmpB])
                    S.op("vector", lambda e, tmpB=tmpB, m4=m4, t0=t0, n=n: e.tensor_tensor(out=yconvT[:, m4, t0:t0 + n], in0=tmpB[:, 0:n],
                                                                               in1=bT[:, m4, t0:t0 + n], op=ALU.mult),
                         reads=[R_tmpB, R_b], writes=[R_yconv], partial=True)

            _stage(sbi * 10 + 4)
            S.fence([R_b, R_u], [R_mg])
            cb = [(0, 512), (512, 512)]
            for mg in range(2):
                sgc, sga, sbr = chunk(6 + mg * 3, 3)
                for m4 in range(4):
                    m = mg * 4 + m4
                    for (t0, n) in cb:
                        pt_ = flip(); tmpA, tmpB, R_tmpA, R_tmpB = tmpA2[pt_], tmpB2[pt_], R_tmpA2[pt_], R_tmpB2[pt_]
                        (bgc,) = projA(sgc, m4, hT, allhT, [(ccol + t0, n)], 8)
                        (bbc,) = projA(sbr, m4, yconvT, [R_yconv], [(t0, n)], 4)
                        (bga,) = projA(sga, m4, hT, allhT, [(ccol + t0, n)], 8)
                        (bba,) = projA(sbr, m4, yattT, [R_yatt], [(t0, n)], 4, kmap=lambda kc: kc + 4)
                        S.op("scalar", lambda e, tmpA=tmpA, b=bgc: e.activation(out=tmpA[:, :], in_=ps[b][:, :], func=AF.Sigmoid),
                             reads=[R_ps[bgc]], writes=[R_tmpA])
                        S.op("vector", lambda e, tmpA=tmpA, b=bbc: e.tensor_tensor(out=tmpA[:, :], in0=tmpA[:, :], in1=ps[b][:, :], op=ALU.mult),
                             reads=[R_tmpA, R_ps[bbc]], writes=[R_tmpA])
                        S.op("scalar", lambda e, tmpB=tmpB, b=bga: e.activation(out=tmpB[:, :], in_=ps[b][:, :], func=AF.Sigmoid),
                             reads=[R_ps[bga]], writes=[R_tmpB])
                        S.op("vector", lambda e, tmpB=tmpB, b=bba: e.tensor_tensor(out=tmpB[:, :], in0=tmpB[:, :], in1=ps[b][:, :], op=ALU.mult),
                             reads=[R_tmpB, R_ps[bba]], writes=[R_tmpB])
                        S.op("gpsimd", lambda e, tmpA=tmpA, tmpB=tmpB, m=m, t0=t0, n=n: e.tensor_tensor(out=mgT[:, m, t0:t0 + n], in0=tmpA[:, 0:n], in1=tmpB[:, 0:n], op=ALU.add),
                             reads=[R_tmpA, R_tmpB], writes=[R_mg], partial=True)

            _stage(sbi * 10 + 5)
            S.fence([R_BT], R_x1)
            S.fence(R_hT, R_h2T)
            so = chunk(12, 2)
            def wo_mm(t):
                bk = [nb(), nb()]
                for nh in range(2):
                    for kc in range(8):
                        S.op("tensor", lambda e, nh=nh, kc=kc, t=t, bk=bk: e.matmul(ps[bk[nh]][:, :], lhsT=mgT[:, kc, t * 128:(t + 1) * 128],
                                                                            rhs=ring[so[nh]][:, kc, :], start=(kc == 0), stop=(kc == 7)),
                             reads=[R_mg, R_ring[so[nh]]], writes=[R_ps[bk[nh]]], partial=True)
                return bk
            bks = {0: wo_mm(0)}
            for t in range(8):
                if t + 1 < 8:
                    bks[t + 1] = wo_mm(t + 1)
                bk = bks[t]
                xi = nxt("xin")
                S.dma("sync", xin[xi][:], xsrc[xrow0 + (c0t + t) * 128: xrow0 + (c0t + t + 1) * 128, :], key=R_xin[xi], writes=[R_xin[xi]])
                post_norm_residual(bk, None, xin[xi], R_xin[xi], GG[0], x1buf[:, t, :], R_x1[t])
                if t >= 2:
                    norm_transpose(x1buf[:, t - 2, :], R_x1[t - 2], h2T, R_h2T[t - 2], (t - 2) * 128, goff + 2, goff + 3, defer=True)
            for t in (6, 7):
                norm_transpose(x1buf[:, t, :], R_x1[t], h2T, R_h2T[t], t * 128, goff + 2, goff + 3, defer=True)
            nt_flush()

            _stage(sbi * 10 + 6)
            S.fence([R_q, R_k, R_v, R_yatt, R_yconv], [R_aT])
            allh2 = R_h2T
            for g in range(6):
                s1, s3 = chunk(14 + 2 * g, 2)
                for m4 in range(4 if g < 5 else 2):
                    m = g * 4 + m4
                    for (t0, n) in cb:
                        pt_ = flip(); tmpA, tmpB, R_tmpA, R_tmpB = tmpA2[pt_], tmpB2[pt_], R_tmpA2[pt_], R_tmpB2[pt_]
                        (b1,) = projA(s1, m4, h2T, allh2, [(t0, n)], 8)
                        (b3,) = projA(s3, m4, h2T, allh2, [(t0, n)], 8)
                        S.op("scalar", lambda e, tmpA=tmpA, b=b1: e.activation(out=tmpA[:, :], in_=ps[b][:, :], func=AF.Silu),
                             reads=[R_ps[b1]], writes=[R_tmpA])
                        S.op("vector", lambda e, tmpA=tmpA, b=b3, m=m, t0=t0, n=n: e.tensor_tensor(out=aT[:, m, t0:t0 + n], in0=tmpA[:, 0:n], in1=ps[b][:, 0:n], op=ALU.mult),
                             reads=[R_tmpA, R_ps[b3]], writes=[R_aT], partial=True)

            _stage(sbi * 10 + 7)
            S.fence(R_tmpA2 + R_tmpB2, R_Sb2)
            for nh in range(2):
                sl = chunk(26 + nh * 3, 3, disable_ctx_slot=(nh == 1 and sbi < 2))
                for t in range(8):
                    b = nb()
                    for kc in range(22):
                        S.op("tensor", lambda e, b=b, kc=kc, t=t, sl=sl: e.matmul(ps[b][:, :], lhsT=aT[:, kc, t * 128:(t + 1) * 128],
                                                                          rhs=ring[sl[kc // 8]][:, kc % 8, :], start=(kc == 0), stop=(kc == 21)),
                             reads=[R_aT, R_ring[sl[kc // 8]]], writes=[R_ps[b]], partial=True)
                    if nh == 0:
                        S.op("vector", lambda e, b=b, t=t: e.tensor_copy(y0ap(t), ps[b][:, :]), reads=[R_ps[b]], writes=[R_y0[t]], partial=True)
                        pj = flip()
                        S.op("scalar", lambda e, t=t, pj=pj: e.activation(out=Pb2[pj][:, 0:512], in_=y0ap(t), func=AF.Square, accum_out=ss0[:, t:t + 1]),
                             reads=[R_y0[t]], writes=[R_Pb2[pj]], pwrites=[R_ss0])
                    else:
                        oi = nxt("ost")
                        post_norm_residual([None, b], (y0ap(t), R_y0[t], ss0[:, t:t + 1]), None, R_x1[t], GG[1], ost[oi][:], R_ost[oi],
                                           xres=x1buf[:, t, :])
                        S.dma("sync", ydst[yrow0 + t * 128: yrow0 + (t + 1) * 128, :], ost[oi][:], key=R_ost[oi], reads=[R_ost[oi]])

        def post_norm_residual(bk, half0, xres_t, R_xres, GGt, dst, R_dst, xres=None):
            p = flip()
            junk, R_junk, stat, R_stat = Pb2[p][:, 0:1024], R_Pb2[p], stat2[p], R_stat2[p]
            if xres is None:
                xres = xres_t[:]
            if bk[0] is not None:
                S.op("scalar", lambda e: e.activation(out=junk[:, 0:512], in_=ps[bk[0]][:, :], func=AF.Square, accum_out=stat[:, 8:9]),
                     reads=[R_ps[bk[0]]], writes=[R_junk, R_stat])
                s0 = stat[:, 8:9]
                rs0 = [R_stat]
            else:
                s0 = half0[2]
                rs0 = [R_ss0]
            S.op("scalar", lambda e: e.activation(out=junk[:, 512:1024], in_=ps[bk[1]][:, :], func=AF.Square, accum_out=stat[:, 9:10]),
                 reads=[R_ps[bk[1]]], writes=[R_junk, R_stat])
            S.op("vector", lambda e: e.tensor_tensor(out=stat[:, 10:11], in0=s0, in1=stat[:, 9:10], op=ALU.add), reads=[R_stat] + rs0, writes=[R_stat])
            S.op("scalar", lambda e: e.activation(out=stat[:, 11:12], in_=stat[:, 10:11], func=AF.Ln, scale=1.0 / D, bias=EPS), reads=[R_stat], writes=[R_stat])
            S.op("scalar", lambda e: e.activation(out=stat[:, 12:13], in_=stat[:, 11:12], func=AF.Exp, scale=-0.5), reads=[R_stat], writes=[R_stat])
            for nh in range(2):
                if bk[nh] is not None:
                    src, rsrc = ps[bk[nh]][:, :], [R_ps[bk[nh]]]
                else:
                    src, rsrc = half0[0], [half0[1]]
                lo, hi = nh * 512, (nh + 1) * 512
                S.op("vector", lambda e, src=src, lo=lo, hi=hi: e.scalar_tensor_tensor(out=dst[:, lo:hi], in0=src, scalar=stat[:, 12:13], in1=GGt[:, lo:hi],
                                                                                     op0=ALU.mult, op1=ALU.mult),
                     reads=rsrc + [R_stat, R_GG], writes=[R_dst], partial=(nh == 1))
            S.op("gpsimd", lambda e: e.tensor_tensor(out=dst, in0=dst, in1=xres, op=ALU.add), reads=[R_dst, R_xres], writes=[R_dst])

        def attention_unit_s(qi, hh, segs, pv):
            pv2 = []
            for i, item in enumerate(pv):
                if item is None:
                    continue
                pv2.append((i,) + item)
            attention_core(qi, hh, segs, pv2, [R_k, R_v, R_ctx])

        att_units = []
        att_ctr = {"n": 0}

        def attention_core(qi, hh, segs, pv, R_kv):
            u = att_ctr["n"]
            att_ctr["n"] += 1
            hp, ho = hh // 2, (hh % 2) * 64
            Sb, R_Sb = Sb2[u % 2], R_Sb2[u % 2]
            Pb, R_Pb = Pb2[u % 3], R_Pb2[u % 3]
            PTs, R_PTs = PTs2[u % 2], R_PTs2[u % 2]
            stat, R_stat = stat2[u % 5], R_stat2[u % 5]
            qap = qT[ho:ho + 64, hp, qi * 128:(qi + 1) * 128]
            ncol = sum(n for (_, n, _) in segs)

            def stA():
                col = 0
                first = True
                for (rhs, n, bias) in segs:
                    off = 0
                    while off < n:
                        w = min(512, n - off)
                        b = nb()
                        S.op("tensor", lambda e, b=b, rhs=rhs, off=off, w=w: e.matmul(ps[b][:, 0:w], lhsT=qap, rhs=rhs[:, off:off + w], start=True, stop=True),
                             reads=[R_q] + R_kv, writes=[R_ps[b]])
                        c = col + off
                        if bias is None:
                            S.op("scalar", lambda e, b=b, c=c, w=w: e.copy(Sb[:, c:c + w], ps[b][:, 0:w]), reads=[R_ps[b]], writes=[R_Sb], partial=not first)
                            first = False
                        else:
                            blk0 = bias + off // 64

                            def badd(p0, p1, c0, c1, fp, b=b, c=c, blk0=blk0):
                                S.op("vector", lambda e: e.tensor_tensor(
                                    out=Sb[p0:p1, c + c0:c + c1], in0=ps[b][p0:p1, c0:c1],
                                    in1=BT[p0:p1, hh, blk0 + c0 // 64:blk0 + c1 // 64, :].rearrange("p r w -> p (r w)"), op=ALU.add),
                                    reads=[R_ps[b], R_BT], writes=[R_Sb], partial=not fp)
                            if n == 576 and off == 0:
                                badd(0, 128, 64, 512, first)
                                first = False
                                badd(0, 64, 0, 64, False)
                                S.op("vector", lambda e, c=c: e.memset(Sb[64:128, c:c + 64], NEG), writes=[R_Sb], partial=True)
                            elif n == 576:
                                badd(64, 128, 0, 64, first)
                                first = False
                                S.op("vector", lambda e, c=c: e.memset(Sb[0:64, c:c + 64], NEG), writes=[R_Sb], partial=True)
                            else:
                                badd(0, 128, 0, w, first)
                                first = False
                        off += w
                    col += n

            def stB():
                S.op("vector", lambda e: e.reduce_max(stat[:, 4:5], Sb[:, 0:ncol], axis=AX.X), reads=[R_Sb], writes=[R_stat])
                S.op("vector", lambda e: e.tensor_scalar(stat[:, 6:7], stat[:, 4:5], -1.0, None, ALU.mult), reads=[R_stat], writes=[R_stat])
                S.op("scalar", lambda e: e.activation(out=Pb[:, 0:ncol], in_=Sb[:, 0:ncol], func=AF.Exp, bias=stat[:, 6:7], scale=1.0, accum_out=stat[:, 5:6]),
                     reads=[R_Sb, R_stat], writes=[R_Pb, R_stat])

            def stC1():
                S.op("vector", lambda e: e.reciprocal(stat[:, 7:8], stat[:, 5:6]), reads=[R_stat], writes=[R_stat])
                if ncol > 600:
                    nsplit = ncol - 512
                    S.op("vector", lambda e: e.tensor_scalar(Pb[:, 0:nsplit], Pb[:, 0:nsplit], stat[:, 7:8], None, ALU.mult),
                         reads=[R_stat], pwrites=[R_Pb])
                    S.op("scalar", lambda e: e.activation(out=Pb[:, nsplit:ncol], in_=Pb[:, nsplit:ncol], func=AF.Copy, scale=stat[:, 7:8]),
                         reads=[R_stat], pwrites=[R_Pb])
                else:
                    S.op("scalar", lambda e: e.activation(out=Pb[:, 0:ncol], in_=Pb[:, 0:ncol], func=AF.Copy, scale=stat[:, 7:8]), reads=[R_Pb, R_stat], writes=[R_Pb])

            used0 = [p for p in pv if p[0] < 5]
            used1 = [p for p in pv if p[0] >= 5]

            def stC2():
                for (i, c0, K, _) in pv:
                    t = 0 if i < 5 else 1
                    sl = i if i < 5 else i - 5
                    S.op("tensor", lambda e, t=t, sl=sl, c0=c0, K=K: e.transpose(pst[t][0:K, sl, :], Pb[:, c0:c0 + K], ident[:]),
                         reads=[R_Pb, R_const], writes=[R_pst[t]], partial=True)
                n0 = max(p[0] for p in used0) + 1
                S.op("vector", lambda e: e.tensor_copy(PTs[:, 0:n0, :], pst[0][:, 0:n0, :]), reads=[R_pst[0]], writes=[R_PTs])
                if used1:
                    n1 = max(p[0] for p in used1) - 4
                    S.op("scalar", lambda e: e.copy(PTs[:, 5:5 + n1, :], pst[1][:, 0:n1, :]), reads=[R_pst[1]], writes=[R_PTs], partial=True)

            def stD():
                ob = nb()
                for j, (i, c0, K, vap) in enumerate(pv):
                    S.op("tensor", lambda e, j=j, i=i, K=K, vap=vap: e.matmul(ps[ob][:, 0:128], lhsT=vap, rhs=PTs[0:K, i, :],
                                                                             start=(j == 0), stop=(j == len(pv) - 1)),
                         reads=[R_PTs] + R_kv, writes=[R_ps[ob]], partial=True)
                S.op("scalar", lambda e: e.copy(yattT[ho:ho + 64, hp, qi * 128:(qi + 1) * 128], ps[ob][ho:ho + 64, 0:128]),
                     reads=[R_ps[ob]], writes=[R_yatt], partial=True)

            att_units.append((stA, stB, stC1, stC2, stD))

        def att_flush(side=None):
            n = len(att_units)
            for j in range(n + 4):
                for st in range(5):
                    u = j - st
                    if 0 <= u < n:
                        att_units[u][st]()
                if side is not None and j % 8 == 7:
                    next(side, None)
            if side is not None:
                for _ in side:
                    pass
            att_units.clear()

        def attention_unit(qi, hh, segs, pv, R_kv):
            attention_core(qi, hh, segs, [(i,) + p for i, p in enumerate(pv)], R_kv)

        def load_ctx():
            ctmp = ost[0][:, :].bitcast(BF16).rearrange("p (c f) -> p c f", c=4)
            for c in range(4):
                S.dma("gpsimd", ctmp[:, c, :].rearrange("p (h d) -> p h d", h=8), ck[:, c * 128:(c + 1) * 128, :].rearrange("h p d -> p h d"),
                      key=R_ctmpk, writes=[R_ost[0]])
                S.dma("gpsimd", vctx[:, c, :].rearrange("p (h d) -> p h d", h=8), cv[:, c * 128:(c + 1) * 128, :].rearrange("h p d -> p h d"),
                      key=R_ctx, writes=[R_ctx])
            for hp in range(4):
                t = nxt("pst")
                for c in range(4):
                    S.op("tensor", lambda e, t=t, c=c, hp=hp: e.transpose(pst[t][:, c, :], ctmp[:, c, hp * 128:(hp + 1) * 128], ident[:]),
                         reads=[R_ost[0], R_const], writes=[R_pst[t]], partial=True)
                S.op("vector", lambda e, t=t, hp=hp: e.tensor_copy(kctxT[:, hp, :].rearrange("p (c k) -> p c k", c=4), pst[t][:, 0:4, :]),
                     reads=[R_pst[t]], writes=[R_ctx], partial=True)

        R_btimg = S.region("btimg")

        def build_btimg():
            negt = Sb2[0]
            S.op("vector", lambda e: e.memset(negt[0:64, 0:960], NEG), writes=[R_Sb2[0]])
            for k in range(8):
                S.dma("sync", btimg[:, k * 960:(k + 1) * 960], negt[0:64, 0:960], key=R_btimg, reads=[R_Sb2[0]], writes=[R_btimg])
            img_h = btimg.tensor
            dst = bass.AP(tensor=img_h, offset=8 * 7680 + 0, ap=[[7680 + 1, 49], [64, 120], [1, 16]])
            src = bass.AP(tensor=rpb_t, offset=7, ap=[[0, 49], [31, 120], [1, 16]])
            S.dma("sync", dst, src, key=R_btimg2, reads=[R_btimg], writes=[R_btimg])
            for wq in list(range(0, 8)) + list(range(57, 64)):
                cc = min(max(wq - 8, 0), 48)
                a = 15 + cc - wq
                dst = bass.AP(tensor=img_h, offset=wq * 7680 + cc, ap=[[64, 120], [1, 16]])
                src = bass.AP(tensor=rpb_t, offset=a, ap=[[31, 120], [1, 16]])
                S.dma("sync", dst, src, key=R_btimg2, reads=[R_btimg], writes=[R_btimg])

        R_btimg2 = S.region("btimg2")

        base = NMOD
        try:
            _stage(-3)
            setup_mod()
            _stage(-2)
            group_pcol(0)
            group_pcol(1)
            for _ in build_GG(0):
                pass
            _stage(0)
            run_sb(0, base)
            _stage(8)
            _stage(9)
            run_sb(1, base + NPER)
            _stage(18)
            run_sb(2, base + 2 * NPER)
        except _Stop:
            pass

        S.emit(nc, st)
    return nc


_NC_CACHE = {}


def kernel(x_prompt, x_sample, cache_k, cache_v, c, c_ctx, w_mod, b_mod, g_pre_mix, g_post_mix, g_pre_ffn, g_post_ffn,
           w_in, w_conv, w_br_conv, w_br_attn, rpb, w_gate, w_o, w_ff1, w_ff3, w_ff2):
    f = lambda a: np.ascontiguousarray(np.asarray(a, dtype=np.float32))
    x_prompt, x_sample, cache_k, cache_v, c, c_ctx = map(f, (x_prompt, x_sample, cache_k, cache_v, c, c_ctx))
    if "nc" not in _NC_CACHE:
        _NC_CACHE["nc"] = build()
    nc = _NC_CACHE["nc"]

    def col(v):
        return f(v).reshape(8, 128).T

    shared = {
        "bmodcol": f(f(b_mod).reshape(48, 128).T),
        "gcol": f(np.stack([col(g_pre_mix[0]), col(g_post_mix[0]), col(g_pre_ffn[0]), col(g_post_ffn[0])], axis=1)),
        "convw": f(f(w_conv)[0].reshape(3, 4, 128).transpose(2, 1, 0)),
        "rpb": f(rpb)[0],
        "w_mod": f(w_mod)[0], "w_in": f(w_in)[0], "w_brc": f(w_br_conv)[0], "w_bra": f(w_br_attn)[0],
        "w_gate": f(w_gate)[0], "w_o": f(w_o)[0], "w_ff1": f(w_ff1)[0], "w_ff3": f(w_ff3)[0], "w_ff2": f(w_ff2)[0],
    }
    in_maps = []
    for i in range(8):
        m = dict(shared)
        m["xp"] = f(x_prompt[4 * i:4 * i + 4].reshape(1024, D))
        m["xs"] = f(x_sample[i])
        m["ck"] = f(cache_k[i, 0])
        m["cv"] = f(cache_v[i, 0])
        m["c2col"] = f(np.stack([col(c_ctx), col(c[i])], axis=2))
        in_maps.append(m)
    res = run_bass_kernel_spmd(nc, in_maps, core_ids=list(range(8)))
    r = res.results
    y_p = np.concatenate([r[i]["yp"].reshape(4, 256, D) for i in range(8)], axis=0)
    y_s = np.stack([r[i]["ys"] for i in range(8)], axis=0)
    n_k = np.concatenate([r[i]["nk"] for i in range(8)], axis=0)[:, None]
    n_v = np.concatenate([r[i]["nv"] for i in range(8)], axis=0)[:, None]
    return (y_p.astype(np.float32), y_s.astype(np.float32), n_k.astype(np.float32), n_v.astype(np.float32))
```

```python
from contextlib import ExitStack
import numpy as np
import concourse.bass as bass
import concourse.mybir as mybir
from concourse.bass_utils import run_bass_kernel_spmd

F32 = mybir.dt.float32
BF16 = mybir.dt.bfloat16
AF = mybir.ActivationFunctionType
ALU = mybir.AluOpType
AX = mybir.AxisListType

SAME_ENGINE_SYNC = True
D = 1024
DFF = 2816
EPS = 1e-6
NEG = -1e30
NSLOT = 4
NTP = 9
STAGE = 999


class _Stop(Exception):
    pass


def _stage(n):
    if STAGE in (-11, -12, -13):
        if n >= 0:
            raise _Stop()
        return
    if n > STAGE:
        raise _Stop()


class Region:
    __slots__ = ("name", "writers", "readers", "war", "dma_n", "sem", "full")

    def __init__(self, name):
        self.name = name
        self.full = []
        self.writers = []
        self.readers = []
        self.war = []
        self.dma_n = 0
        self.sem = None


class Op:
    __slots__ = ("eng", "fn", "deps", "needs_inc", "val", "is_dma", "dma_reg", "dma_val")

    def __init__(self, eng, fn, is_dma=False):
        self.eng = eng
        self.fn = fn
        self.deps = []
        self.needs_inc = False
        self.val = None
        self.is_dma = is_dma
        self.dma_reg = None
        self.dma_val = None


COMPUTE = ("tensor", "vector", "scalar", "gpsimd")
QUEUES = ("tensor", "vector", "scalar", "gpsimd", "sync")


class Sched:
    def __init__(self):
        self.ops = {q: [] for q in QUEUES}
        self.dma_regions = []

    def region(self, name):
        return Region(name)

    def fence(self, olds, news):
        ops = []
        for r in olds:
            ops += r.readers + r.writers
        for r in news:
            r.readers = list(r.readers) + ops

    def _track(self, op, reads, writes, partial, pwrites=()):
        deps = []
        for r in reads:
            deps.extend(r.writers)
            r.readers.append(op)
        wl = [(r, partial) for r in writes] + [(r, True) for r in pwrites]
        for r, partial in wl:
            if partial and not r.readers and r.writers:
                deps.extend(r.war)
                deps.extend(r.full)
                r.writers.append(op)
            else:
                war = list(r.readers) + list(r.writers)
                deps.extend(war)
                r.war = war
                r.writers = [op]
                r.readers = []
                r.full = [] if partial else [op]
        seen = set()
        for d in deps:
            if d is op or id(d) in seen:
                continue
            seen.add(id(d))
            if d.eng == op.eng and not d.is_dma:
                if op.eng == "tensor" or not SAME_ENGINE_SYNC:
                    continue
            op.deps.append(d)
            if not d.is_dma:
                d.needs_inc = True

    def op(self, eng, fn, reads=(), writes=(), partial=False, pwrites=()):
        o = Op(eng, fn)
        self._track(o, reads, writes, partial, pwrites)
        self.ops[eng].append(o)
        return o

    def dma(self, eng, out, in_, key, reads=(), writes=(), partial=True, **kw):
        def fn(e, out=out, in_=in_, kw=kw):
            return e.dma_start(out=out, in_=in_, **kw)
        o = Op(eng, fn, is_dma=True)
        o.dma_reg = key
        key.dma_n += 1
        o.dma_val = 16 * key.dma_n
        if key not in self.dma_regions:
            self.dma_regions.append(key)
        self._track(o, reads, writes, partial)
        self.ops[eng].append(o)
        return o

    def emit(self, nc, stack):
        esem = {q: stack.enter_context(nc.semaphore(f"e_{q}")) for q in COMPUTE}
        for i, r in enumerate(self.dma_regions):
            r.sem = stack.enter_context(nc.semaphore(f"d{i}_{r.name}"))
        for q in QUEUES:
            c = 0
            for o in self.ops[q]:
                if not o.is_dma and o.needs_inc:
                    c += 1
                    o.val = c
        block = stack.enter_context(nc.Block())
        all_dma = [(r.sem, 16 * r.dma_n) for r in self.dma_regions]

        def make(q):
            def body(e):
                waited = {}
                for o in self.ops[q]:
                    need = {}
                    for d in o.deps:
                        if d.is_dma:
                            sem, v = d.dma_reg.sem, d.dma_val
                        else:
                            sem, v = esem[d.eng], d.val
                        k = id(sem)
                        if k not in need or need[k][1] < v:
                            need[k] = (sem, v)
                    for k, (sem, v) in need.items():
                        if waited.get(k, 0) >= v:
                            continue
                        waited[k] = v
                        e.wait_ge(sem, v)
                    ins = o.fn(e)
                    if o.is_dma:
                        ins.then_inc(o.dma_reg.sem, 16)
                    elif o.needs_inc:
                        ins.then_inc(esem[q], 1)
                if q == "sync":
                    for sem, v in all_dma:
                        if waited.get(id(sem), 0) < v:
                            e.wait_ge(sem, v)
            return body

        block.tensor(make("tensor"))
        block.vector(make("vector"))
        block.scalar(make("scalar"))
        block.gpsimd(make("gpsimd"))
        block.sync(make("sync"))


def build():
    nc = bass.Bass("TRN2", target_bir_lowering=False)

    def din(name, shape):
        return nc.dram_tensor(name, shape, F32, kind="ExternalInput").ap()

    def dout(name, shape):
        return nc.dram_tensor(name, shape, F32, kind="ExternalOutput").ap()

    xp = din("xp", [1024, D])
    xs = din("xs", [2048, D])
    ck = din("ck", [8, 512, 64])
    cv = din("cv", [8, 512, 64])
    c2col = din("c2col", [128, 8, 2])
    bmodcol = din("bmodcol", [128, 48])
    gcol_d = din("gcol", [128, 4, 8])
    convw_d = din("convw", [128, 4, 3])
    rpb = din("rpb", [8, 15, 31])
    w_mod = din("w_mod", [D, 6 * D])
    w_in = din("w_in", [D, 3072])
    w_brc = din("w_brc", [512, D])
    w_bra = din("w_bra", [512, D])
    w_gate = din("w_gate", [D, 2 * D])
    w_o = din("w_o", [D, D])
    w_ff1 = din("w_ff1", [D, DFF])
    w_ff3 = din("w_ff3", [D, DFF])
    w_ff2 = din("w_ff2", [DFF, D])
    yp = dout("yp", [1024, D])
    ys = dout("ys", [2048, D])
    nk = dout("nk", [4, 8, 256, 64])
    nv = dout("nv", [4, 8, 256, 64])

    btimg_t = nc.dram_tensor("btimg", [64, 7680], F32, kind="Internal")
    btimg = btimg_t.ap()
    rpb_t = rpb.tensor

    S = Sched()
    with ExitStack() as st:
        def sb(name, shape, dt):
            return st.enter_context(nc.sbuf_tensor(name, shape, dt))

        NS = NSLOT + 1
        ring = [sb(f"ring{i}", [128, 8, 512], BF16) for i in range(NS)]
        R_ring = [S.region(f"ring{i}") for i in range(NS)]
        xin = [sb(f"xin{i}", [128, D], F32) for i in range(2)]
        R_xin = [S.region(f"xin{i}") for i in range(2)]
        ost = [sb(f"ost{i}", [128, D], F32) for i in range(2)]
        R_ost = [S.region(f"ost{i}") for i in range(2)]
        GG = [sb(f"GG{i}", [128, D], F32) for i in range(2)]
        R_GG = S.region("GG")
        kctxT = ring[NSLOT][:, 0:4, :]
        vctx = ring[NSLOT][:, 4:8, :]
        R_ctx = R_ring[NSLOT]
        R_ctmpk = S.region("ctmpk")
        Sb2 = [sb(f"Sb{i}", [128, 1088], F32) for i in range(2)]
        R_Sb2 = [S.region(f"Sb{i}") for i in range(2)]
        Pb2 = [sb(f"Pb{i}", [128, 1088], BF16) for i in range(3)]
        R_Pb2 = [S.region(f"Pb{i}") for i in range(3)]
        PTs2 = [sb(f"PTs{i}", [128, 9, 128], BF16) for i in range(2)]
        R_PTs2 = [S.region(f"PTs{i}") for i in range(2)]
        stat2 = [sb(f"stat{i}", [128, 16], F32) for i in range(5)]
        R_stat2 = [S.region(f"stat{i}") for i in range(5)]
        tmpA2 = [Sb2[i][:, 0:512] for i in range(2)]
        tmpB2 = [Sb2[i][:, 512:1024] for i in range(2)]
        R_tmpA2 = [S.region(f"tmpA{i}") for i in range(2)]
        R_tmpB2 = [S.region(f"tmpB{i}") for i in range(2)]
        par = {"v": 0}

        def flip():
            par["v"] ^= 1
            return par["v"]

        ident = sb("ident", [128, 128], BF16)
        identf = sb("identf", [128, 128], F32)
        diag = sb("diag", [128, 128], F32)
        dhi = sb("dhi", [128, 128], BF16)
        dlo = sb("dlo", [128, 128], BF16)
        onesb = sb("onesb", [128, 128], BF16)
        R_dhi = S.region("dhi")
        R_dlo = S.region("dlo")
        R_const = S.region("const")
        R_diag = S.region("diag")
        cT = sb("cT", [128, 8, 2], F32)
        scT = sb("scT", [128, 8, 2], BF16)
        bmc = sb("bmc", [128, 48], F32)
        modcol = sb("modcol", [128, 48, 2], F32)
        gcol = sb("gcolsb", [128, 4, 8], F32)
        convw = sb("convwsb", [128, 4, 3], F32)
        pcol = sb("pcol", [128, 12, 8], F32)
        R_mod = S.region("mod")
        R_pcol = S.region("pcol")
        ss0 = sb("ss0", [128, 8], F32)
        R_ss0 = S.region("ss0")

        FA = sb("FA", [128, 8256], F32)
        x1buf = FA[:, 0:8192].rearrange("p (t d) -> p t d", t=8)
        BT = FA[:, 0:7680].rearrange("p (h r w) -> p h r w", h=8, r=15)
        BTf_top = FA[:, 0:7680].rearrange("p (n w) -> p n w", w=64)
        BTf_bot = FA[:, 64:64 + 7680].rearrange("p (n w) -> p n w", w=64)
        hT = sb("HT", [128, 8, 1280], BF16)
        h2T = hT[:, :, 0:1024]
        BU = sb("BU", [128, 9232], BF16)
        UW = 1284
        bT = BU[:, 0:4096].rearrange("p (c t) -> p c t", c=4)
        uT = BU[:, 4096:4096 + 4 * UW].rearrange("p (c t) -> p c t", c=4)
        mgT = BU[:, 0:8192].rearrange("p (k t) -> p k t", k=8)
        ATb = sb("ATb", [128, 22528], BF16)
        aT = ATb[:, :].rearrange("p (k t) -> p k t", k=22)
        qT = ATb[:, 0:4096].rearrange("p (c t) -> p c t", c=4)
        kT = ATb[:, 4096:9216].rearrange("p (c t) -> p c t", c=4)
        vv = ATb[:, 9216:14336].rearrange("p (t f) -> p t f", t=10)
        yattT = ATb[:, 14336:18432].rearrange("p (c t) -> p c t", c=4)
        yconvT = ATb[:, 18432:22528].rearrange("p (c t) -> p c t", c=4)

        def y0ap(t):
            if t < 4:
                return xin[t // 2][:, (t % 2) * 512:(t % 2 + 1) * 512]
            return Sb2[(t - 4) // 2][:, ((t - 4) % 2) * 512:((t - 4) % 2 + 1) * 512]

        R_x1 = [S.region(f"x1_{i}") for i in range(8)]
        R_BT = S.region("BT")
        R_hT = [S.region(f"hT{i}") for i in range(10)]
        R_h2T = [S.region(f"h2T{i}") for i in range(8)]
        R_y0 = [R_xin[0], R_xin[0], R_xin[1], R_xin[1], R_Sb2[0], R_Sb2[0], R_Sb2[1], R_Sb2[1]]
        R_b, R_u, R_mg = S.region("b"), S.region("u"), S.region("mg")
        R_q, R_k, R_v = S.region("q"), S.region("k"), S.region("v")
        R_yatt, R_yconv, R_aT = S.region("yatt"), S.region("yconv"), S.region("aT")

        ps = [st.enter_context(nc.psum_tensor(f"ps{i}", [128, 512], F32)) for i in range(6)]
        pst = [st.enter_context(nc.psum_tensor(f"pst{i}", [128, 8, 128], BF16)) for i in range(2)]
        R_ps = [S.region(f"ps{i}") for i in range(6)]
        R_pst = [S.region(f"pst{i}") for i in range(2)]
        rr = {"ps": 0, "pst": 0, "xin": 0, "ost": 0}

        def nb():
            i = rr["ps"]
            rr["ps"] = (i + 1) % 6
            return i

        def nxt(key, n=2):
            i = rr[key]
            rr[key] = (i + 1) % n
            return i

        chunks = []
        state = {"issued": 0}
        slot_of = {}
        slot_free = [True] * NS
        slot_enabled = [True] * NS

        def ensure(i, i1=None, disable_ctx_slot=False):
            i1 = i if i1 is None else i1
            for j in list(slot_of):
                if j < i:
                    slot_free[slot_of.pop(j)] = True
            if disable_ctx_slot:
                slot_enabled[NSLOT] = False
            while state["issued"] < len(chunks):
                cand = [q for q in range(NS) if slot_free[q] and slot_enabled[q]]
                if not cand:
                    break
                j = state["issued"]
                q = cand[0]
                slot_free[q] = False
                slot_of[j] = q
                for (k0, k1, f0, f1, src) in chunks[j]:
                    S.dma("gpsimd", ring[q][:, k0:k1, f0:f1], src.rearrange("(kc p) f -> p kc f", p=128),
                          key=R_ring[q], writes=[R_ring[q]])
                state["issued"] += 1
            assert state["issued"] > i1, (i, i1, state["issued"])
            return slot_of[i]

        def std(w, f0, nf=512):
            return [(0, 8, 0, nf, w[:, f0:f0 + nf])]

        def sb_chunks():
            cl = []
            for c in (3, 4, 5, 0, 1, 2):
                cl.append(std(w_in, c * 512))
            for mg in range(2):
                cl.append(std(w_gate, mg * 512))
                cl.append(std(w_gate, 1024 + mg * 512))
                cl.append([(0, 4, 0, 512, w_brc[:, mg * 512:(mg + 1) * 512]),
                           (4, 8, 0, 512, w_bra[:, mg * 512:(mg + 1) * 512])])
            cl.append(std(w_o, 0))
            cl.append(std(w_o, 512))
            for g in range(6):
                nf = 512 if g < 5 else 256
                cl.append(std(w_ff1, g * 512, nf))
                cl.append(std(w_ff3, g * 512, nf))
            for nh in range(2):
                for (r0, r1) in ((0, 1024), (1024, 2048), (2048, 2816)):
                    cl.append([(0, (r1 - r0) // 128, 0, 512, w_ff2[r0:r1, nh * 512:(nh + 1) * 512])])
            return cl

        for j in range(12):
            chunks.append(std(w_mod, j * 512))
        NMOD = 12
        per_sb = sb_chunks()
        NPER = len(per_sb)
        for _ in range(3):
            chunks.extend(per_sb)

        S.op("gpsimd", lambda e: e.memset(identf[:], 0.0), writes=[R_const])
        S.op("gpsimd", lambda e: e.affine_select(out=identf[:], in_=identf[:], compare_op=ALU.not_equal, fill=1.0,
                                                 base=0, pattern=[[-1, 128]], channel_multiplier=1),
             reads=[R_const], writes=[R_const])
        S.op("vector", lambda e: e.tensor_copy(ident[:], identf[:]), reads=[R_const], writes=[R_const])
        S.op("vector", lambda e: e.memset(onesb[:], 1.0), reads=[R_const], writes=[R_const])
        S.dma("sync", cT[:], c2col[:, :, :], key=R_mod, writes=[R_mod])
        S.dma("sync", bmc[:], bmodcol[:, :], key=R_mod, writes=[R_mod])
        S.dma("sync", gcol[:], gcol_d[:, :, :], key=R_mod, writes=[R_mod])
        S.dma("sync", convw[:], convw_d[:, :, :], key=R_mod, writes=[R_mod])
        S.op("scalar", lambda e: e.activation(out=scT[:], in_=cT[:], func=AF.Silu), reads=[R_mod], writes=[R_mod])
        def setup_mod():
            mp = nb()
            modps = ps[mp][:, 0:96].rearrange("p (j g) -> p j g", g=2)
            for j in range(NMOD):
                s = ensure(j)
                for m4 in range(4):
                    for kc in range(8):
                        S.op("tensor", lambda e, s=s, m4=m4, kc=kc, j=j: e.matmul(
                            modps[:, j * 4 + m4, :], lhsT=ring[s][:, kc, m4 * 128:(m4 + 1) * 128], rhs=scT[:, kc, :],
                            start=(kc == 0), stop=(kc == 7)),
                            reads=[R_ring[s], R_mod], writes=[R_ps[mp]], partial=True)
            for g in range(2):
                S.op("vector", lambda e, g=g: e.tensor_tensor(out=modcol[:, :, g], in0=modps[:, :, g], in1=bmc[:, :], op=ALU.add),
                     reads=[R_ps[mp], R_mod], writes=[R_mod], partial=True)

        def group_pcol(g):
            def mc(i):
                return modcol[:, i * 8:(i + 1) * 8, g]
            V = "vector"
            o = g * 6
            S.op(V, lambda e: e.scalar_tensor_tensor(out=pcol[:, o + 0, :], in0=mc(1), scalar=1.0, in1=gcol[:, 0, :], op0=ALU.add, op1=ALU.mult),
                 reads=[R_mod], writes=[R_pcol], partial=True)
            S.op(V, lambda e: e.tensor_copy(pcol[:, o + 1, :], mc(0)), reads=[R_mod], writes=[R_pcol], partial=True)
            S.op(V, lambda e: e.scalar_tensor_tensor(out=pcol[:, o + 2, :], in0=mc(4), scalar=1.0, in1=gcol[:, 2, :], op0=ALU.add, op1=ALU.mult),
                 reads=[R_mod], writes=[R_pcol], partial=True)
            S.op(V, lambda e: e.tensor_copy(pcol[:, o + 3, :], mc(3)), reads=[R_mod], writes=[R_pcol], partial=True)
            S.op(V, lambda e: e.tensor_tensor(out=pcol[:, o + 4, :], in0=mc(2), in1=gcol[:, 1, :], op=ALU.mult),
                 reads=[R_mod], writes=[R_pcol], partial=True)
            S.op(V, lambda e: e.tensor_tensor(out=pcol[:, o + 5, :], in0=mc(5), in1=gcol[:, 3, :], op=ALU.mult),
                 reads=[R_mod], writes=[R_pcol], partial=True)

        def build_GG(g):
            V = "vector"
            first = True
            for gi in range(2):
                for half in range(2):
                    b = nb()
                    for k4 in range(4):
                        kc = half * 4 + k4
                        S.op(V, lambda e, kc=kc, gi=gi: e.tensor_scalar(diag[:], identf[:], pcol[:, g * 6 + 4 + gi, kc:kc + 1], None, ALU.mult),
                             reads=[R_pcol, R_const], writes=[R_diag])
                        S.op(V, lambda e: e.tensor_copy(dhi[:], diag[:]), reads=[R_diag], writes=[R_dhi])
                        S.op(V, lambda e: e.tensor_tensor(out=diag[:], in0=diag[:], in1=dhi[:], op=ALU.subtract), reads=[R_diag, R_dhi], writes=[R_diag])
                        S.op(V, lambda e: e.tensor_copy(dlo[:], diag[:]), reads=[R_diag], writes=[R_dlo])
                        S.op("tensor", lambda e, b=b, k4=k4: e.matmul(ps[b][:, k4 * 128:(k4 + 1) * 128], lhsT=onesb[:], rhs=dhi[:],
                                                                       start=True, stop=False),
                             reads=[R_dhi, R_const], writes=[R_ps[b]], partial=True)
                        S.op("tensor", lambda e, b=b, k4=k4: e.matmul(ps[b][:, k4 * 128:(k4 + 1) * 128], lhsT=onesb[:], rhs=dlo[:],
                                                                       start=False, stop=True),
                             reads=[R_dlo, R_const], writes=[R_ps[b]], partial=True)
                    S.op("vector", lambda e, b=b, gi=gi, half=half: e.tensor_copy(GG[gi][:, half * 512:(half + 1) * 512], ps[b][:, :]),
                         reads=[R_ps[b]], writes=[R_GG], partial=not first)
                    first = False
                    yield

        nt_par = {"v": 0}

        def norm_transpose(src_ap, R_src, dstT, R_dst, col0, pa, ps_, defer=False):
            nt_par["v"] ^= 1
            p = nt_par["v"]
            junk, R_junk, stat, R_stat = Pb2[p][:, 0:1024], R_Pb2[p], stat2[2 + p], R_stat2[2 + p]
            xsb, R_xsb = PTs2[p][:, 0:8, :].rearrange("p k t -> p (k t)"), R_PTs2[p]
            S.op("scalar", lambda e: e.activation(out=junk[:], in_=src_ap, func=AF.Square, accum_out=stat[:, 0:1]),
                 reads=[R_src], writes=[R_junk, R_stat])
            S.op("scalar", lambda e: e.activation(out=stat[:, 1:2], in_=stat[:, 0:1], func=AF.Ln, scale=1.0 / D, bias=EPS),
                 reads=[R_stat], writes=[R_stat])
            S.op("scalar", lambda e: e.activation(out=stat[:, 2:3], in_=stat[:, 1:2], func=AF.Exp, scale=-0.5),
                 reads=[R_stat], writes=[R_stat])
            if NTP < 2:
                return
            S.op("vector", lambda e: e.tensor_scalar(xsb[:], src_ap, stat[:, 2:3], None, ALU.mult),
                 reads=[R_src, R_stat], writes=[R_xsb])
            if NTP < 3:
                return

            def stage2():
                nt_stage2(dstT, R_dst, col0, pa, ps_, xsb, R_xsb)
            if defer:
                prev = nt_pending["f"]
                nt_pending["f"] = stage2
                if prev is not None:
                    prev()
            else:
                stage2()

        nt_pending = {"f": None}

        def nt_flush():
            if nt_pending["f"] is not None:
                nt_pending["f"]()
                nt_pending["f"] = None

        def nt_stage2(dstT, R_dst, col0, pa, ps_, xsb, R_xsb):
            t = nxt("pst")
            for kc in range(8):
                S.op("tensor", lambda e, kc=kc, t=t: e.transpose(pst[t][:, kc, :], xsb[:, kc * 128:(kc + 1) * 128], ident[:]),
                     reads=[R_xsb, R_const], writes=[R_pst[t]], partial=True)
            if NTP < 4:
                return
            use_act = (col0 // 128) % 2 == 0
            for kc in range(8):
                if use_act:
                    S.op("scalar", lambda e, kc=kc, t=t: e.activation(out=dstT[:, kc, col0:col0 + 128], in_=pst[t][:, kc, :], func=AF.Identity,
                                                                       scale=pcol[:, pa, kc:kc + 1], bias=pcol[:, ps_, kc:kc + 1]),
                         reads=[R_pst[t], R_pcol], writes=[R_dst], partial=True)
                else:
                    S.op("vector", lambda e, kc=kc, t=t: e.tensor_scalar(dstT[:, kc, col0:col0 + 128], pst[t][:, kc, :],
                                                                          pcol[:, pa, kc:kc + 1], pcol[:, ps_, kc:kc + 1], ALU.mult, ALU.add),
                         reads=[R_pst[t], R_pcol], writes=[R_dst], partial=True)

        def projA(slot, m4, src, R_src_list, blocks, nk_, kmap=None):
            banks = [nb() for _ in blocks]
            for kc in range(nk_):
                kk = kc if kmap is None else kmap(kc)
                for bi, (c0, n) in enumerate(blocks):
                    b = banks[bi]
                    S.op("tensor", lambda e, b=b, kk=kk, kc=kc, c0=c0, n=n: e.matmul(
                        ps[b][:, 0:n], lhsT=ring[slot][:, kk, m4 * 128:(m4 + 1) * 128], rhs=src[:, kc, c0:c0 + n],
                        start=(kc == 0), stop=(kc == nk_ - 1)),
                        reads=[R_ring[slot]] + R_src_list, writes=[R_ps[b]], partial=True)
            return banks

        def run_sb(sbi, base):
            prompt = (sbi == 0)
            goff = 0 if prompt else 6
            NTE = 8 if prompt else 10
            c0t = 0 if sbi < 2 else 2
            ext_row0 = 0 if sbi < 2 else 12
            core_row0 = 0 if sbi == 1 else 16
            xsrc = xp if prompt else xs
            xrow0 = 0 if sbi < 2 else 768
            ydst = yp if prompt else ys
            yrow0 = 0 if prompt else (0 if sbi == 1 else 1024)
            NEXT = NTE * 128
            ccol = c0t * 128

            def ucol(t):
                return (t + 1 + 2 * (t // 256)) if prompt else (t + ccol + 1)

            def chunk(i, n=1, **kw):
                ensure(base + i, base + i + n - 1, **kw)
                return [slot_of[base + i + j] for j in range(n)] if n > 1 else slot_of[base + i]

            if not prompt:
                for j in list(slot_of):
                    if j < base:
                        slot_free[slot_of.pop(j)] = True
                assert slot_free[NSLOT] and not slot_enabled[NSLOT]
                load_ctx()
            S.fence([R_h2T[i] for i in range(8)], R_hT)
            S.fence(R_tmpA2 + R_tmpB2, R_Sb2)
            S.fence([R_mg], [R_b, R_u])
            S.fence([R_aT], [R_q, R_k, R_v, R_yatt, R_yconv])

            for i in range(NTE):
                xi = nxt("xin")
                S.dma("sync", xin[xi][:], xsrc[xrow0 + i * 128: xrow0 + (i + 1) * 128, :], key=R_xin[xi], writes=[R_xin[xi]])
                norm_transpose(xin[xi][:], R_xin[xi], hT, R_hT[i], i * 128, goff + 0, goff + 1, defer=True)
            nt_flush()
            if sbi == 0:
                build_btimg()

            _stage(sbi * 10 + 1)
            extblocks = [(0, 512), (512, 512)] + ([(1024, 256)] if not prompt else [])
            coreblocks = [(ccol, 512), (ccol + 512, 512)]
            allhT = R_hT[:NTE]

            s = chunk(0)
            for m4 in range(4):
                banks = projA(s, m4, hT, allhT, coreblocks, 8)
                for bi, b in enumerate(banks):
                    S.op("scalar", lambda e, b=b, bi=bi, m4=m4: e.activation(out=qT[:, m4, bi * 512:(bi + 1) * 512], in_=ps[b][:, :],
                                                                             func=AF.Copy, scale=0.125),
                         reads=[R_ps[b]], writes=[R_q], partial=True)
            s = chunk(1)
            for m4 in range(4):
                banks = projA(s, m4, hT, allhT, extblocks, 8)
                for bi, b in enumerate(banks):
                    c0_, n = extblocks[bi]
                    S.op("vector", lambda e, b=b, c0_=c0_, n=n, m4=m4: e.tensor_copy(kT[:, m4, c0_:c0_ + n], ps[b][:, 0:n]),
                         reads=[R_ps[b]], writes=[R_k], partial=True)
            if prompt:
                for i in range(8):
                    b = nb()
                    for kc in range(8):
                        S.op("tensor", lambda e, b=b, kc=kc, i=i, s=s: e.matmul(ps[b][:, :], lhsT=hT[:, kc, i * 128:(i + 1) * 128], rhs=ring[s][:, kc, :],
                                                                          start=(kc == 0), stop=(kc == 7)),
                             reads=[R_ring[s], R_hT[i]], writes=[R_ps[b]], partial=True)
                    oi = nxt("ost")
                    S.op("scalar", lambda e, b=b, oi=oi: e.copy(ost[oi][:, 0:512], ps[b][:, :]), reads=[R_ps[b]], writes=[R_ost[oi]])
                    S.dma("sync", nk[i // 2].rearrange("h s d -> s h d")[(i % 2) * 128:(i % 2 + 1) * 128],
                          ost[oi][:, 0:512].rearrange("p (h d) -> p h d", h=8), key=R_ost[oi], reads=[R_ost[oi]])
            s = chunk(2)
            for i in range(NTE):
                b = nb()
                for kc in range(8):
                    S.op("tensor", lambda e, b=b, kc=kc, i=i, s=s: e.matmul(ps[b][:, :], lhsT=hT[:, kc, i * 128:(i + 1) * 128], rhs=ring[s][:, kc, :],
                                                                      start=(kc == 0), stop=(kc == 7)),
                         reads=[R_ring[s], R_hT[i]], writes=[R_ps[b]], partial=True)
                if prompt:
                    oi = nxt("ost")
                    S.op("scalar", lambda e, b=b, oi=oi: e.copy(ost[oi][:, 0:512], ps[b][:, :]), reads=[R_ps[b]], writes=[R_ost[oi]])
                    S.op("vector", lambda e, oi=oi, i=i: e.tensor_copy(vv[:, i, :], ost[oi][:, 0:512]), reads=[R_ost[oi]], writes=[R_v], partial=True)
                else:
                    S.op("vector", lambda e, b=b, i=i: e.tensor_copy(vv[:, i, :], ps[b][:, :]), reads=[R_ps[b]], writes=[R_v], partial=True)
                if prompt:
                    S.dma("sync", nv[i // 2].rearrange("h s d -> s h d")[(i % 2) * 128:(i % 2 + 1) * 128],
                          ost[oi][:, 0:512].rearrange("p (h d) -> p h d", h=8), key=R_ost[oi], reads=[R_ost[oi]])

            _stage(sbi * 10 + 2)
            if prompt:
                for qi in range(8):
                    sq = qi // 2
                    for hh in range(8):
                        hp, ho = hh // 2, (hh % 2) * 64
                        segs = [(kT[ho:ho + 64, hp, sq * 256:(sq + 1) * 256], 256, None)]
                        pv = [(j * 128, 128, vv[:, sq * 2 + j, hp * 128:(hp + 1) * 128]) for j in range(2)]
                        attention_unit(qi, hh, segs, pv, [R_k, R_v])
            else:
                S.fence(R_x1, [R_BT])
                S.dma("sync", FA[0:64, 0:7680], btimg[:, :], key=R_BT, reads=[R_btimg], writes=[R_BT])
                S.dma("sync", FA[64:128, 64:64 + 7680], btimg[:, :], key=R_BT, reads=[R_btimg], writes=[R_BT])
                for qi in range(8):
                    Rr = core_row0 + 2 * qi
                    r0t = min(max(Rr - 4, 0), 24)
                    r0b = min(max(Rr + 1 - 4, 0), 24)
                    rot = r0t - Rr + 7
                    nband = 576 if r0b != r0t else 512
                    kc0 = (r0t - ext_row0) * 64
                    vt0 = (r0t - ext_row0) // 2
                    for hh in range(8):
                        hp, ho = hh // 2, (hh % 2) * 64
                        segs = [(kT[ho:ho + 64, hp, kc0:kc0 + nband], nband, rot),
                                (kctxT[ho:ho + 64, hp, :], 512, None)]
                        pv = []
                        for j in range(4):
                            pv.append((j * 128, 128, vv[:, vt0 + j, hp * 128:(hp + 1) * 128]))
                        if nband == 576:
                            pv.append((512, 64, vv[0:64, vt0 + 4, hp * 128:(hp + 1) * 128]))
                        for j in range(4):
                            pv.append((nband + j * 128, 128, vctx[:, j, hp * 128:(hp + 1) * 128]))
                        if nband == 512:
                            pv = pv[:4] + [None] + pv[4:]
                        attention_unit_s(qi, hh, segs, pv)

            _stage(sbi * 10 + 3)
            att_flush(build_GG(1) if sbi == 1 else None)
            slot_enabled[NSLOT] = True
            S.fence(R_Sb2, R_tmpA2 + R_tmpB2)
            s = chunk(3)
            for m4 in range(4):
                banks = projA(s, m4, hT, allhT, coreblocks, 8)
                for bi, b in enumerate(banks):
                    S.op("scalar", lambda e, b=b, bi=bi, m4=m4: e.copy(bT[:, m4, bi * 512:(bi + 1) * 512], ps[b][:, :]),
                         reads=[R_ps[b]], writes=[R_b], partial=True)
            S.op("gpsimd", lambda e: e.memset(uT[:, :, :], 0.0), writes=[R_u])
            s1, s2 = chunk(4, 2)
            ublocks = extblocks
            for m4 in range(4):
                bk1 = projA(s1, m4, hT, allhT, ublocks, 8)
                bk2 = projA(s2, m4, hT, allhT, ublocks, 8)
                for bi, (c0_, n) in enumerate(ublocks):
                    pt_ = flip(); tmpA, tmpB, R_tmpA, R_tmpB = tmpA2[pt_], tmpB2[pt_], R_tmpA2[pt_], R_tmpB2[pt_]
                    S.op("scalar", lambda e, tmpA=tmpA, b=bk1[bi], n=n: e.copy(tmpA[:, 0:n], ps[b][:, 0:n]), reads=[R_ps[bk1[bi]]], writes=[R_tmpA])
                    if prompt:
                        for hs in range(2):
                            t0 = c0_ + hs * 256
                            S.op("vector", lambda e, tmpA=tmpA, b=bk2[bi], hs=hs, t0=t0, m4=m4: e.tensor_tensor(
                                out=uT[:, m4, ucol(t0):ucol(t0) + 256], in0=tmpA[:, hs * 256:(hs + 1) * 256],
                                in1=ps[b][:, hs * 256:(hs + 1) * 256], op=ALU.mult),
                                reads=[R_tmpA, R_ps[bk2[bi]]], writes=[R_u], partial=True)
                    else:
                        S.op("vector", lambda e, tmpA=tmpA, b=bk2[bi], c0_=c0_, n=n, m4=m4: e.tensor_tensor(
                            out=uT[:, m4, c0_ + 1:c0_ + 1 + n], in0=tmpA[:, 0:n], in1=ps[b][:, 0:n], op=ALU.mult),
                            reads=[R_tmpA, R_ps[bk2[bi]]], writes=[R_u], partial=True)

            ranges = [(sq * 256, 256) for sq in range(4)] if prompt else [(0, 512), (512, 512)]
            for m4 in range(4):
                for (t0, n) in ranges:
                    pt_ = flip(); tmpA, tmpB, R_tmpA, R_tmpB = tmpA2[pt_], tmpB2[pt_], R_tmpA2[pt_], R_tmpB2[pt_]
                    u0 = ucol(t0)
                    S.op("vector", lambda e, tmpB=tmpB, m4=m4, u0=u0, n=n: e.tensor_scalar(tmpB[:, 0:n], uT[:, m4, u0:u0 + n], convw[:, m4, 1:2], None, ALU.mult),
                         reads=[R_u], writes=[R_tmpB])
                    S.op("vector", lambda e, tmpB=tmpB, m4=m4, u0=u0, n=n: e.scalar_tensor_tensor(
                        out=tmpB[:, 0:n], in0=uT[:, m4, u0 - 1:u0 - 1 + n], scalar=convw[:, m4, 0:1], in1=tmpB[:, 0:n], op0=ALU.mult, op1=ALU.add),
                        reads=[R_u, R_tmpB], writes=[R_tmpB])
                    S.op("vector", lambda e, tmpB=tmpB, m4=m4, u0=u0, n=n: e.scalar_tensor_tensor(
                        out=tmpB[:, 0:n], in0=uT[:, m4, u0 + 1:u0 + 1 + n], scalar=convw[:, m4, 2:3], in1=tmpB[:, 0:n], op0=ALU.mult, op1=ALU.add),
                        reads=[R_u, R_tmpB], writes=[R_tmpB])
                    S.op("vector", lambda e, tmpB=tmpB, m4=m4, t0=t0, n=n: e.tensor_tensor(out=yconvT[:, m4, t0:t0 + n], in0=tmpB[:, 0:n],
                                                                               in1=bT[:, m4, t0:t0 + n], op=ALU.mult),
                         reads=[R_tmpB, R_b], writes=[R_yconv], partial=True)

            _stage(sbi * 10 + 4)
            S.fence([R_b, R_u], [R_mg])
            cb = [(0, 512), (512, 512)]
            for mg in range(2):
                sgc, sga, sbr = chunk(6 + mg * 3, 3)
                for m4 in range(4):
                    m = mg * 4 + m4
                    for (t0, n) in cb:
                        pt_ = flip(); tmpA, tmpB, R_tmpA, R_tmpB = tmpA2[pt_], tmpB2[pt_], R_tmpA2[pt_], R_tmpB2[pt_]
                        (bgc,) = projA(sgc, m4, hT, allhT, [(ccol + t0, n)], 8)
                        (bbc,) = projA(sbr, m4, yconvT, [R_yconv], [(t0, n)], 4)
                        (bga,) = projA(sga, m4, hT, allhT, [(ccol + t0, n)], 8)
                        (bba,) = projA(sbr, m4, yattT, [R_yatt], [(t0, n)], 4, kmap=lambda kc: kc + 4)
                        S.op("scalar", lambda e, tmpA=tmpA, b=bgc: e.activation(out=tmpA[:, :], in_=ps[b][:, :], func=AF.Sigmoid),
                             reads=[R_ps[bgc]], writes=[R_tmpA])
                        S.op("vector", lambda e, tmpA=tmpA, b=bbc: e.tensor_tensor(out=tmpA[:, :], in0=tmpA[:, :], in1=ps[b][:, :], op=ALU.mult),
                             reads=[R_tmpA, R_ps[bbc]], writes=[R_tmpA])
                        S.op("scalar", lambda e, tmpB=tmpB, b=bga: e.activation(out=tmpB[:, :], in_=ps[b][:, :], func=AF.Sigmoid),
                             reads=[R_ps[bga]], writes=[R_tmpB])
                        S.op("vector", lambda e, tmpB=tmpB, b=bba: e.tensor_tensor(out=tmpB[:, :], in0=tmpB[:, :], in1=ps[b][:, :], op=ALU.mult),
                             reads=[R_tmpB, R_ps[bba]], writes=[R_tmpB])
                        S.op("gpsimd", lambda e, tmpA=tmpA, tmpB=tmpB, m=m, t0=t0, n=n: e.tensor_tensor(out=mgT[:, m, t0:t0 + n], in0=tmpA[:, 0:n], in1=tmpB[:, 0:n], op=ALU.add),
                             reads=[R_tmpA, R_tmpB], writes=[R_mg], partial=True)

            _stage(sbi * 10 + 5)
            S.fence([R_BT], R_x1)
            S.fence(R_hT, R_h2T)
            so = chunk(12, 2)
            def wo_mm(t):
                bk = [nb(), nb()]
                for nh in range(2):
                    for kc in range(8):
                        S.op("tensor", lambda e, nh=nh, kc=kc, t=t, bk=bk: e.matmul(ps[bk[nh]][:, :], lhsT=mgT[:, kc, t * 128:(t + 1) * 128],
                                                                            rhs=ring[so[nh]][:, kc, :], start=(kc == 0), stop=(kc == 7)),
                             reads=[R_mg, R_ring[so[nh]]], writes=[R_ps[bk[nh]]], partial=True)
                return bk
            bks = {0: wo_mm(0)}
            for t in range(8):
                if t + 1 < 8:
                    bks[t + 1] = wo_mm(t + 1)
                bk = bks[t]
                xi = nxt("xin")
                S.dma("sync", xin[xi][:], xsrc[xrow0 + (c0t + t) * 128: xrow0 + (c0t + t + 1) * 128, :], key=R_xin[xi], writes=[R_xin[xi]])
                post_norm_residual(bk, None, xin[xi], R_xin[xi], GG[0], x1buf[:, t, :], R_x1[t])
                if t >= 2:
                    norm_transpose(x1buf[:, t - 2, :], R_x1[t - 2], h2T, R_h2T[t - 2], (t - 2) * 128, goff + 2, goff + 3, defer=True)
            for t in (6, 7):
                norm_transpose(x1buf[:, t, :], R_x1[t], h2T, R_h2T[t], t * 128, goff + 2, goff + 3, defer=True)
            nt_flush()

            _stage(sbi * 10 + 6)
            S.fence([R_q, R_k, R_v, R_yatt, R_yconv], [R_aT])
            allh2 = R_h2T
            for g in range(6):
                s1, s3 = chunk(14 + 2 * g, 2)
                for m4 in range(4 if g < 5 else 2):
                    m = g * 4 + m4
                    for (t0, n) in cb:
                        pt_ = flip(); tmpA, tmpB, R_tmpA, R_tmpB = tmpA2[pt_], tmpB2[pt_], R_tmpA2[pt_], R_tmpB2[pt_]
                        (b1,) = projA(s1, m4, h2T, allh2, [(t0, n)], 8)
                        (b3,) = projA(s3, m4, h2T, allh2, [(t0, n)], 8)
                        S.op("scalar", lambda e, tmpA=tmpA, b=b1: e.activation(out=tmpA[:, :], in_=ps[b][:, :], func=AF.Silu),
                             reads=[R_ps[b1]], writes=[R_tmpA])
                        S.op("vector", lambda e, tmpA=tmpA, b=b3, m=m, t0=t0, n=n: e.tensor_tensor(out=aT[:, m, t0:t0 + n], in0=tmpA[:, 0:n], in1=ps[b][:, 0:n], op=ALU.mult),
                             reads=[R_tmpA, R_ps[b3]], writes=[R_aT], partial=True)

            _stage(sbi * 10 + 7)
            S.fence(R_tmpA2 + R_tmpB2, R_Sb2)
            for nh in range(2):
                sl = chunk(26 + nh * 3, 3, disable_ctx_slot=(nh == 1 and sbi < 2))
                for t in range(8):
                    b = nb()
                    for kc in range(22):
                        S.op("tensor", lambda e, b=b, kc=kc, t=t, sl=sl: e.matmul(ps[b][:, :], lhsT=aT[:, kc, t * 128:(t + 1) * 128],
                                                                          rhs=ring[sl[kc // 8]][:, kc % 8, :], start=(kc == 0), stop=(kc == 21)),
                             reads=[R_aT, R_ring[sl[kc // 8]]], writes=[R_ps[b]], partial=True)
                    if nh == 0:
                        S.op("vector", lambda e, b=b, t=t: e.tensor_copy(y0ap(t), ps[b][:, :]), reads=[R_ps[b]], writes=[R_y0[t]], partial=True)
                        pj = flip()
                        S.op("scalar", lambda e, t=t, pj=pj: e.activation(out=Pb2[pj][:, 0:512], in_=y0ap(t), func=AF.Square, accum_out=ss0[:, t:t + 1]),
                             reads=[R_y0[t]], writes=[R_Pb2[pj]], pwrites=[R_ss0])
                    else:
                        oi = nxt("ost")
                        post_norm_residual([None, b], (y0ap(t), R_y0[t], ss0[:, t:t + 1]), None, R_x1[t], GG[1], ost[oi][:], R_ost[oi],
                                           xres=x1buf[:, t, :])
                        S.dma("sync", ydst[yrow0 + t * 128: yrow0 + (t + 1) * 128, :], ost[oi][:], key=R_ost[oi], reads=[R_ost[oi]])

        def post_norm_residual(bk, half0, xres_t, R_xres, GGt, dst, R_dst, xres=None):
            p = flip()
            junk, R_junk, stat, R_stat = Pb2[p][:, 0:1024], R_Pb2[p], stat2[p], R_stat2[p]
            if xres is None:
                xres = xres_t[:]
            if bk[0] is not None:
                S.op("scalar", lambda e: e.activation(out=junk[:, 0:512], in_=ps[bk[0]][:, :], func=AF.Square, accum_out=stat[:, 8:9]),
                     reads=[R_ps[bk[0]]], writes=[R_junk, R_stat])
                s0 = stat[:, 8:9]
                rs0 = [R_stat]
            else:
                s0 = half0[2]
                rs0 = [R_ss0]
            S.op("scalar", lambda e: e.activation(out=junk[:, 512:1024], in_=ps[bk[1]][:, :], func=AF.Square, accum_out=stat[:, 9:10]),
                 reads=[R_ps[bk[1]]], writes=[R_junk, R_stat])
            S.op("vector", lambda e: e.tensor_tensor(out=stat[:, 10:11], in0=s0, in1=stat[:, 9:10], op=ALU.add), reads=[R_stat] + rs0, writes=[R_stat])
            S.op("scalar", lambda e: e.activation(out=stat[:, 11:12], in_=stat[:, 10:11], func=AF.Ln, scale=1.0 / D, bias=EPS), reads=[R_stat], writes=[R_stat])
            S.op("scalar", lambda e: e.activation(out=stat[:, 12:13], in_=stat[:, 11:12], func=AF.Exp, scale=-0.5), reads=[R_stat], writes=[R_stat])
            for nh in range(2):
                if bk[nh] is not None:
                    src, rsrc = ps[bk[nh]][:, :], [R_ps[bk[nh]]]
                else:
                    src, rsrc = half0[0], [half0[1]]
                lo, hi = nh * 512, (nh + 1) * 512
                S.op("vector", lambda e, src=src, lo=lo, hi=hi: e.scalar_tensor_tensor(out=dst[:, lo:hi], in0=src, scalar=stat[:, 12:13], in1=GGt[:, lo:hi],
                                                                                     op0=ALU.mult, op1=ALU.mult),
                     reads=rsrc + [R_stat, R_GG], writes=[R_dst], partial=(nh == 1))
            S.op("gpsimd", lambda e: e.tensor_tensor(out=dst, in0=dst, in1=xres, op=ALU.add), reads=[R_dst, R_xres], writes=[R_dst])

        def attention_unit_s(qi, hh, segs, pv):
            pv2 = []
            for i, item in enumerate(pv):
                if item is None:
                    continue
                pv2.append((i,) + item)
            attention_core(qi, hh, segs, pv2, [R_k, R_v, R_ctx])

        att_units = []
        att_ctr = {"n": 0}

        def attention_core(qi, hh, segs, pv, R_kv):
            u = att_ctr["n"]
            att_ctr["n"] += 1
            hp, ho = hh // 2, (hh % 2) * 64
            Sb, R_Sb = Sb2[u % 2], R_Sb2[u % 2]
            Pb, R_Pb = Pb2[u % 3], R_Pb2[u % 3]
            PTs, R_PTs = PTs2[u % 2], R_PTs2[u % 2]
            stat, R_stat = stat2[u % 5], R_stat2[u % 5]
            qap = qT[ho:ho + 64, hp, qi * 128:(qi + 1) * 128]
            ncol = sum(n for (_, n, _) in segs)

            def stA():
                col = 0
                first = True
                for (rhs, n, bias) in segs:
                    off = 0
                    while off < n:
                        w = min(512, n - off)
                        b = nb()
                        S.op("tensor", lambda e, b=b, rhs=rhs, off=off, w=w: e.matmul(ps[b][:, 0:w], lhsT=qap, rhs=rhs[:, off:off + w], start=True, stop=True),
                             reads=[R_q] + R_kv, writes=[R_ps[b]])
                        c = col + off
                        if bias is None:
                            S.op("scalar", lambda e, b=b, c=c, w=w: e.copy(Sb[:, c:c + w], ps[b][:, 0:w]), reads=[R_ps[b]], writes=[R_Sb], partial=not first)
                            first = False
                        else:
                            blk0 = bias + off // 64

                            def badd(p0, p1, c0, c1, fp, b=b, c=c, blk0=blk0):
                                S.op("vector", lambda e: e.tensor_tensor(
                                    out=Sb[p0:p1, c + c0:c + c1], in0=ps[b][p0:p1, c0:c1],
                                    in1=BT[p0:p1, hh, blk0 + c0 // 64:blk0 + c1 // 64, :].rearrange("p r w -> p (r w)"), op=ALU.add),
                                    reads=[R_ps[b], R_BT], writes=[R_Sb], partial=not fp)
                            if n == 576 and off == 0:
                                badd(0, 128, 64, 512, first)
                                first = False
                                badd(0, 64, 0, 64, False)
                                S.op("vector", lambda e, c=c: e.memset(Sb[64:128, c:c + 64], NEG), writes=[R_Sb], partial=True)
                            elif n == 576:
                                badd(64, 128, 0, 64, first)
                                first = False
                                S.op("vector", lambda e, c=c: e.memset(Sb[0:64, c:c + 64], NEG), writes=[R_Sb], partial=True)
                            else:
                                badd(0, 128, 0, w, first)
                                first = False
                        off += w
                    col += n

            def stB():
                S.op("vector", lambda e: e.reduce_max(stat[:, 6:7], Sb[:, 0:ncol], axis=AX.X, negate=True), reads=[R_Sb], writes=[R_stat])
                S.op("scalar", lambda e: e.activation(out=Pb[:, 0:ncol], in_=Sb[:, 0:ncol], func=AF.Exp, bias=stat[:, 6:7], scale=1.0, accum_out=stat[:, 5:6]),
                     reads=[R_Sb, R_stat], writes=[R_Pb, R_stat])

            def stC1():
                S.op("vector", lambda e: e.reciprocal(stat[:, 7:8], stat[:, 5:6]), reads=[R_stat], writes=[R_stat])
                if ncol > 600:
                    nsplit = ncol - 512
                    S.op("vector", lambda e: e.tensor_scalar(Pb[:, 0:nsplit], Pb[:, 0:nsplit], stat[:, 7:8], None, ALU.mult),
                         reads=[R_stat], pwrites=[R_Pb])
                    S.op("scalar", lambda e: e.activation(out=Pb[:, nsplit:ncol], in_=Pb[:, nsplit:ncol], func=AF.Copy, scale=stat[:, 7:8]),
                         reads=[R_stat], pwrites=[R_Pb])
                else:
                    S.op("scalar", lambda e: e.activation(out=Pb[:, 0:ncol], in_=Pb[:, 0:ncol], func=AF.Copy, scale=stat[:, 7:8]), reads=[R_Pb, R_stat], writes=[R_Pb])

            used0 = [p for p in pv if p[0] < 5]
            used1 = [p for p in pv if p[0] >= 5]

            def stC2():
                for (i, c0, K, _) in pv:
                    t = 0 if i < 5 else 1
                    sl = i if i < 5 else i - 5
                    S.op("tensor", lambda e, t=t, sl=sl, c0=c0, K=K: e.transpose(pst[t][0:K, sl, :], Pb[:, c0:c0 + K], ident[:]),
                         reads=[R_Pb, R_const], writes=[R_pst[t]], partial=True)
                n0 = max(p[0] for p in used0) + 1
                S.op("vector", lambda e: e.tensor_copy(PTs[:, 0:n0, :], pst[0][:, 0:n0, :]), reads=[R_pst[0]], writes=[R_PTs])
                if used1:
                    n1 = max(p[0] for p in used1) - 4
                    S.op("scalar", lambda e: e.copy(PTs[:, 5:5 + n1, :], pst[1][:, 0:n1, :]), reads=[R_pst[1]], writes=[R_PTs], partial=True)

            def stD():
                ob = nb()
                for j, (i, c0, K, vap) in enumerate(pv):
                    S.op("tensor", lambda e, j=j, i=i, K=K, vap=vap: e.matmul(ps[ob][:, 0:128], lhsT=vap, rhs=PTs[0:K, i, :],
                                                                             start=(j == 0), stop=(j == len(pv) - 1)),
                         reads=[R_PTs] + R_kv, writes=[R_ps[ob]], partial=True)
                S.op("scalar", lambda e: e.copy(yattT[ho:ho + 64, hp, qi * 128:(qi + 1) * 128], ps[ob][ho:ho + 64, 0:128]),
                     reads=[R_ps[ob]], writes=[R_yatt], partial=True)

            att_units.append((stA, stB, stC1, stC2, stD))

        def att_flush(side=None):
            n = len(att_units)
            for j in range(n + 4):
                for st in range(5):
                    u = j - st
                    if 0 <= u < n:
                        att_units[u][st]()
                if side is not None and j % 8 == 7:
                    next(side, None)
            if side is not None:
                for _ in side:
                    pass
            att_units.clear()

        def attention_unit(qi, hh, segs, pv, R_kv):
            attention_core(qi, hh, segs, [(i,) + p for i, p in enumerate(pv)], R_kv)

        def load_ctx():
            ctmp = ost[0][:, :].bitcast(BF16).rearrange("p (c f) -> p c f", c=4)
            for c in range(4):
                S.dma("gpsimd", ctmp[:, c, :].rearrange("p (h d) -> p h d", h=8), ck[:, c * 128:(c + 1) * 128, :].rearrange("h p d -> p h d"),
                      key=R_ctmpk, writes=[R_ost[0]])
                S.dma("gpsimd", vctx[:, c, :].rearrange("p (h d) -> p h d", h=8), cv[:, c * 128:(c + 1) * 128, :].rearrange("h p d -> p h d"),
                      key=R_ctx, writes=[R_ctx])
            for hp in range(4):
                t = nxt("pst")
                for c in range(4):
                    S.op("tensor", lambda e, t=t, c=c, hp=hp: e.transpose(pst[t][:, c, :], ctmp[:, c, hp * 128:(hp + 1) * 128], ident[:]),
                         reads=[R_ost[0], R_const], writes=[R_pst[t]], partial=True)
                S.op("vector", lambda e, t=t, hp=hp: e.tensor_copy(kctxT[:, hp, :].rearrange("p (c k) -> p c k", c=4), pst[t][:, 0:4, :]),
                     reads=[R_pst[t]], writes=[R_ctx], partial=True)

        R_btimg = S.region("btimg")

        def build_btimg():
            negt = Sb2[0]
            S.op("vector", lambda e: e.memset(negt[0:64, 0:960], NEG), writes=[R_Sb2[0]])
            for k in range(8):
                S.dma("sync", btimg[:, k * 960:(k + 1) * 960], negt[0:64, 0:960], key=R_btimg, reads=[R_Sb2[0]], writes=[R_btimg])
            img_h = btimg.tensor
            dst = bass.AP(tensor=img_h, offset=8 * 7680 + 0, ap=[[7680 + 1, 49], [64, 120], [1, 16]])
            src = bass.AP(tensor=rpb_t, offset=7, ap=[[0, 49], [31, 120], [1, 16]])
            S.dma("sync", dst, src, key=R_btimg2, reads=[R_btimg], writes=[R_btimg])
            for wq in list(range(0, 8)) + list(range(57, 64)):
                cc = min(max(wq - 8, 0), 48)
                a = 15 + cc - wq
                dst = bass.AP(tensor=img_h, offset=wq * 7680 + cc, ap=[[64, 120], [1, 16]])
                src = bass.AP(tensor=rpb_t, offset=a, ap=[[31, 120], [1, 16]])
                S.dma("sync", dst, src, key=R_btimg2, reads=[R_btimg], writes=[R_btimg])

        R_btimg2 = S.region("btimg2")

        base = NMOD
        try:
            _stage(-3)
            setup_mod()
            _stage(-2)
            group_pcol(0)
            group_pcol(1)
            for _ in build_GG(0):
                pass
            _stage(0)
            run_sb(0, base)
            _stage(8)
            _stage(9)
            run_sb(1, base + NPER)
            _stage(18)
            run_sb(2, base + 2 * NPER)
        except _Stop:
            pass

        S.emit(nc, st)
    return nc


_NC_CACHE = {}


def kernel(x_prompt, x_sample, cache_k, cache_v, c, c_ctx, w_mod, b_mod, g_pre_mix, g_post_mix, g_pre_ffn, g_post_ffn,
           w_in, w_conv, w_br_conv, w_br_attn, rpb, w_gate, w_o, w_ff1, w_ff3, w_ff2):
    f = lambda a: np.ascontiguousarray(np.asarray(a, dtype=np.float32))
    x_prompt, x_sample, cache_k, cache_v, c, c_ctx = map(f, (x_prompt, x_sample, cache_k, cache_v, c, c_ctx))
    if "nc" not in _NC_CACHE:
        _NC_CACHE["nc"] = build()
    nc = _NC_CACHE["nc"]

    def col(v):
        return f(v).reshape(8, 128).T

    shared = {
        "bmodcol": f(f(b_mod).reshape(48, 128).T),
        "gcol": f(np.stack([col(g_pre_mix[0]), col(g_post_mix[0]), col(g_pre_ffn[0]), col(g_post_ffn[0])], axis=1)),
        "convw": f(f(w_conv)[0].reshape(3, 4, 128).transpose(2, 1, 0)),
        "rpb": f(rpb)[0],
        "w_mod": f(w_mod)[0], "w_in": f(w_in)[0], "w_brc": f(w_br_conv)[0], "w_bra": f(w_br_attn)[0],
        "w_gate": f(w_gate)[0], "w_o": f(w_o)[0], "w_ff1": f(w_ff1)[0], "w_ff3": f(w_ff3)[0], "w_ff2": f(w_ff2)[0],
    }
    in_maps = []
    for i in range(8):
        m = dict(shared)
        m["xp"] = f(x_prompt[4 * i:4 * i + 4].reshape(1024, D))
        m["xs"] = f(x_sample[i])
        m["ck"] = f(cache_k[i, 0])
        m["cv"] = f(cache_v[i, 0])
        m["c2col"] = f(np.stack([col(c_ctx), col(c[i])], axis=2))
        in_maps.append(m)
    res = run_bass_kernel_spmd(nc, in_maps, core_ids=list(range(8)))
    r = res.results
    y_p = np.concatenate([r[i]["yp"].reshape(4, 256, D) for i in range(8)], axis=0)
    y_s = np.stack([r[i]["ys"] for i in range(8)], axis=0)
    n_k = np.concatenate([r[i]["nk"] for i in range(8)], axis=0)[:, None]
    n_v = np.concatenate([r[i]["nv"] for i in range(8)], axis=0)[:, None]
    return (y_p.astype(np.float32), y_s.astype(np.float32), n_k.astype(np.float32), n_v.astype(np.float32))
```

```python
from contextlib import ExitStack
import numpy as np
import concourse.bass as bass
import concourse.mybir as mybir
from concourse.bass_utils import run_bass_kernel_spmd

F32 = mybir.dt.float32
BF16 = mybir.dt.bfloat16
AF = mybir.ActivationFunctionType
ALU = mybir.AluOpType
AX = mybir.AxisListType

SAME_ENGINE_SYNC = True
D = 1024
DFF = 2816
EPS = 1e-6
NEG = -1e30
NSLOT = 4
NTP = 9
STAGE = 999


class _Stop(Exception):
    pass


def _stage(n):
    if STAGE in (-11, -12, -13):
        if n >= 0:
            raise _Stop()
        return
    if n > STAGE:
        raise _Stop()


class Region:
    __slots__ = ("name", "writers", "readers", "war", "dma_n", "sem", "full")

    def __init__(self, name):
        self.name = name
        self.full = []
        self.writers = []
        self.readers = []
        self.war = []
        self.dma_n = 0
        self.sem = None


class Op:
    __slots__ = ("eng", "fn", "deps", "needs_inc", "val", "is_dma", "dma_reg", "dma_val")

    def __init__(self, eng, fn, is_dma=False):
        self.eng = eng
        self.fn = fn
        self.deps = []
        self.needs_inc = False
        self.val = None
        self.is_dma = is_dma
        self.dma_reg = None
        self.dma_val = None


COMPUTE = ("tensor", "vector", "scalar", "gpsimd")
QUEUES = ("tensor", "vector", "scalar", "gpsimd", "sync")


class Sched:
    def __init__(self):
        self.ops = {q: [] for q in QUEUES}
        self.dma_regions = []

    def region(self, name):
        return Region(name)

    def fence(self, olds, news):
        ops = []
        for r in olds:
            ops += r.readers + r.writers
        for r in news:
            r.readers = list(r.readers) + ops

    def _track(self, op, reads, writes, partial, pwrites=()):
        deps = []
        for r in reads:
            deps.extend(r.writers)
            r.readers.append(op)
        wl = [(r, partial) for r in writes] + [(r, True) for r in pwrites]
        for r, partial in wl:
            if partial and not r.readers and r.writers:
                deps.extend(r.war)
                deps.extend(r.full)
                r.writers.append(op)
            else:
                war = list(r.readers) + list(r.writers)
                deps.extend(war)
                r.war = war
                r.writers = [op]
                r.readers = []
                r.full = [] if partial else [op]
        seen = set()
        for d in deps:
            if d is op or id(d) in seen:
                continue
            seen.add(id(d))
            if d.eng == op.eng and not d.is_dma:
                if op.eng == "tensor" or not SAME_ENGINE_SYNC:
                    continue
            op.deps.append(d)
            if not d.is_dma:
                d.needs_inc = True

    def op(self, eng, fn, reads=(), writes=(), partial=False, pwrites=()):
        o = Op(eng, fn)
        self._track(o, reads, writes, partial, pwrites)
        self.ops[eng].append(o)
        return o

    def dma(self, eng, out, in_, key, reads=(), writes=(), partial=True, **kw):
        def fn(e, out=out, in_=in_, kw=kw):
            return e.dma_start(out=out, in_=in_, **kw)
        o = Op(eng, fn, is_dma=True)
        o.dma_reg = key
        key.dma_n += 1
        o.dma_val = 16 * key.dma_n
        if key not in self.dma_regions:
            self.dma_regions.append(key)
        self._track(o, reads, writes, partial)
        self.ops[eng].append(o)
        return o

    def emit(self, nc, stack):
        esem = {q: stack.enter_context(nc.semaphore(f"e_{q}")) for q in COMPUTE}
        for i, r in enumerate(self.dma_regions):
            r.sem = stack.enter_context(nc.semaphore(f"d{i}_{r.name}"))
        for q in QUEUES:
            c = 0
            for o in self.ops[q]:
                if not o.is_dma and o.needs_inc:
                    c += 1
                    o.val = c
        block = stack.enter_context(nc.Block())
        all_dma = [(r.sem, 16 * r.dma_n) for r in self.dma_regions]

        def make(q):
            def body(e):
                waited = {}
                for o in self.ops[q]:
                    need = {}
                    for d in o.deps:
                        if d.is_dma:
                            sem, v = d.dma_reg.sem, d.dma_val
                        else:
                            sem, v = esem[d.eng], d.val
                        k = id(sem)
                        if k not in need or need[k][1] < v:
                            need[k] = (sem, v)
                    for k, (sem, v) in need.items():
                        if waited.get(k, 0) >= v:
                            continue
                        waited[k] = v
                        e.wait_ge(sem, v)
                    ins = o.fn(e)
                    if o.is_dma:
                        ins.then_inc(o.dma_reg.sem, 16)
                    elif o.needs_inc:
                        ins.then_inc(esem[q], 1)
                if q == "sync":
                    for sem, v in all_dma:
                        if waited.get(id(sem), 0) < v:
                            e.wait_ge(sem, v)
            return body

        block.tensor(make("tensor"))
        block.vector(make("vector"))
        block.scalar(make("scalar"))
        block.gpsimd(make("gpsimd"))
        block.sync(make("sync"))


def build():
    nc = bass.Bass("TRN2", target_bir_lowering=False)

    def din(name, shape):
        return nc.dram_tensor(name, shape, F32, kind="ExternalInput").ap()

    def dout(name, shape):
        return nc.dram_tensor(name, shape, F32, kind="ExternalOutput").ap()

    xp = din("xp", [1024, D])
    xs = din("xs", [2048, D])
    ck = din("ck", [8, 512, 64])
    cv = din("cv", [8, 512, 64])
    c2col = din("c2col", [128, 8, 2])
    bmodcol = din("bmodcol", [128, 48])
    gcol_d = din("gcol", [128, 4, 8])
    convw_d = din("convw", [128, 4, 3])
    rpb = din("rpb", [8, 15, 31])
    w_mod = din("w_mod", [D, 6 * D])
    w_in = din("w_in", [D, 3072])
    w_brc = din("w_brc", [512, D])
    w_bra = din("w_bra", [512, D])
    w_gate = din("w_gate", [D, 2 * D])
    w_o = din("w_o", [D, D])
    w_ff1 = din("w_ff1", [D, DFF])
    w_ff3 = din("w_ff3", [D, DFF])
    w_ff2 = din("w_ff2", [DFF, D])
    yp = dout("yp", [1024, D])
    ys = dout("ys", [2048, D])
    nk = dout("nk", [4, 8, 256, 64])
    nv = dout("nv", [4, 8, 256, 64])

    btimg_t = nc.dram_tensor("btimg", [64, 7680], F32, kind="Internal")
    btimg = btimg_t.ap()
    rpb_t = rpb.tensor

    S = Sched()
    with ExitStack() as st:
        def sb(name, shape, dt):
            return st.enter_context(nc.sbuf_tensor(name, shape, dt))

        NS = NSLOT + 1
        ring = [sb(f"ring{i}", [128, 8, 512], BF16) for i in range(NS)]
        R_ring = [S.region(f"ring{i}") for i in range(NS)]
        xin = [sb(f"xin{i}", [128, D], F32) for i in range(2)]
        R_xin = [S.region(f"xin{i}") for i in range(2)]
        ost = [sb(f"ost{i}", [128, D], F32) for i in range(2)]
        R_ost = [S.region(f"ost{i}") for i in range(2)]
        GG = [sb(f"GG{i}", [128, D], F32) for i in range(2)]
        R_GG = S.region("GG")
        kctxT = ring[NSLOT][:, 0:4, :]
        vctx = ring[NSLOT][:, 4:8, :]
        R_ctx = R_ring[NSLOT]
        R_ctmpk = S.region("ctmpk")
        Sb2 = [sb(f"Sb{i}", [128, 1088], F32) for i in range(2)]
        R_Sb2 = [S.region(f"Sb{i}") for i in range(2)]
        Pb2 = [sb(f"Pb{i}", [128, 1088], BF16) for i in range(3)]
        R_Pb2 = [S.region(f"Pb{i}") for i in range(3)]
        PTs2 = [sb(f"PTs{i}", [128, 9, 128], BF16) for i in range(2)]
        R_PTs2 = [S.region(f"PTs{i}") for i in range(2)]
        stat2 = [sb(f"stat{i}", [128, 16], F32) for i in range(5)]
        R_stat2 = [S.region(f"stat{i}") for i in range(5)]
        tmpA2 = [Sb2[i][:, 0:512] for i in range(2)]
        tmpB2 = [Sb2[i][:, 512:1024] for i in range(2)]
        R_tmpA2 = [S.region(f"tmpA{i}") for i in range(2)]
        R_tmpB2 = [S.region(f"tmpB{i}") for i in range(2)]
        par = {"v": 0}

        def flip():
            par["v"] ^= 1
            return par["v"]

        ident = sb("ident", [128, 128], BF16)
        identf = sb("identf", [128, 128], F32)
        diag = sb("diag", [128, 128], F32)
        dhi = sb("dhi", [128, 128], BF16)
        dlo = sb("dlo", [128, 128], BF16)
        onesb = sb("onesb", [128, 128], BF16)
        R_dhi = S.region("dhi")
        R_dlo = S.region("dlo")
        R_const = S.region("const")
        R_diag = S.region("diag")
        cT = sb("cT", [128, 8, 2], F32)
        scT = sb("scT", [128, 8, 2], BF16)
        bmc = sb("bmc", [128, 48], F32)
        modcol = sb("modcol", [128, 48, 2], F32)
        gcol = sb("gcolsb", [128, 4, 8], F32)
        convw = sb("convwsb", [128, 4, 3], F32)
        pcol = sb("pcol", [128, 12, 8], F32)
        R_mod = S.region("mod")
        R_pcol = S.region("pcol")
        ss0 = sb("ss0", [128, 8], F32)
        R_ss0 = S.region("ss0")

        FA = sb("FA", [128, 8256], F32)
        x1buf = FA[:, 0:8192].rearrange("p (t d) -> p t d", t=8)
        BT = FA[:, 0:7680].rearrange("p (h r w) -> p h r w", h=8, r=15)
        BTf_top = FA[:, 0:7680].rearrange("p (n w) -> p n w", w=64)
        BTf_bot = FA[:, 64:64 + 7680].rearrange("p (n w) -> p n w", w=64)
        hT = sb("HT", [128, 8, 1280], BF16)
        h2T = hT[:, :, 0:1024]
        BU = sb("BU", [128, 9232], BF16)
        UW = 1284
        bT = BU[:, 0:4096].rearrange("p (c t) -> p c t", c=4)
        uT = BU[:, 4096:4096 + 4 * UW].rearrange("p (c t) -> p c t", c=4)
        mgT = BU[:, 0:8192].rearrange("p (k t) -> p k t", k=8)
        ATb = sb("ATb", [128, 22528], BF16)
        aT = ATb[:, :].rearrange("p (k t) -> p k t", k=22)
        qT = ATb[:, 0:4096].rearrange("p (c t) -> p c t", c=4)
        kT = ATb[:, 4096:9216].rearrange("p (c t) -> p c t", c=4)
        vv = ATb[:, 9216:14336].rearrange("p (t f) -> p t f", t=10)
        yattT = ATb[:, 14336:18432].rearrange("p (c t) -> p c t", c=4)
        yconvT = ATb[:, 18432:22528].rearrange("p (c t) -> p c t", c=4)

        def y0ap(t):
            if t < 4:
                return xin[t // 2][:, (t % 2) * 512:(t % 2 + 1) * 512]
            return Sb2[(t - 4) // 2][:, ((t - 4) % 2) * 512:((t - 4) % 2 + 1) * 512]

        R_x1 = [S.region(f"x1_{i}") for i in range(8)]
        R_BT = S.region("BT")
        R_hT = [S.region(f"hT{i}") for i in range(10)]
        R_h2T = [S.region(f"h2T{i}") for i in range(8)]
        R_y0 = [R_xin[0], R_xin[0], R_xin[1], R_xin[1], R_Sb2[0], R_Sb2[0], R_Sb2[1], R_Sb2[1]]
        R_b, R_u, R_mg = S.region("b"), S.region("u"), S.region("mg")
        R_q, R_k, R_v = S.region("q"), S.region("k"), S.region("v")
        R_yatt, R_yconv, R_aT = S.region("yatt"), S.region("yconv"), S.region("aT")

        ps = [st.enter_context(nc.psum_tensor(f"ps{i}", [128, 512], F32)) for i in range(6)]
        pst = [st.enter_context(nc.psum_tensor(f"pst{i}", [128, 8, 128], BF16)) for i in range(2)]
        R_ps = [S.region(f"ps{i}") for i in range(6)]
        R_pst = [S.region(f"pst{i}") for i in range(2)]
        rr = {"ps": 0, "pst": 0, "xin": 0, "ost": 0}

        def nb():
            i = rr["ps"]
            rr["ps"] = (i + 1) % 6
            return i

        def nxt(key, n=2):
            i = rr[key]
            rr[key] = (i + 1) % n
            return i

        chunks = []
        state = {"issued": 0}
        slot_of = {}
        slot_free = [True] * NS
        slot_enabled = [True] * NS

        def ensure(i, i1=None, disable_ctx_slot=False):
            i1 = i if i1 is None else i1
            for j in list(slot_of):
                if j < i:
                    slot_free[slot_of.pop(j)] = True
            if disable_ctx_slot:
                slot_enabled[NSLOT] = False
            while state["issued"] < len(chunks):
                cand = [q for q in range(NS) if slot_free[q] and slot_enabled[q]]
                if not cand:
                    break
                j = state["issued"]
                q = cand[0]
                slot_free[q] = False
                slot_of[j] = q
                for (k0, k1, f0, f1, src) in chunks[j]:
                    S.dma("gpsimd", ring[q][:, k0:k1, f0:f1], src.rearrange("(kc p) f -> p kc f", p=128),
                          key=R_ring[q], writes=[R_ring[q]])
                state["issued"] += 1
            assert state["issued"] > i1, (i, i1, state["issued"])
            return slot_of[i]

        def std(w, f0, nf=512):
            return [(0, 8, 0, nf, w[:, f0:f0 + nf])]

        def sb_chunks():
            cl = []
            for c in (3, 4, 5, 0, 1, 2):
                cl.append(std(w_in, c * 512))
            for mg in range(2):
                cl.append(std(w_gate, mg * 512))
                cl.append(std(w_gate, 1024 + mg * 512))
                cl.append([(0, 4, 0, 512, w_brc[:, mg * 512:(mg + 1) * 512]),
                           (4, 8, 0, 512, w_bra[:, mg * 512:(mg + 1) * 512])])
            cl.append(std(w_o, 0))
            cl.append(std(w_o, 512))
            for g in range(6):
                nf = 512 if g < 5 else 256
                cl.append(std(w_ff1, g * 512, nf))
                cl.append(std(w_ff3, g * 512, nf))
            for nh in range(2):
                for (r0, r1) in ((0, 1024), (1024, 2048), (2048, 2816)):
                    cl.append([(0, (r1 - r0) // 128, 0, 512, w_ff2[r0:r1, nh * 512:(nh + 1) * 512])])
            return cl

        for j in range(12):
            chunks.append(std(w_mod, j * 512))
        NMOD = 12
        per_sb = sb_chunks()
        NPER = len(per_sb)
        for _ in range(3):
            chunks.extend(per_sb)

        S.op("gpsimd", lambda e: e.memset(identf[:], 0.0), writes=[R_const])
        S.op("gpsimd", lambda e: e.affine_select(out=identf[:], in_=identf[:], compare_op=ALU.not_equal, fill=1.0,
                                                 base=0, pattern=[[-1, 128]], channel_multiplier=1),
             reads=[R_const], writes=[R_const])
        S.op("vector", lambda e: e.tensor_copy(ident[:], identf[:]), reads=[R_const], writes=[R_const])
        S.op("vector", lambda e: e.memset(onesb[:], 1.0), reads=[R_const], writes=[R_const])
        S.dma("sync", cT[:], c2col[:, :, :], key=R_mod, writes=[R_mod])
        S.dma("sync", bmc[:], bmodcol[:, :], key=R_mod, writes=[R_mod])
        S.dma("sync", gcol[:], gcol_d[:, :, :], key=R_mod, writes=[R_mod])
        S.dma("sync", convw[:], convw_d[:, :, :], key=R_mod, writes=[R_mod])
        S.op("scalar", lambda e: e.activation(out=scT[:], in_=cT[:], func=AF.Silu), reads=[R_mod], writes=[R_mod])
        def setup_mod():
            mp = nb()
            modps = ps[mp][:, 0:96].rearrange("p (j g) -> p j g", g=2)
            for j in range(NMOD):
                s = ensure(j)
                for m4 in range(4):
                    for kc in range(8):
                        S.op("tensor", lambda e, s=s, m4=m4, kc=kc, j=j: e.matmul(
                            modps[:, j * 4 + m4, :], lhsT=ring[s][:, kc, m4 * 128:(m4 + 1) * 128], rhs=scT[:, kc, :],
                            start=(kc == 0), stop=(kc == 7)),
                            reads=[R_ring[s], R_mod], writes=[R_ps[mp]], partial=True)
            for g in range(2):
                S.op("vector", lambda e, g=g: e.tensor_tensor(out=modcol[:, :, g], in0=modps[:, :, g], in1=bmc[:, :], op=ALU.add),
                     reads=[R_ps[mp], R_mod], writes=[R_mod], partial=True)

        def group_pcol(g):
            def mc(i):
                return modcol[:, i * 8:(i + 1) * 8, g]
            V = "vector"
            o = g * 6
            S.op(V, lambda e: e.scalar_tensor_tensor(out=pcol[:, o + 0, :], in0=mc(1), scalar=1.0, in1=gcol[:, 0, :], op0=ALU.add, op1=ALU.mult),
                 reads=[R_mod], writes=[R_pcol], partial=True)
            S.op(V, lambda e: e.tensor_copy(pcol[:, o + 1, :], mc(0)), reads=[R_mod], writes=[R_pcol], partial=True)
            S.op(V, lambda e: e.scalar_tensor_tensor(out=pcol[:, o + 2, :], in0=mc(4), scalar=1.0, in1=gcol[:, 2, :], op0=ALU.add, op1=ALU.mult),
                 reads=[R_mod], writes=[R_pcol], partial=True)
            S.op(V, lambda e: e.tensor_copy(pcol[:, o + 3, :], mc(3)), reads=[R_mod], writes=[R_pcol], partial=True)
            S.op(V, lambda e: e.tensor_tensor(out=pcol[:, o + 4, :], in0=mc(2), in1=gcol[:, 1, :], op=ALU.mult),
                 reads=[R_mod], writes=[R_pcol], partial=True)
            S.op(V, lambda e: e.tensor_tensor(out=pcol[:, o + 5, :], in0=mc(5), in1=gcol[:, 3, :], op=ALU.mult),
                 reads=[R_mod], writes=[R_pcol], partial=True)

        def build_GG(g):
            V = "vector"
            first = True
            for gi in range(2):
                for half in range(2):
                    b = nb()
                    for k4 in range(4):
                        kc = half * 4 + k4
                        S.op(V, lambda e, kc=kc, gi=gi: e.tensor_scalar(diag[:], identf[:], pcol[:, g * 6 + 4 + gi, kc:kc + 1], None, ALU.mult),
                             reads=[R_pcol, R_const], writes=[R_diag])
                        S.op(V, lambda e: e.tensor_copy(dhi[:], diag[:]), reads=[R_diag], writes=[R_dhi])
                        S.op(V, lambda e: e.tensor_tensor(out=diag[:], in0=diag[:], in1=dhi[:], op=ALU.subtract), reads=[R_diag, R_dhi], writes=[R_diag])
                        S.op(V, lambda e: e.tensor_copy(dlo[:], diag[:]), reads=[R_diag], writes=[R_dlo])
                        S.op("tensor", lambda e, b=b, k4=k4: e.matmul(ps[b][:, k4 * 128:(k4 + 1) * 128], lhsT=onesb[:], rhs=dhi[:],
                                                                       start=True, stop=False),
                             reads=[R_dhi, R_const], writes=[R_ps[b]], partial=True)
                        S.op("tensor", lambda e, b=b, k4=k4: e.matmul(ps[b][:, k4 * 128:(k4 + 1) * 128], lhsT=onesb[:], rhs=dlo[:],
                                                                       start=False, stop=True),
                             reads=[R_dlo, R_const], writes=[R_ps[b]], partial=True)
                    S.op("vector", lambda e, b=b, gi=gi, half=half: e.tensor_copy(GG[gi][:, half * 512:(half + 1) * 512], ps[b][:, :]),
                         reads=[R_ps[b]], writes=[R_GG], partial=not first)
                    first = False
                    yield

        nt_par = {"v": 0}

        def norm_transpose(src_ap, R_src, dstT, R_dst, col0, pa, ps_, defer=False):
            nt_par["v"] ^= 1
            p = nt_par["v"]
            junk, R_junk, stat, R_stat = Pb2[p][:, 0:1024], R_Pb2[p], stat2[2 + p], R_stat2[2 + p]
            xsb, R_xsb = PTs2[p][:, 0:8, :].rearrange("p k t -> p (k t)"), R_PTs2[p]
            S.op("scalar", lambda e: e.activation(out=junk[:], in_=src_ap, func=AF.Square, accum_out=stat[:, 0:1]),
                 reads=[R_src], writes=[R_junk, R_stat])
            S.op("scalar", lambda e: e.activation(out=stat[:, 1:2], in_=stat[:, 0:1], func=AF.Ln, scale=1.0 / D, bias=EPS),
                 reads=[R_stat], writes=[R_stat])
            S.op("scalar", lambda e: e.activation(out=stat[:, 2:3], in_=stat[:, 1:2], func=AF.Exp, scale=-0.5),
                 reads=[R_stat], writes=[R_stat])
            if NTP < 2:
                return
            S.op("vector", lambda e: e.tensor_scalar(xsb[:], src_ap, stat[:, 2:3], None, ALU.mult),
                 reads=[R_src, R_stat], writes=[R_xsb])
            if NTP < 3:
                return

            def stage2():
                nt_stage2(dstT, R_dst, col0, pa, ps_, xsb, R_xsb)
            if defer:
                prev = nt_pending["f"]
                nt_pending["f"] = stage2
                if prev is not None:
                    prev()
            else:
                stage2()

        nt_pending = {"f": None}

        def nt_flush():
            if nt_pending["f"] is not None:
                nt_pending["f"]()
                nt_pending["f"] = None

        def nt_stage2(dstT, R_dst, col0, pa, ps_, xsb, R_xsb):
            t = nxt("pst")
            for kc in range(8):
                S.op("tensor", lambda e, kc=kc, t=t: e.transpose(pst[t][:, kc, :], xsb[:, kc * 128:(kc + 1) * 128], ident[:]),
                     reads=[R_xsb, R_const], writes=[R_pst[t]], partial=True)
            if NTP < 4:
                return
            use_act = (col0 // 128) % 2 == 0
            for kc in range(8):
                if use_act:
                    S.op("scalar", lambda e, kc=kc, t=t: e.activation(out=dstT[:, kc, col0:col0 + 128], in_=pst[t][:, kc, :], func=AF.Identity,
                                                                       scale=pcol[:, pa, kc:kc + 1], bias=pcol[:, ps_, kc:kc + 1]),
                         reads=[R_pst[t], R_pcol], writes=[R_dst], partial=True)
                else:
                    S.op("vector", lambda e, kc=kc, t=t: e.tensor_scalar(dstT[:, kc, col0:col0 + 128], pst[t][:, kc, :],
                                                                          pcol[:, pa, kc:kc + 1], pcol[:, ps_, kc:kc + 1], ALU.mult, ALU.add),
                         reads=[R_pst[t], R_pcol], writes=[R_dst], partial=True)

        def projA(slot, m4, src, R_src_list, blocks, nk_, kmap=None):
            banks = [nb() for _ in blocks]
            for kc in range(nk_):
                kk = kc if kmap is None else kmap(kc)
                for bi, (c0, n) in enumerate(blocks):
                    b = banks[bi]
                    S.op("tensor", lambda e, b=b, kk=kk, kc=kc, c0=c0, n=n: e.matmul(
                        ps[b][:, 0:n], lhsT=ring[slot][:, kk, m4 * 128:(m4 + 1) * 128], rhs=src[:, kc, c0:c0 + n],
                        start=(kc == 0), stop=(kc == nk_ - 1)),
                        reads=[R_ring[slot]] + R_src_list, writes=[R_ps[b]], partial=True)
            return banks

        def run_sb(sbi, base):
            prompt = (sbi == 0)
            goff = 0 if prompt else 6
            NTE = 8 if prompt else 10
            c0t = 0 if sbi < 2 else 2
            ext_row0 = 0 if sbi < 2 else 12
            core_row0 = 0 if sbi == 1 else 16
            xsrc = xp if prompt else xs
            xrow0 = 0 if sbi < 2 else 768
            ydst = yp if prompt else ys
            yrow0 = 0 if prompt else (0 if sbi == 1 else 1024)
            NEXT = NTE * 128
            ccol = c0t * 128

            def ucol(t):
                return (t + 1 + 2 * (t // 256)) if prompt else (t + ccol + 1)

            def chunk(i, n=1, **kw):
                ensure(base + i, base + i + n - 1, **kw)
                return [slot_of[base + i + j] for j in range(n)] if n > 1 else slot_of[base + i]

            if not prompt:
                for j in list(slot_of):
                    if j < base:
                        slot_free[slot_of.pop(j)] = True
                assert slot_free[NSLOT] and not slot_enabled[NSLOT]
                load_ctx()
            S.fence([R_h2T[i] for i in range(8)], R_hT)
            S.fence(R_tmpA2 + R_tmpB2, R_Sb2)
            S.fence([R_mg], [R_b, R_u])
            S.fence([R_aT], [R_q, R_k, R_v, R_yatt, R_yconv])

            for i in range(NTE):
                xi = nxt("xin")
                S.dma("sync", xin[xi][:], xsrc[xrow0 + i * 128: xrow0 + (i + 1) * 128, :], key=R_xin[xi], writes=[R_xin[xi]])
                norm_transpose(xin[xi][:], R_xin[xi], hT, R_hT[i], i * 128, goff + 0, goff + 1, defer=True)
            nt_flush()
            if sbi == 0:
                build_btimg()

            _stage(sbi * 10 + 1)
            extblocks = [(0, 512), (512, 512)] + ([(1024, 256)] if not prompt else [])
            coreblocks = [(ccol, 512), (ccol + 512, 512)]
            allhT = R_hT[:NTE]

            s = chunk(0)
            for m4 in range(4):
                banks = projA(s, m4, hT, allhT, coreblocks, 8)
                for bi, b in enumerate(banks):
                    S.op("scalar", lambda e, b=b, bi=bi, m4=m4: e.activation(out=qT[:, m4, bi * 512:(bi + 1) * 512], in_=ps[b][:, :],
                                                                             func=AF.Copy, scale=0.125),
                         reads=[R_ps[b]], writes=[R_q], partial=True)
            s = chunk(1)
            for m4 in range(4):
                banks = projA(s, m4, hT, allhT, extblocks, 8)
                for bi, b in enumerate(banks):
                    c0_, n = extblocks[bi]
                    S.op("vector", lambda e, b=b, c0_=c0_, n=n, m4=m4: e.tensor_copy(kT[:, m4, c0_:c0_ + n], ps[b][:, 0:n]),
                         reads=[R_ps[b]], writes=[R_k], partial=True)
            if prompt:
                for i in range(8):
                    b = nb()
                    for kc in range(8):
                        S.op("tensor", lambda e, b=b, kc=kc, i=i, s=s: e.matmul(ps[b][:, :], lhsT=hT[:, kc, i * 128:(i + 1) * 128], rhs=ring[s][:, kc, :],
                                                                          start=(kc == 0), stop=(kc == 7)),
                             reads=[R_ring[s], R_hT[i]], writes=[R_ps[b]], partial=True)
                    oi = nxt("ost")
                    S.op("scalar", lambda e, b=b, oi=oi: e.copy(ost[oi][:, 0:512], ps[b][:, :]), reads=[R_ps[b]], writes=[R_ost[oi]])
                    S.dma("sync", nk[i // 2].rearrange("h s d -> s h d")[(i % 2) * 128:(i % 2 + 1) * 128],
                          ost[oi][:, 0:512].rearrange("p (h d) -> p h d", h=8), key=R_ost[oi], reads=[R_ost[oi]])
            s = chunk(2)
            for i in range(NTE):
                b = nb()
                for kc in range(8):
                    S.op("tensor", lambda e, b=b, kc=kc, i=i, s=s: e.matmul(ps[b][:, :], lhsT=hT[:, kc, i * 128:(i + 1) * 128], rhs=ring[s][:, kc, :],
                                                                      start=(kc == 0), stop=(kc == 7)),
                         reads=[R_ring[s], R_hT[i]], writes=[R_ps[b]], partial=True)
                if prompt:
                    oi = nxt("ost")
                    S.op("scalar", lambda e, b=b, oi=oi: e.copy(ost[oi][:, 0:512], ps[b][:, :]), reads=[R_ps[b]], writes=[R_ost[oi]])
                    S.op("vector", lambda e, oi=oi, i=i: e.tensor_copy(vv[:, i, :], ost[oi][:, 0:512]), reads=[R_ost[oi]], writes=[R_v], partial=True)
                else:
                    S.op("vector", lambda e, b=b, i=i: e.tensor_copy(vv[:, i, :], ps[b][:, :]), reads=[R_ps[b]], writes=[R_v], partial=True)
                if prompt:
                    S.dma("sync", nv[i // 2].rearrange("h s d -> s h d")[(i % 2) * 128:(i % 2 + 1) * 128],
                          ost[oi][:, 0:512].rearrange("p (h d) -> p h d", h=8), key=R_ost[oi], reads=[R_ost[oi]])

            _stage(sbi * 10 + 2)
            if prompt:
                for qi in range(8):
                    sq = qi // 2
                    for hh in range(8):
                        hp, ho = hh // 2, (hh % 2) * 64
                        segs = [(kT[ho:ho + 64, hp, sq * 256:(sq + 1) * 256], 256, None)]
                        pv = [(j * 128, 128, vv[:, sq * 2 + j, hp * 128:(hp + 1) * 128]) for j in range(2)]
                        attention_unit(qi, hh, segs, pv, [R_k, R_v])
            else:
                S.fence(R_x1, [R_BT])
                S.dma("sync", FA[0:64, 0:7680], btimg[:, :], key=R_BT, reads=[R_btimg], writes=[R_BT])
                S.dma("sync", FA[64:128, 64:64 + 7680], btimg[:, :], key=R_BT, reads=[R_btimg], writes=[R_BT])
                for qi in range(8):
                    Rr = core_row0 + 2 * qi
                    r0t = min(max(Rr - 4, 0), 24)
                    r0b = min(max(Rr + 1 - 4, 0), 24)
                    rot = r0t - Rr + 7
                    nband = 576 if r0b != r0t else 512
                    kc0 = (r0t - ext_row0) * 64
                    vt0 = (r0t - ext_row0) // 2
                    for hh in range(8):
                        hp, ho = hh // 2, (hh % 2) * 64
                        segs = [(kT[ho:ho + 64, hp, kc0:kc0 + nband], nband, rot),
                                (kctxT[ho:ho + 64, hp, :], 512, None)]
                        pv = []
                        for j in range(4):
                            pv.append((j * 128, 128, vv[:, vt0 + j, hp * 128:(hp + 1) * 128]))
                        if nband == 576:
                            pv.append((512, 64, vv[0:64, vt0 + 4, hp * 128:(hp + 1) * 128]))
                        for j in range(4):
                            pv.append((nband + j * 128, 128, vctx[:, j, hp * 128:(hp + 1) * 128]))
                        if nband == 512:
                            pv = pv[:4] + [None] + pv[4:]
                        attention_unit_s(qi, hh, segs, pv)

            _stage(sbi * 10 + 3)
            att_flush(build_GG(1) if sbi == 1 else None)
            slot_enabled[NSLOT] = True
            S.fence(R_Sb2, R_tmpA2 + R_tmpB2)
            s = chunk(3)
            for m4 in range(4):
                banks = projA(s, m4, hT, allhT, coreblocks, 8)
                for bi, b in enumerate(banks):
                    S.op("scalar", lambda e, b=b, bi=bi, m4=m4: e.copy(bT[:, m4, bi * 512:(bi + 1) * 512], ps[b][:, :]),
                         reads=[R_ps[b]], writes=[R_b], partial=True)
            S.op("gpsimd", lambda e: e.memset(uT[:, :, :], 0.0), writes=[R_u])
            s1, s2 = chunk(4, 2)
            ublocks = extblocks
            for m4 in range(4):
                bk1 = projA(s1, m4, hT, allhT, ublocks, 8)
                bk2 = projA(s2, m4, hT, allhT, ublocks, 8)
                for bi, (c0_, n) in enumerate(ublocks):
                    pt_ = flip(); tmpA, tmpB, R_tmpA, R_tmpB = tmpA2[pt_], tmpB2[pt_], R_tmpA2[pt_], R_tmpB2[pt_]
                    S.op("scalar", lambda e, tmpA=tmpA, b=bk1[bi], n=n: e.copy(tmpA[:, 0:n], ps[b][:, 0:n]), reads=[R_ps[bk1[bi]]], writes=[R_tmpA])
                    if prompt:
                        for hs in range(2):
                            t0 = c0_ + hs * 256
                            S.op("vector", lambda e, tmpA=tmpA, b=bk2[bi], hs=hs, t0=t0, m4=m4: e.tensor_tensor(
                                out=uT[:, m4, ucol(t0):ucol(t0) + 256], in0=tmpA[:, hs * 256:(hs + 1) * 256],
                                in1=ps[b][:, hs * 256:(hs + 1) * 256], op=ALU.mult),
                                reads=[R_tmpA, R_ps[bk2[bi]]], writes=[R_u], partial=True)
                    else:
                        S.op("vector", lambda e, tmpA=tmpA, b=bk2[bi], c0_=c0_, n=n, m4=m4: e.tensor_tensor(
                            out=uT[:, m4, c0_ + 1:c0_ + 1 + n], in0=tmpA[:, 0:n], in1=ps[b][:, 0:n], op=ALU.mult),
                            reads=[R_tmpA, R_ps[bk2[bi]]], writes=[R_u], partial=True)

            ranges = [(sq * 256, 256) for sq in range(4)] if prompt else [(0, 512), (512, 512)]
            for m4 in range(4):
                for (t0, n) in ranges:
                    pt_ = flip(); tmpA, tmpB, R_tmpA, R_tmpB = tmpA2[pt_], tmpB2[pt_], R_tmpA2[pt_], R_tmpB2[pt_]
                    u0 = ucol(t0)
                    S.op("vector", lambda e, tmpB=tmpB, m4=m4, u0=u0, n=n: e.tensor_scalar(tmpB[:, 0:n], uT[:, m4, u0:u0 + n], convw[:, m4, 1:2], None, ALU.mult),
                         reads=[R_u], writes=[R_tmpB])
                    S.op("vector", lambda e, tmpB=tmpB, m4=m4, u0=u0, n=n: e.scalar_tensor_tensor(
                        out=tmpB[:, 0:n], in0=uT[:, m4, u0 - 1:u0 - 1 + n], scalar=convw[:, m4, 0:1], in1=tmpB[:, 0:n], op0=ALU.mult, op1=ALU.add),
                        reads=[R_u, R_tmpB], writes=[R_tmpB])
                    S.op("vector", lambda e, tmpB=tmpB, m4=m4, u0=u0, n=n: e.scalar_tensor_tensor(
                        out=tmpB[:, 0:n], in0=uT[:, m4, u0 + 1:u0 + 1 + n], scalar=convw[:, m4, 2:3], in1=tmpB[:, 0:n], op0=ALU.mult, op1=ALU.add),
                        reads=[R_u, R_tmpB], writes=[R_tmpB])
                    S.op("vector", lambda e, tmpB=tmpB, m4=m4, t0=t0, n=n: e.tensor_tensor(out=yconvT[:, m4, t0:t0 + n], in0=tmpB[:, 0:n],
                                                                               in1=bT[:, m4, t0:t0 + n], op=ALU.mult),
                         reads=[R_tmpB, R_b], writes=[R_yconv], partial=True)

            _stage(sbi * 10 + 4)
            S.fence([R_b, R_u], [R_mg])
            cb = [(0, 512), (512, 512)]
            for mg in range(2):
                sgc, sga, sbr = chunk(6 + mg * 3, 3)
                for m4 in range(4):
                    m = mg * 4 + m4
                    for (t0, n) in cb:
                        pt_ = flip(); tmpA, tmpB, R_tmpA, R_tmpB = tmpA2[pt_], tmpB2[pt_], R_tmpA2[pt_], R_tmpB2[pt_]
                        (bgc,) = projA(sgc, m4, hT, allhT, [(ccol + t0, n)], 8)
                        (bbc,) = projA(sbr, m4, yconvT, [R_yconv], [(t0, n)], 4)
                        (bga,) = projA(sga, m4, hT, allhT, [(ccol + t0, n)], 8)
                        (bba,) = projA(sbr, m4, yattT, [R_yatt], [(t0, n)], 4, kmap=lambda kc: kc + 4)
                        S.op("scalar", lambda e, tmpA=tmpA, b=bgc: e.activation(out=tmpA[:, :], in_=ps[b][:, :], func=AF.Sigmoid),
                             reads=[R_ps[bgc]], writes=[R_tmpA])
                        S.op("vector", lambda e, tmpA=tmpA, b=bbc: e.tensor_tensor(out=tmpA[:, :], in0=tmpA[:, :], in1=ps[b][:, :], op=ALU.mult),
                             reads=[R_tmpA, R_ps[bbc]], writes=[R_tmpA])
                        S.op("scalar", lambda e, tmpB=tmpB, b=bga: e.activation(out=tmpB[:, :], in_=ps[b][:, :], func=AF.Sigmoid),
                             reads=[R_ps[bga]], writes=[R_tmpB])
                        S.op("vector", lambda e, tmpB=tmpB, b=bba: e.tensor_tensor(out=tmpB[:, :], in0=tmpB[:, :], in1=ps[b][:, :], op=ALU.mult),
                             reads=[R_tmpB, R_ps[bba]], writes=[R_tmpB])
                        S.op("gpsimd", lambda e, tmpA=tmpA, tmpB=tmpB, m=m, t0=t0, n=n: e.tensor_tensor(out=mgT[:, m, t0:t0 + n], in0=tmpA[:, 0:n], in1=tmpB[:, 0:n], op=ALU.add),
                             reads=[R_tmpA, R_tmpB], writes=[R_mg], partial=True)

            _stage(sbi * 10 + 5)
            S.fence([R_BT], R_x1)
            S.fence(R_hT, R_h2T)
            so = chunk(12, 2)
            def wo_mm(t):
                bk = [nb(), nb()]
                for nh in range(2):
                    for kc in range(8):
                        S.op("tensor", lambda e, nh=nh, kc=kc, t=t, bk=bk: e.matmul(ps[bk[nh]][:, :], lhsT=mgT[:, kc, t * 128:(t + 1) * 128],
                                                                            rhs=ring[so[nh]][:, kc, :], start=(kc == 0), stop=(kc == 7)),
                             reads=[R_mg, R_ring[so[nh]]], writes=[R_ps[bk[nh]]], partial=True)
                return bk
            bks = {0: wo_mm(0)}
            for t in range(8):
                if t + 1 < 8:
                    bks[t + 1] = wo_mm(t + 1)
                bk = bks[t]
                xi = nxt("xin")
                S.dma("sync", xin[xi][:], xsrc[xrow0 + (c0t + t) * 128: xrow0 + (c0t + t + 1) * 128, :], key=R_xin[xi], writes=[R_xin[xi]])
                post_norm_residual(bk, None, xin[xi], R_xin[xi], GG[0], x1buf[:, t, :], R_x1[t])
                if t >= 2:
                    norm_transpose(x1buf[:, t - 2, :], R_x1[t - 2], h2T, R_h2T[t - 2], (t - 2) * 128, goff + 2, goff + 3, defer=True)
            for t in (6, 7):
                norm_transpose(x1buf[:, t, :], R_x1[t], h2T, R_h2T[t], t * 128, goff + 2, goff + 3, defer=True)
            nt_flush()

            _stage(sbi * 10 + 6)
            S.fence([R_q, R_k, R_v, R_yatt, R_yconv], [R_aT])
            allh2 = R_h2T
            for g in range(6):
                s1, s3 = chunk(14 + 2 * g, 2)
                for m4 in range(4 if g < 5 else 2):
                    m = g * 4 + m4
                    for (t0, n) in cb:
                        pt_ = flip(); tmpA, tmpB, R_tmpA, R_tmpB = tmpA2[pt_], tmpB2[pt_], R_tmpA2[pt_], R_tmpB2[pt_]
                        (b1,) = projA(s1, m4, h2T, allh2, [(t0, n)], 8)
                        (b3,) = projA(s3, m4, h2T, allh2, [(t0, n)], 8)
                        S.op("scalar", lambda e, tmpA=tmpA, b=b1: e.activation(out=tmpA[:, :], in_=ps[b][:, :], func=AF.Silu),
                             reads=[R_ps[b1]], writes=[R_tmpA])
                        S.op("vector", lambda e, tmpA=tmpA, b=b3, m=m, t0=t0, n=n: e.tensor_tensor(out=aT[:, m, t0:t0 + n], in0=tmpA[:, 0:n], in1=ps[b][:, 0:n], op=ALU.mult),
                             reads=[R_tmpA, R_ps[b3]], writes=[R_aT], partial=True)

            _stage(sbi * 10 + 7)
            S.fence(R_tmpA2 + R_tmpB2, R_Sb2)
            for nh in range(2):
                sl = chunk(26 + nh * 3, 3, disable_ctx_slot=(nh == 1 and sbi < 2))
                for t in range(8):
                    b = nb()
                    for kc in range(22):
                        S.op("tensor", lambda e, b=b, kc=kc, t=t, sl=sl: e.matmul(ps[b][:, :], lhsT=aT[:, kc, t * 128:(t + 1) * 128],
                                                                          rhs=ring[sl[kc // 8]][:, kc % 8, :], start=(kc == 0), stop=(kc == 21)),
                             reads=[R_aT, R_ring[sl[kc // 8]]], writes=[R_ps[b]], partial=True)
                    if nh == 0:
                        S.op("vector", lambda e, b=b, t=t: e.tensor_copy(y0ap(t), ps[b][:, :]), reads=[R_ps[b]], writes=[R_y0[t]], partial=True)
                        pj = flip()
                        S.op("scalar", lambda e, t=t, pj=pj: e.activation(out=Pb2[pj][:, 0:512], in_=y0ap(t), func=AF.Square, accum_out=ss0[:, t:t + 1]),
                             reads=[R_y0[t]], writes=[R_Pb2[pj]], pwrites=[R_ss0])
                    else:
                        oi = nxt("ost")
                        post_norm_residual([None, b], (y0ap(t), R_y0[t], ss0[:, t:t + 1]), None, R_x1[t], GG[1], ost[oi][:], R_ost[oi],
                                           xres=x1buf[:, t, :])
                        S.dma("sync", ydst[yrow0 + t * 128: yrow0 + (t + 1) * 128, :], ost[oi][:], key=R_ost[oi], reads=[R_ost[oi]])

        def post_norm_residual(bk, half0, xres_t, R_xres, GGt, dst, R_dst, xres=None):
            p = flip()
            junk, R_junk, stat, R_stat = Pb2[p][:, 0:1024], R_Pb2[p], stat2[p], R_stat2[p]
            if xres is None:
                xres = xres_t[:]
            if bk[0] is not None:
                S.op("scalar", lambda e: e.activation(out=junk[:, 0:512], in_=ps[bk[0]][:, :], func=AF.Square, accum_out=stat[:, 8:9]),
                     reads=[R_ps[bk[0]]], writes=[R_junk, R_stat])
                s0 = stat[:, 8:9]
                rs0 = [R_stat]
            else:
                s0 = half0[2]
                rs0 = [R_ss0]
            S.op("scalar", lambda e: e.activation(out=junk[:, 512:1024], in_=ps[bk[1]][:, :], func=AF.Square, accum_out=stat[:, 9:10]),
                 reads=[R_ps[bk[1]]], writes=[R_junk, R_stat])
            S.op("vector", lambda e: e.tensor_tensor(out=stat[:, 10:11], in0=s0, in1=stat[:, 9:10], op=ALU.add), reads=[R_stat] + rs0, writes=[R_stat])
            S.op("scalar", lambda e: e.activation(out=stat[:, 11:12], in_=stat[:, 10:11], func=AF.Ln, scale=1.0 / D, bias=EPS), reads=[R_stat], writes=[R_stat])
            S.op("scalar", lambda e: e.activation(out=stat[:, 12:13], in_=stat[:, 11:12], func=AF.Exp, scale=-0.5), reads=[R_stat], writes=[R_stat])
            for nh in range(2):
                if bk[nh] is not None:
                    src, rsrc = ps[bk[nh]][:, :], [R_ps[bk[nh]]]
                else:
                    src, rsrc = half0[0], [half0[1]]
                lo, hi = nh * 512, (nh + 1) * 512
                S.op("vector", lambda e, src=src, lo=lo, hi=hi: e.scalar_tensor_tensor(out=dst[:, lo:hi], in0=src, scalar=stat[:, 12:13], in1=GGt[:, lo:hi],
                                                                                     op0=ALU.mult, op1=ALU.mult),
                     reads=rsrc + [R_stat, R_GG], writes=[R_dst], partial=(nh == 1))
            S.op("gpsimd", lambda e: e.tensor_tensor(out=dst, in0=dst, in1=xres, op=ALU.add), reads=[R_dst, R_xres], writes=[R_dst])

        def attention_unit_s(qi, hh, segs, pv):
            pv2 = []
            for i, item in enumerate(pv):
                if item is None:
                    continue
                pv2.append((i,) + item)
            attention_core(qi, hh, segs, pv2, [R_k, R_v, R_ctx])

        att_units = []
        neg_ok = [False, False]
        att_ctr = {"n": 0}

        def attention_core(qi, hh, segs, pv, R_kv):
            u = att_ctr["n"]
            att_ctr["n"] += 1
            hp, ho = hh // 2, (hh % 2) * 64
            Sb, R_Sb = Sb2[u % 2], R_Sb2[u % 2]
            Pb, R_Pb = Pb2[u % 3], R_Pb2[u % 3]
            PTs, R_PTs = PTs2[u % 2], R_PTs2[u % 2]
            stat, R_stat = stat2[u % 5], R_stat2[u % 5]
            qap = qT[ho:ho + 64, hp, qi * 128:(qi + 1) * 128]
            ncol = sum(n for (_, n, _) in segs)

            def stA():
                col = 0
                first = True
                if not any(n == 576 for (_, n, _) in segs):
                    neg_ok[u % 2] = False
                for (rhs, n, bias) in segs:
                    off = 0
                    while off < n:
                        w = min(512, n - off)
                        b = nb()
                        S.op("tensor", lambda e, b=b, rhs=rhs, off=off, w=w: e.matmul(ps[b][:, 0:w], lhsT=qap, rhs=rhs[:, off:off + w], start=True, stop=True),
                             reads=[R_q] + R_kv, writes=[R_ps[b]])
                        c = col + off
                        if bias is None:
                            S.op("scalar", lambda e, b=b, c=c, w=w: e.copy(Sb[:, c:c + w], ps[b][:, 0:w]), reads=[R_ps[b]], writes=[R_Sb], partial=not first)
                            first = False
                        else:
                            blk0 = bias + off // 64

                            def badd(p0, p1, c0, c1, fp, b=b, c=c, blk0=blk0):
                                S.op("vector", lambda e: e.tensor_tensor(
                                    out=Sb[p0:p1, c + c0:c + c1], in0=ps[b][p0:p1, c0:c1],
                                    in1=BT[p0:p1, hh, blk0 + c0 // 64:blk0 + c1 // 64, :].rearrange("p r w -> p (r w)"), op=ALU.add),
                                    reads=[R_ps[b], R_BT], writes=[R_Sb], partial=not fp)
                            if n == 576 and off == 0:
                                badd(0, 128, 64, 512, first)
                                first = False
                                badd(0, 64, 0, 64, False)
                                if not neg_ok[u % 2]:
                                    S.op("vector", lambda e, c=c: e.memset(Sb[64:128, c:c + 64], NEG), writes=[R_Sb], partial=True)
                            elif n == 576:
                                badd(64, 128, 0, 64, first)
                                first = False
                                if not neg_ok[u % 2]:
                                    S.op("vector", lambda e, c=c: e.memset(Sb[0:64, c:c + 64], NEG), writes=[R_Sb], partial=True)
                                neg_ok[u % 2] = True
                            else:
                                badd(0, 128, 0, w, first)
                                first = False
                        off += w
                    col += n

            def stB():
                S.op("vector", lambda e: e.reduce_max(stat[:, 6:7], Sb[:, 0:ncol], axis=AX.X, negate=True), reads=[R_Sb], writes=[R_stat])
                S.op("scalar", lambda e: e.activation(out=Pb[:, 0:ncol], in_=Sb[:, 0:ncol], func=AF.Exp, bias=stat[:, 6:7], scale=1.0, accum_out=stat[:, 5:6]),
                     reads=[R_Sb, R_stat], writes=[R_Pb, R_stat])

            def stC1():
                S.op("vector", lambda e: e.reciprocal(stat[:, 7:8], stat[:, 5:6]), reads=[R_stat], writes=[R_stat])
                if ncol > 600:
                    nsplit = ncol - 512
                    S.op("vector", lambda e: e.tensor_scalar(Pb[:, 0:nsplit], Pb[:, 0:nsplit], stat[:, 7:8], None, ALU.mult),
                         reads=[R_stat], pwrites=[R_Pb])
                    S.op("scalar", lambda e: e.activation(out=Pb[:, nsplit:ncol], in_=Pb[:, nsplit:ncol], func=AF.Copy, scale=stat[:, 7:8]),
                         reads=[R_stat], pwrites=[R_Pb])
                else:
                    S.op("scalar", lambda e: e.activation(out=Pb[:, 0:ncol], in_=Pb[:, 0:ncol], func=AF.Copy, scale=stat[:, 7:8]), reads=[R_Pb, R_stat], writes=[R_Pb])

            used0 = [p for p in pv if p[0] < 5]
            used1 = [p for p in pv if p[0] >= 5]

            def stC2():
                for (i, c0, K, _) in pv:
                    t = 0 if i < 5 else 1
                    sl = i if i < 5 else i - 5
                    S.op("tensor", lambda e, t=t, sl=sl, c0=c0, K=K: e.transpose(pst[t][0:K, sl, :], Pb[:, c0:c0 + K], ident[:]),
                         reads=[R_Pb, R_const], writes=[R_pst[t]], partial=True)
                n0 = max(p[0] for p in used0) + 1
                S.op("vector", lambda e: e.tensor_copy(PTs[:, 0:n0, :], pst[0][:, 0:n0, :]), reads=[R_pst[0]], writes=[R_PTs])
                if used1:
                    n1 = max(p[0] for p in used1) - 4
                    S.op("scalar", lambda e: e.copy(PTs[:, 5:5 + n1, :], pst[1][:, 0:n1, :]), reads=[R_pst[1]], writes=[R_PTs], partial=True)

            def stD():
                ob = nb()
                for j, (i, c0, K, vap) in enumerate(pv):
                    S.op("tensor", lambda e, j=j, i=i, K=K, vap=vap: e.matmul(ps[ob][:, 0:128], lhsT=vap, rhs=PTs[0:K, i, :],
                                                                             start=(j == 0), stop=(j == len(pv) - 1)),
                         reads=[R_PTs] + R_kv, writes=[R_ps[ob]], partial=True)
                S.op("scalar", lambda e: e.copy(yattT[ho:ho + 64, hp, qi * 128:(qi + 1) * 128], ps[ob][ho:ho + 64, 0:128]),
                     reads=[R_ps[ob]], writes=[R_yatt], partial=True)

            att_units.append((stA, stB, stC1, stC2, stD))

        def att_flush(side=None):
            n = len(att_units)
            for j in range(n + 4):
                for st in range(5):
                    u = j - st
                    if 0 <= u < n:
                        att_units[u][st]()
                if side is not None and j % 8 == 7:
                    next(side, None)
            if side is not None:
                for _ in side:
                    pass
            att_units.clear()
            neg_ok[0] = neg_ok[1] = False

        def attention_unit(qi, hh, segs, pv, R_kv):
            attention_core(qi, hh, segs, [(i,) + p for i, p in enumerate(pv)], R_kv)

        def load_ctx():
            ctmp = ost[0][:, :].bitcast(BF16).rearrange("p (c f) -> p c f", c=4)
            for c in range(4):
                S.dma("gpsimd", ctmp[:, c, :].rearrange("p (h d) -> p h d", h=8), ck[:, c * 128:(c + 1) * 128, :].rearrange("h p d -> p h d"),
                      key=R_ctmpk, writes=[R_ost[0]])
                S.dma("gpsimd", vctx[:, c, :].rearrange("p (h d) -> p h d", h=8), cv[:, c * 128:(c + 1) * 128, :].rearrange("h p d -> p h d"),
                      key=R_ctx, writes=[R_ctx])
            for hp in range(4):
                t = nxt("pst")
                for c in range(4):
                    S.op("tensor", lambda e, t=t, c=c, hp=hp: e.transpose(pst[t][:, c, :], ctmp[:, c, hp * 128:(hp + 1) * 128], ident[:]),
                         reads=[R_ost[0], R_const], writes=[R_pst[t]], partial=True)
                S.op("vector", lambda e, t=t, hp=hp: e.tensor_copy(kctxT[:, hp, :].rearrange("p (c k) -> p c k", c=4), pst[t][:, 0:4, :]),
                     reads=[R_pst[t]], writes=[R_ctx], partial=True)

        R_btimg = S.region("btimg")

        def build_btimg():
            negt = Sb2[0]
            S.op("vector", lambda e: e.memset(negt[0:64, 0:960], NEG), writes=[R_Sb2[0]])
            for k in range(8):
                S.dma("sync", btimg[:, k * 960:(k + 1) * 960], negt[0:64, 0:960], key=R_btimg, reads=[R_Sb2[0]], writes=[R_btimg])
            img_h = btimg.tensor
            dst = bass.AP(tensor=img_h, offset=8 * 7680 + 0, ap=[[7680 + 1, 49], [64, 120], [1, 16]])
            src = bass.AP(tensor=rpb_t, offset=7, ap=[[0, 49], [31, 120], [1, 16]])
            S.dma("sync", dst, src, key=R_btimg2, reads=[R_btimg], writes=[R_btimg])
            for wq in list(range(0, 8)) + list(range(57, 64)):
                cc = min(max(wq - 8, 0), 48)
                a = 15 + cc - wq
                dst = bass.AP(tensor=img_h, offset=wq * 7680 + cc, ap=[[64, 120], [1, 16]])
                src = bass.AP(tensor=rpb_t, offset=a, ap=[[31, 120], [1, 16]])
                S.dma("sync", dst, src, key=R_btimg2, reads=[R_btimg], writes=[R_btimg])

        R_btimg2 = S.region("btimg2")

        base = NMOD
        try:
            _stage(-3)
            setup_mod()
            _stage(-2)
            group_pcol(0)
            group_pcol(1)
            for _ in build_GG(0):
                pass
            _stage(0)
            run_sb(0, base)
            _stage(8)
            _stage(9)
            run_sb(1, base + NPER)
            _stage(18)
            run_sb(2, base + 2 * NPER)
        except _Stop:
            pass

        S.emit(nc, st)
    return nc


_NC_CACHE = {}


def kernel(x_prompt, x_sample, cache_k, cache_v, c, c_ctx, w_mod, b_mod, g_pre_mix, g_post_mix, g_pre_ffn, g_post_ffn,
           w_in, w_conv, w_br_conv, w_br_attn, rpb, w_gate, w_o, w_ff1, w_ff3, w_ff2):
    f = lambda a: np.ascontiguousarray(np.asarray(a, dtype=np.float32))
    x_prompt, x_sample, cache_k, cache_v, c, c_ctx = map(f, (x_prompt, x_sample, cache_k, cache_v, c, c_ctx))
    if "nc" not in _NC_CACHE:
        _NC_CACHE["nc"] = build()
    nc = _NC_CACHE["nc"]

    def col(v):
        return f(v).reshape(8, 128).T

    shared = {
        "bmodcol": f(f(b_mod).reshape(48, 128).T),
        "gcol": f(np.stack([col(g_pre_mix[0]), col(g_post_mix[0]), col(g_pre_ffn[0]), col(g_post_ffn[0])], axis=1)),
        "convw": f(f(w_conv)[0].reshape(3, 4, 128).transpose(2, 1, 0)),
        "rpb": f(rpb)[0],
        "w_mod": f(w_mod)[0], "w_in": f(w_in)[0], "w_brc": f(w_br_conv)[0], "w_bra": f(w_br_attn)[0],
        "w_gate": f(w_gate)[0], "w_o": f(w_o)[0], "w_ff1": f(w_ff1)[0], "w_ff3": f(w_ff3)[0], "w_ff2": f(w_ff2)[0],
    }
    in_maps = []
    for i in range(8):
        m = dict(shared)
        m["xp"] = f(x_prompt[4 * i:4 * i + 4].reshape(1024, D))
        m["xs"] = f(x_sample[i])
        m["ck"] = f(cache_k[i, 0])
        m["cv"] = f(cache_v[i, 0])
        m["c2col"] = f(np.stack([col(c_ctx), col(c[i])], axis=2))
        in_maps.append(m)
    res = run_bass_kernel_spmd(nc, in_maps, core_ids=list(range(8)))
    r = res.results
    y_p = np.concatenate([r[i]["yp"].reshape(4, 256, D) for i in range(8)], axis=0)
    y_s = np.stack([r[i]["ys"] for i in range(8)], axis=0)
    n_k = np.concatenate([r[i]["nk"] for i in range(8)], axis=0)[:, None]
    n_v = np.concatenate([r[i]["nv"] for i in range(8)], axis=0)[:, None]
    return (y_p.astype(np.float32), y_s.astype(np.float32), n_k.astype(np.float32), n_v.astype(np.float32))
```
